# Optimizing a Trainium2 kernel written in Bass

```python
import math
import jax, jax.numpy as jnp
from jax import lax
import numpy as np

D_MODEL = 2048
BATCH = 4
SEQ = 2048
DEPTH = 4

MLA_HEADS = 8
MLA_Q_LORA = 512
MLA_KV_LORA = 512
MLA_NOPE_DIM = 128
MLA_ROPE_DIM = 64
MLA_V_DIM = 128
MLA_QK_DIM = MLA_NOPE_DIM + MLA_ROPE_DIM
ATTN_Q_BLOCK = 128

POOL_WIDTH = 1024
POOL_WINDOWS = (2, 4, 8, 16)
POOL_GROUP = POOL_WIDTH // len(POOL_WINDOWS)

CONV_WIDTH = 1024
CONV_KSIZE = 31

DIL_CONFIGS = ((128, 1), (512, 4), (2048, 16))
DIL_HEADS = 8
DIL_HEAD_DIM = 128
DIL_WIDTH = DIL_HEADS * DIL_HEAD_DIM
DIL_ROT_DIM = DIL_HEAD_DIM // 4

N_BRANCHES = 4
ROPE_THETA = 500000.0

FFN_DIM = 5632
FFN_CONV_KSIZE = 3

ALPHA = (2 * DEPTH) ** 0.25
BETA = (8 * DEPTH) ** -0.25

LN_EPS = 1e-5
RMS_EPS = 1e-6
NEG_INF = -1e30

_SPLITS = (MLA_Q_LORA, MLA_KV_LORA, MLA_ROPE_DIM, POOL_WIDTH, 2 * CONV_WIDTH,
           len(DIL_CONFIGS) * 3 * DIL_WIDTH, N_BRANCHES * D_MODEL)
IN_COLS = sum(_SPLITS)

kernel_name = "hybrid_gated_mla_pool_conv_dilated_deepnorm"


def layer_norm(x, g, b):
    xf = x.astype(jnp.float32)
    mu = jnp.mean(xf, axis=-1, keepdims=True)
    var = jnp.mean(jnp.square(xf - mu), axis=-1, keepdims=True)
    return ((xf - mu) * lax.rsqrt(var + LN_EPS) * g + b).astype(x.dtype)


def rms_norm(x, g):
    xf = x.astype(jnp.float32)
    return (xf * lax.rsqrt(jnp.mean(jnp.square(xf), axis=-1, keepdims=True) + RMS_EPS) * g).astype(x.dtype)


def rope(x, positions, rot_dim):
    half = rot_dim // 2
    inv_freq = ROPE_THETA ** (-jnp.arange(half, dtype=jnp.float32) * 2.0 / rot_dim)
    ang = positions.astype(jnp.float32)[..., None] * inv_freq
    if x.ndim == 4:
        ang = ang[:, :, None, :]
    cos, sin = jnp.cos(ang), jnp.sin(ang)
    xr = x[..., :rot_dim].astype(jnp.float32)
    x1, x2 = xr[..., :half], xr[..., half:]
    rot = jnp.concatenate([x1 * cos - x2 * sin, x2 * cos + x1 * sin], axis=-1)
    return jnp.concatenate([rot.astype(x.dtype), x[..., rot_dim:]], axis=-1)


def causal_dwconv(x, w, b):
    k = w.shape[0]
    y = lax.conv_general_dilated(
        x, w[:, None, :].astype(x.dtype), window_strides=(1,), padding=[(k - 1, 0)],
        dimension_numbers=("NWC", "WIO", "NWC"), feature_group_count=x.shape[-1])
    return y + b


def mla_attention(q_nope, q_pe, k_nope, k_pe, v):
    S = q_nope.shape[1]
    scale = MLA_QK_DIM ** -0.5
    outs = []
    for n in range(S // ATTN_Q_BLOCK):
        s0, s1 = n * ATTN_Q_BLOCK, (n + 1) * ATTN_Q_BLOCK
        sc = (jnp.einsum("bqhd,bkhd->bhqk", q_nope[:, s0:s1], k_nope[:, :s1])
              + jnp.einsum("bqhd,bkd->bhqk", q_pe[:, s0:s1], k_pe[:, :s1]))
        sc = sc.astype(jnp.float32) * scale
        mask = jnp.arange(s1)[None, :] <= jnp.arange(s0, s1)[:, None]
        sc = jnp.where(mask, sc, NEG_INF)
        p = jax.nn.softmax(sc, axis=-1).astype(v.dtype)
        outs.append(jnp.einsum("bhqk,bkhd->bqhd", p, v[:, :s1]))
    return jnp.concatenate(outs, axis=1)


def dilated_window_attention(q, k, v, window, dilation):
    B, S, H, D = q.shape
    nb = window // dilation
    L = S // dilation
    nblk = -(-L // nb)
    Lp = nblk * nb
    Bz = B * dilation

    def to_sub(t):
        t = t.reshape(B, L, dilation, H, D).transpose(0, 2, 1, 3, 4).reshape(Bz, L, H, D)
        t = jnp.pad(t, ((0, 0), (0, Lp - L), (0, 0), (0, 0)))
        return t.reshape(Bz, nblk, nb, H, D)

    def with_prev(t):
        prev = jnp.concatenate([jnp.zeros_like(t[:, :1]), t[:, :-1]], axis=1)
        return jnp.concatenate([prev, t], axis=2)

    qs = to_sub(q)
    kb = with_prev(to_sub(k))
    vb = with_prev(to_sub(v))
    sc = jnp.einsum("znqhd,znkhd->znhqk", qs, kb).astype(jnp.float32) * (D ** -0.5)
    blk = jnp.arange(nblk)[:, None, None] * nb
    qi = blk + jnp.arange(nb)[None, :, None]
    kj = blk - nb + jnp.arange(2 * nb)[None, None, :]
    rel = qi - kj
    valid = (rel >= 0) & (rel <= nb) & (kj >= 0)
    sc = jnp.where(valid[None, :, None], sc, NEG_INF)
    lse = jax.nn.logsumexp(sc, axis=-1)
    p = jnp.exp(sc - lse[..., None]).astype(v.dtype)
    o = jnp.einsum("znhqk,znkhd->znqhd", p, vb)
    o = o.reshape(Bz, Lp, H, D)[:, :L]
    o = o.reshape(B, dilation, L, H, D).transpose(0, 2, 1, 3, 4).reshape(B, S, H, D)
    lse = lse.transpose(0, 1, 3, 2).reshape(Bz, Lp, H)[:, :L]
    lse = lse.reshape(B, dilation, L, H).transpose(0, 2, 1, 3).reshape(B, S, H)
    return o, lse


def multiscale_pool(p, w_grp, scale):
    B, S, C = p.shape
    pf = p.astype(jnp.float32)
    cs = jnp.concatenate([jnp.zeros((B, 1, C), jnp.float32), jnp.cumsum(pf, axis=1)], axis=1)
    t = jnp.arange(S)
    diffs = []
    for g, w in enumerate(POOL_WINDOWS):
        sl = slice(g * POOL_GROUP, (g + 1) * POOL_GROUP)
        lo = jnp.maximum(t + 1 - w, 0)
        cnt = jnp.minimum(t + 1, w).astype(jnp.float32)[None, :, None]
        mean = (cs[:, 1:, sl] - cs[:, lo, sl]) / cnt
        diffs.append(mean - pf[..., sl])
    d = jnp.stack(diffs, axis=2).astype(p.dtype)
    y = jnp.einsum("bsgi,gio->bsgo", d, w_grp).reshape(B, S, C)
    return y * scale


def conformer_conv(u, w_dw, b_dw, ln_g, ln_b):
    a, b = jnp.split(u, 2, axis=-1)
    h = causal_dwconv(a * jax.nn.sigmoid(b), w_dw, b_dw)
    return jax.nn.silu(layer_norm(h, ln_g, ln_b))


def hybrid_mixer(x, positions, w_in, b_gate, mla_gq, mla_gkv, mla_w_uq, mla_w_ukv, mla_w_proj,
                 pool_w, pool_scale, pool_w_proj, conv_dw, conv_dw_b, conv_ln_g, conv_ln_b,
                 conv_w_proj, dil_w_proj, mix_w_out):
    B, S, _ = x.shape
    u = x @ w_in
    split_pts = [int(i) for i in np.cumsum(_SPLITS)[:-1]]
    c_q, c_kv, k_pe, u_pool, u_conv, u_dil, u_gate = jnp.split(u, split_pts, axis=-1)

    q = (rms_norm(c_q, mla_gq) @ mla_w_uq).reshape(B, S, MLA_HEADS, MLA_QK_DIM)
    q_nope = q[..., :MLA_NOPE_DIM]
    q_pe = rope(q[..., MLA_NOPE_DIM:], positions, MLA_ROPE_DIM)
    kv = (rms_norm(c_kv, mla_gkv) @ mla_w_ukv).reshape(B, S, MLA_HEADS, MLA_NOPE_DIM + MLA_V_DIM)
    k_nope, v_a = kv[..., :MLA_NOPE_DIM], kv[..., MLA_NOPE_DIM:]
    k_pe = rope(k_pe, positions, MLA_ROPE_DIM)
    y_a = mla_attention(q_nope, q_pe, k_nope, k_pe, v_a).reshape(B, S, MLA_HEADS * MLA_V_DIM) @ mla_w_proj

    y_b = multiscale_pool(u_pool, pool_w, pool_scale) @ pool_w_proj

    y_c = conformer_conv(u_conv, conv_dw, conv_dw_b, conv_ln_g, conv_ln_b) @ conv_w_proj

    qkv = u_dil.reshape(B, S, len(DIL_CONFIGS), 3, DIL_HEADS, DIL_HEAD_DIM)
    outs, lses = [], []
    for g, (window, dilation) in enumerate(DIL_CONFIGS):
        q_g = rope(qkv[:, :, g, 0], positions, DIL_ROT_DIM)
        k_g = rope(qkv[:, :, g, 1], positions, DIL_ROT_DIM)
        o_g, l_g = dilated_window_attention(q_g, k_g, qkv[:, :, g, 2], window, dilation)
        outs.append(o_g)
        lses.append(l_g)
    wts = jax.nn.softmax(jnp.stack(lses, axis=0), axis=0)
    o_d = jnp.sum(wts[..., None].astype(outs[0].dtype) * jnp.stack(outs, axis=0), axis=0)
    y_d = o_d.reshape(B, S, DIL_WIDTH) @ dil_w_proj

    gates = jax.nn.sigmoid(u_gate.reshape(B, S, N_BRANCHES, D_MODEL) + b_gate)
    merged = (gates[:, :, 0] * y_a + gates[:, :, 1] * y_b
              + gates[:, :, 2] * y_c + gates[:, :, 3] * y_d)
    return merged @ mix_w_out


def conv_ffn(x, w_up, w_dw, b_dw, w_down):
    h = causal_dwconv(x @ w_up, w_dw, b_dw)
    val, gate = jnp.split(h, 2, axis=-1)
    return (val * jax.nn.silu(gate)) @ w_down


def setup_inputs(seed: int = 0) -> dict:
    key = jax.random.key(seed)
    ks = jax.random.split(key, 32)
    f32 = jnp.float32
    L = DEPTH

    def nrm(k, shape, scale):
        return jax.random.normal(k, shape, f32) * scale

    def gain(k, shape):
        return 1.0 + 0.02 * jax.random.normal(k, shape, f32)

    return {
        "x": jax.random.normal(ks[0], (BATCH, SEQ, D_MODEL), f32),
        "positions": jnp.tile(jnp.arange(SEQ, dtype=jnp.int32)[None, :], (BATCH, 1)),
        "w_in": nrm(ks[1], (L, D_MODEL, IN_COLS), D_MODEL ** -0.5),
        "b_gate": nrm(ks[2], (L, N_BRANCHES, D_MODEL), 0.02),
        "mla_gq": gain(ks[3], (L, MLA_Q_LORA)),
        "mla_gkv": gain(ks[4], (L, MLA_KV_LORA)),
        "mla_w_uq": nrm(ks[5], (L, MLA_Q_LORA, MLA_HEADS * MLA_QK_DIM), MLA_Q_LORA ** -0.5),
        "mla_w_ukv": nrm(ks[6], (L, MLA_KV_LORA, MLA_HEADS * (MLA_NOPE_DIM + MLA_V_DIM)), MLA_KV_LORA ** -0.5),
        "mla_w_proj": nrm(ks[7], (L, MLA_HEADS * MLA_V_DIM, D_MODEL), (MLA_HEADS * MLA_V_DIM) ** -0.5 * BETA),
        "pool_w": nrm(ks[8], (L, len(POOL_WINDOWS), POOL_GROUP, POOL_GROUP), POOL_GROUP ** -0.5),
        "pool_scale": gain(ks[9], (L, POOL_WIDTH)),
        "pool_w_proj": nrm(ks[10], (L, POOL_WIDTH, D_MODEL), POOL_WIDTH ** -0.5 * BETA),
        "conv_dw": nrm(ks[11], (L, CONV_KSIZE, CONV_WIDTH), CONV_KSIZE ** -0.5),
        "conv_dw_b": nrm(ks[12], (L, CONV_WIDTH), 0.02),
        "conv_ln_g": gain(ks[13], (L, CONV_WIDTH)),
        "conv_ln_b": nrm(ks[14], (L, CONV_WIDTH), 0.02),
        "conv_w_proj": nrm(ks[15], (L, CONV_WIDTH, D_MODEL), CONV_WIDTH ** -0.5 * BETA),
        "dil_w_proj": nrm(ks[16], (L, DIL_WIDTH, D_MODEL), DIL_WIDTH ** -0.5 * BETA),
        "mix_w_out": nrm(ks[17], (L, D_MODEL, D_MODEL), D_MODEL ** -0.5 * BETA),
        "ln1_g": gain(ks[18], (L, D_MODEL)),
        "ln1_b": nrm(ks[19], (L, D_MODEL), 0.02),
        "ffn_w_up": nrm(ks[20], (L, D_MODEL, 2 * FFN_DIM), D_MODEL ** -0.5),
        "ffn_dw": nrm(ks[21], (L, FFN_CONV_KSIZE, 2 * FFN_DIM), FFN_CONV_KSIZE ** -0.5),
        "ffn_dw_b": nrm(ks[22], (L, 2 * FFN_DIM), 0.02),
        "ffn_w_down": nrm(ks[23], (L, FFN_DIM, D_MODEL), FFN_DIM ** -0.5 * BETA),
        "ln2_g": gain(ks[24], (L, D_MODEL)),
        "ln2_b": nrm(ks[25], (L, D_MODEL), 0.02),
    }


def reference(x, positions, w_in, b_gate, mla_gq, mla_gkv, mla_w_uq, mla_w_ukv, mla_w_proj,
              pool_w, pool_scale, pool_w_proj, conv_dw, conv_dw_b, conv_ln_g, conv_ln_b,
              conv_w_proj, dil_w_proj, mix_w_out, ln1_g, ln1_b, ffn_w_up, ffn_dw, ffn_dw_b,
              ffn_w_down, ln2_g, ln2_b):
    for l in range(DEPTH):
        m = hybrid_mixer(x, positions, w_in[l], b_gate[l], mla_gq[l], mla_gkv[l], mla_w_uq[l],
                         mla_w_ukv[l], mla_w_proj[l], pool_w[l], pool_scale[l], pool_w_proj[l],
                         conv_dw[l], conv_dw_b[l], conv_ln_g[l], conv_ln_b[l], conv_w_proj[l],
                         dil_w_proj[l], mix_w_out[l])
        x = layer_norm(ALPHA * x + m, ln1_g[l], ln1_b[l])
        f = conv_ffn(x, ffn_w_up[l], ffn_dw[l], ffn_dw_b[l], ffn_w_down[l])
        x = layer_norm(ALPHA * x + f, ln2_g[l], ln2_b[l])
    return x
```

```python
import math
import numpy as np
from contextlib import ExitStack
import concourse.bass as bass
import concourse.mybir as mybir
from concourse.bass_utils import run_bass_kernel_spmd

F32 = mybir.dt.float32
BF16 = mybir.dt.bfloat16
I32 = mybir.dt.int32
AF = mybir.ActivationFunctionType
ALU = mybir.AluOpType

D = 2048
S = 2048
TB = 1024
DEPTH = 4
FFN = 5632
ALPHA = (2 * DEPTH) ** 0.25
THETA = 500000.0
DIL_D = (1, 4, 16)
PSKIP = set()


class Buf:
    __slots__ = ("name", "lw", "rd_c", "rd_d")

    def __init__(self, name):
        self.name = name
        self.lw = None
        self.rd_c = {}
        self.rd_d = []


class Op:
    __slots__ = ("idx", "eng", "meth", "args", "kw", "deps", "dma", "inc", "sem", "val", "incv")


class Sched:
    def __init__(self, nc, es):
        self.nc = nc
        self.es = es
        self.ops = []
        self.bufs = {}
        self.eng = {"pe": nc.tensor, "act": nc.scalar, "dve": nc.vector, "pool": nc.gpsimd, "sp": nc.sync}
        self.out_dmas = []

    def b(self, *key):
        bb = self.bufs.get(key)
        if bb is None:
            bb = self.bufs[key] = Buf(key)
        return bb

    def op(self, eng, meth, *args, R=(), W=(), dma=None, **kw):
        o = Op()
        o.idx = len(self.ops)
        o.eng = eng
        o.meth = meth
        o.args = args
        o.kw = kw
        o.dma = dma
        o.inc = False
        o.sem = None
        o.val = 0
        o.incv = 16 if dma is not None else 1
        deps = set()
        for b in R:
            if b.lw is not None:
                deps.add(b.lw)
        for b in W:
            if b.lw is not None:
                deps.add(b.lw)
            deps.update(b.rd_c.values())
            deps.update(b.rd_d)
        keep = []
        for d in deps:
            p = self.ops[d]
            if p.dma is not None or dma is not None or p.eng != eng:
                keep.append(d)
        o.deps = sorted(keep)
        for b in R:
            if dma is not None:
                b.rd_d.append(o.idx)
            else:
                b.rd_c[eng] = o.idx
        for b in W:
            b.lw = o.idx
            b.rd_c = {}
            b.rd_d = []
        self.ops.append(o)
        return o

    def pe(self, meth, *a, **k):
        return self.op("pe", meth, *a, **k)

    def act(self, meth, *a, **k):
        return self.op("act", meth, *a, **k)

    def dve(self, meth, *a, **k):
        return self.op("dve", meth, *a, **k)

    def pool(self, meth, *a, **k):
        return self.op("pool", meth, *a, **k)

    def dma(self, q, key, out, in_, R=(), W=(), is_out=False, **kw):
        o = self.op(q, "dma_start", out=out, in_=in_, R=R, W=W, dma=key, **kw)
        if is_out:
            self.out_dmas.append(o.idx)
        return o

    def finish(self, eng="sp"):
        o = self.op(eng, None, dma="__fin__")
        o.deps = sorted(set(o.deps) | set(p.idx for p in self.ops if p.dma is not None and p.meth is not None))

    def emit(self):
        nc = self.nc
        for o in self.ops:
            for d in o.deps:
                self.ops[d].inc = True
        sems = {}
        cnt = {}
        for o in self.ops:
            if not o.inc:
                continue
            k = o.eng if o.dma is None else ("d", o.dma)
            if k not in sems:
                sems[k] = self.es.enter_context(nc.semaphore("s%d" % len(sems)))
                cnt[k] = 0
            o.sem = sems[k]
            cnt[k] += o.incv
            o.val = cnt[k]
        waited = {e: {} for e in self.eng}
        for o in self.ops:
            e = self.eng[o.eng]
            wd = waited[o.eng]
            need = {}
            for d in o.deps:
                p = self.ops[d]
                key = id(p.sem)
                if wd.get(key, 0) < p.val:
                    if key not in need or need[key][1] < p.val:
                        need[key] = (p.sem, p.val)
            for key, (sem, val) in need.items():
                e.wait_ge(sem, val)
                wd[key] = val
            if o.meth is None:
                continue
            inst = getattr(e, o.meth)(*o.args, **o.kw)
            if o.inc:
                inst.then_inc(o.sem, o.incv)
        self.n_sems = len(sems)


def tile_w(W, G):
    K, N = W.shape
    KC = K // 128
    return np.ascontiguousarray(W.reshape(KC, 128, N // G, G).transpose(2, 1, 0, 3)).reshape(N // G, 128, KC * G)


def col_vec(v):
    return np.ascontiguousarray(v.reshape(-1, 128).T)


V_BG = 0
V_GQ = 64
V_GKV = 68
V_PS = 72
V_CW = 80
V_CB = 328
V_CG = 336
V_CBB = 344
V_L1G = 352
V_L1B = 368
V_L2G = 384
V_L2B = 400
V_FW = 416
V_FB = 680
NV = 768

DIL_PERM = np.array(list(range(0, 16)) + list(range(32, 48)) + list(range(16, 32)) + list(range(48, 128)))


def host_layout(inp, l):
    w_in = inp["w_in"][l]
    o = {}
    c0 = 0
    cq = w_in[:, 0:512]
    ckv = w_in[:, 512:1024]
    kpe = w_in[:, 1024:1088]
    up = w_in[:, 1088:2112]
    uc = w_in[:, 2112:4160]
    ud = w_in[:, 4160:13376]
    ug = w_in[:, 13376:21568]
    o["w_p1"] = tile_w(np.concatenate([cq, ckv, up], 1), 512)
    o["w_kpe"] = tile_w(kpe, 64)
    a, b = uc[:, :1024], uc[:, 1024:]
    pr = np.stack([a.reshape(D, 8, 128), b.reshape(D, 8, 128)], 2).reshape(D, 2048)
    o["w_p2"] = tile_w(pr, 512)
    ud6 = ud.reshape(D, 3, 3, 8, 128)
    qk = ud6[:, :, 0:2][..., DIL_PERM].reshape(D, 6144)
    o["w_dqk"] = tile_w(qk, 512)
    o["w_dv"] = tile_w(np.ascontiguousarray(ud6[:, :, 2]).reshape(D, 3072), 512)
    o["w_gate"] = tile_w(ug, 512)
    pj = np.concatenate([inp["mla_w_proj"][l], inp["pool_w_proj"][l], inp["conv_w_proj"][l], inp["dil_w_proj"][l]], 1)
    o["w_proj"] = tile_w(pj, 512)
    uq = inp["mla_w_uq"][l].reshape(512, 8, 192)
    o["w_uqn"] = tile_w(np.ascontiguousarray(uq[:, :, :128]).reshape(512, 1024), 512)
    o["w_uqr"] = tile_w(np.ascontiguousarray(uq[:, :, 128:]).reshape(512, 512), 512)
    ukv = inp["mla_w_ukv"][l].reshape(512, 8, 256)
    o["w_ukn"] = tile_w(np.ascontiguousarray(ukv[:, :, :128]).reshape(512, 1024), 512)
    o["w_ukv"] = tile_w(np.ascontiguousarray(ukv[:, :, 128:]).reshape(512, 1024), 512)
    pw = inp["pool_w"][l].reshape(4, 2, 128, 256)
    o["w_pool"] = np.ascontiguousarray(pw.transpose(2, 0, 1, 3)).reshape(128, 2048)
    o["w_mix"] = tile_w(inp["mix_w_out"][l], 512)
    wu = inp["ffn_w_up"][l]
    pr = np.stack([wu[:, :FFN].reshape(D, 44, 128), wu[:, FFN:].reshape(D, 44, 128)], 2).reshape(D, 2 * FFN)
    o["w_up"] = tile_w(pr, 512)
    o["w_dn"] = tile_w(inp["ffn_w_down"][l], 128)
    v = np.zeros((128, NV), np.float32)
    v[:, V_BG:V_BG + 64] = col_vec(inp["b_gate"][l].reshape(-1))
    v[:, V_GQ:V_GQ + 4] = col_vec(inp["mla_gq"][l])
    v[:, V_GKV:V_GKV + 4] = col_vec(inp["mla_gkv"][l])
    v[:, V_PS:V_PS + 8] = col_vec(inp["pool_scale"][l])
    v[:, V_CW:V_CW + 248] = col_vec(inp["conv_dw"][l].reshape(-1))
    v[:, V_CB:V_CB + 8] = col_vec(inp["conv_dw_b"][l])
    v[:, V_CG:V_CG + 8] = col_vec(inp["conv_ln_g"][l])
    v[:, V_CBB:V_CBB + 8] = col_vec(inp["conv_ln_b"][l])
    v[:, V_L1G:V_L1G + 16] = col_vec(inp["ln1_g"][l])
    v[:, V_L1B:V_L1B + 16] = col_vec(inp["ln1_b"][l])
    v[:, V_L2G:V_L2G + 16] = col_vec(inp["ln2_g"][l])
    v[:, V_L2B:V_L2B + 16] = col_vec(inp["ln2_b"][l])
    v[:, V_FW:V_FW + 264] = col_vec(inp["ffn_dw"][l].reshape(-1))
    v[:, V_FB:V_FB + 88] = col_vec(inp["ffn_dw_b"][l])
    o["vecs"] = v
    return o


W_SHAPES = {"w_p1": [4, 128, 8192], "w_kpe": [1, 128, 1024], "w_p2": [4, 128, 8192], "w_dqk": [12, 128, 8192],
            "w_dv": [6, 128, 8192], "w_gate": [16, 128, 8192], "w_proj": [16, 128, 4096], "w_uqn": [2, 128, 2048],
            "w_uqr": [1, 128, 2048], "w_ukn": [2, 128, 2048], "w_ukv": [2, 128, 2048], "w_pool": [128, 2048],
            "w_mix": [4, 128, 8192], "w_up": [22, 128, 8192], "w_dn": [16, 128, 5632], "vecs": [128, NV]}


def host_consts():
    c = np.zeros((128, 2048 + 512 + 8), np.float32)
    i = np.arange(128)
    for kb in range(4):
        for j in range(4):
            blk = np.zeros((128, 128), np.float32) if j < kb else (
                (i[None, :] >= i[:, None]).astype(np.float32) if j == kb else np.ones((128, 128), np.float32))
            c[:, kb * 512 + j * 128: kb * 512 + (j + 1) * 128] = blk
    m = np.concatenate([(i[None, :] <= i[:, None]), (i[None, :] >= i[:, None])], 1).astype(np.float32)
    c[:, 2048:2304] = m
    c[:, 2304:2560] = m
    f64 = THETA ** (-np.arange(32, dtype=np.float32) * 2.0 / 64)
    f32_ = THETA ** (-np.arange(16, dtype=np.float32) * 2.0 / 32)
    c[0:32, 2560] = f64
    c[32:64, 2560] = f64
    c[0:16, 2561] = f32_
    c[32:48, 2561] = f32_
    c[0:32, 2562] = -1
    c[32:64, 2562] = 1
    c[0:16, 2563] = -1
    c[32:48, 2563] = 1
    ic = np.zeros((4, S), np.float32)
    t = np.arange(S)
    for g, w in enumerate((2, 4, 8, 16)):
        ic[g] = 1.0 / np.minimum(t + 1, w)
    return c, ic


class Prog:
    def __init__(self, nc, es, n_layers, debug=()):
        self.nc = nc
        self.es = es
        self.S = Sched(nc, es)
        self.nl = n_layers
        self.debug = debug
        S_ = self.S
        sb = self.sb
        self.ps = [es.enter_context(nc.psum_tensor("ps%d" % i, [128, 512], F32)) for i in range(8)]
        self.ps_i = 0
        self.wslots = [sb("wslot%d" % i, [128, 8192], BF16) for i in range(2)]
        self.ws_i = 0
        self.BFA = sb("BFA", [128, 24576], BF16)
        self.FA = sb("FA", [128, 16384], F32)
        self.vecs = sb("vecs_sb", [128, NV], F32)
        self.cst_f = sb("cst_f", [128, 8], F32)
        self.cst_b = sb("cst_b", [128, 2560], BF16)
        self.ones_b = sb("ones_b", [128, 128], BF16)
        self.ones_f = sb("ones_f", [128, 128], F32)
        self.stg_f = [sb("stg_f%d" % i, [128, 1024], F32) for i in range(2)]
        self.stg_b = [sb("stg_b%d" % i, [128, 1024], BF16) for i in range(2)]
        self.tmp_f = [sb("tmp_f%d" % i, [128, 1024], F32) for i in range(4)]
        self.pt = [sb("pt%d" % i, [128, 512], BF16) for i in range(3)]
        self.halo = sb("halo", [128, 88, 2], F32)
        self.posi = sb("posi", [64, TB], I32)
        self.cnt = {}
        dr = self.dram
        self.xT_in = nc.dram_tensor("xT", [D, S], F32, kind="ExternalInput").ap()
        self.pos_in = nc.dram_tensor("pos", [1, S], I32, kind="ExternalInput").ap()
        self.cst_in = nc.dram_tensor("cst", [128, 2568], F32, kind="ExternalInput").ap()
        self.icnt_in = nc.dram_tensor("icnt", [4, S], F32, kind="ExternalInput").ap()
        self.w = {k: nc.dram_tensor(k, [n_layers] + shp, F32, kind="ExternalInput").ap() for k, shp in W_SHAPES.items()}
        self.out = nc.dram_tensor("yT", [D, S], F32, kind="ExternalOutput").ap()
        self.XF = dr("XF", [D, S], F32)
        self.XB = dr("XB", [D, S], BF16)
        self.X1F = dr("X1F", [D, S], F32)
        self.X1B = dr("X1B", [D, S], BF16)
        self.CQ = dr("CQ", [1024, S], F32)
        self.UP = dr("UP", [1024, S], F32)
        self.GL = dr("GL", [1024, S], F32)
        self.DQ = dr("DQ", [3, 1024, S], BF16)
        self.DK = dr("DK", [3, 1024, S], BF16)
        self.DV = dr("DV", [3, S, 1024], BF16)
        self.MQ = dr("MQ", [8, 192, S], BF16)
        self.MK = dr("MK", [8, 128, S], BF16)
        self.MKPE = dr("MKPE", [64, S], BF16)
        self.MV = dr("MV", [S, 1024], BF16)
        self.ACTS = dr("ACTS", [4, 1024, S], BF16)
        self.MG = dr("MG", [D, S], BF16)
        self.Z = dr("Z", [D, S], F32)
        self.HH = dr("HH", [FFN, S], BF16)
        self.ROPE = dr("ROPE", [4, 64, S], F32)
        self.q_i = 0

    def sb(self, name, shape, dt):
        return self.es.enter_context(self.nc.sbuf_tensor(name, shape, dt))

    def dram(self, name, shape, dt):
        kind = "ExternalOutput" if name in self.debug else "Internal"
        return self.nc.dram_tensor(name, shape, dt, kind=kind).ap()

    def next_ps(self, n):
        if self.ps_i + n > 8:
            self.ps_i = 0
        r = list(range(self.ps_i, self.ps_i + n))
        self.ps_i = (self.ps_i + n) % 8
        return r

    def acc_banks(self):
        i = self.rot("accb", 2)
        return (0, 1) if i == 0 else (2, 3)

    def s_bank(self):
        return 4 + self.rot("sbank", 4)

    def rot(self, name, n):
        i = self.cnt.get(name, 0)
        self.cnt[name] = (i + 1) % n
        return i

    def q(self):
        self.q_i ^= 1
        return "sp" if self.q_i else "act"

    def ld(self, key, out, in_, W, R=(), q=None):
        return self.S.dma(q or "sp", key, out, in_, R=list(R), W=list(W))

    def st(self, key, out, in_, R, W=(), q=None, is_out=False):
        return self.S.dma(q or "sp", key, out, in_, R=list(R), W=list(W), is_out=is_out)

    def linear_fm(self, w_dram, KC, G, groups, xT, xbufs, evac, mw=128, per_evac=1, nch=2):
        S_ = self.S
        tpg = G // mw
        loaded = {}
        groups = list(groups)

        def load(i):
            wi = self.rot("ws", 2)
            S_.dma("pool", "wslot%d" % wi, self.wslots[wi][:, 0:KC * G], w_dram[groups[i]], W=[S_.b("wslot", wi)])
            loaded[i] = wi

        load(0)
        pm, pp = [], []
        for i, g in enumerate(groups):
            if i + 1 < len(groups):
                load(i + 1)
            wi = loaded.pop(i)
            wv = self.wslots[wi][:, 0:KC * G].rearrange("p (k c) -> p k c", c=G)
            for mt in range(tpg):
                banks = self.next_ps(nch)
                for kc in range(KC):
                    for c in range(nch):
                        S_.pe("matmul", self.ps[banks[c]][0:mw, :], lhsT=wv[:, kc, mt * mw:(mt + 1) * mw],
                              rhs=xT[:, kc, c * 512:(c + 1) * 512], start=(kc == 0), stop=(kc == KC - 1),
                              R=[S_.b("wslot", wi)] + list(xbufs), W=[S_.b("ps", banks[c])])
                pm.append((g, mt))
                pp.append(banks)
                if len(pm) == per_evac:
                    evac(pm, pp)
                    pm, pp = [], []
        assert not pm

    def linear_tm(self, w_dram, KC, groups, xT, xbufs, evac):
        S_ = self.S
        G = 512
        for g in groups:
            wi = self.rot("ws", 2)
            S_.dma("pool", "wslot%d" % wi, self.wslots[wi][:, 0:KC * G], w_dram[g], W=[S_.b("wslot", wi)])
            wv = self.wslots[wi][:, 0:KC * G].rearrange("p (k c) -> p k c", c=G)
            for tt in range(8):
                bk = self.next_ps(1)[0]
                for kc in range(KC):
                    S_.pe("matmul", self.ps[bk][:, :], lhsT=xT[:, kc, tt * 128:(tt + 1) * 128], rhs=wv[:, kc, :],
                          start=(kc == 0), stop=(kc == KC - 1),
                          R=[S_.b("wslot", wi)] + list(xbufs), W=[S_.b("ps", bk)])
                evac(g, tt, bk)

    def evac_copy(self, dst, dbuf, bk, rows=128, eng=None):
        S_ = self.S
        e = eng or ("act" if self.rot("ev", 2) == 0 else "dve")
        if e == "act":
            S_.act("copy", out=dst, in_=self.ps[bk][0:rows, :], R=[S_.b("ps", bk)], W=[dbuf])
        else:
            S_.dve("tensor_copy", out=dst, in_=self.ps[bk][0:rows, :], R=[S_.b("ps", bk)], W=[dbuf])

    def load_xT(self, src, t0, dep=()):
        S_ = self.S
        v = self.BFA[:, 0:16384].rearrange("p (k t) -> p k t", t=TB)
        for hlf in range(2):
            self.ld("bfa_x%d" % hlf, v[:, hlf * 8:(hlf + 1) * 8, :],
                    src[hlf * 1024:(hlf + 1) * 1024, t0:t0 + TB].rearrange("(k p) t -> p k t", p=128),
                    R=list(dep), W=[S_.b("bfa_x", hlf)], q=self.q())
        return v, [S_.b("bfa_x", 0), S_.b("bfa_x", 1)]

    def setup(self):
        S_ = self.S
        self.ld("cst_f", self.cst_f[:, :], self.cst_in[:, 2560:2568], W=[S_.b("cst_f")])
        S_.dma("pool", "cst_b", self.cst_b[:, :], self.cst_in[:, 0:2560], W=[S_.b("cst_b")])
        S_.dve("memset", self.ones_f[:, :], 1.0, W=[S_.b("ones_f")])
        S_.dve("memset", self.ones_b[:, :], 1.0, W=[S_.b("ones_b")])
        S_.dve("memset", self.halo[:, :, :], 0.0, W=[S_.b("halo", ct) for ct in range(88)])
        for tb in range(2):
            t0 = tb * TB
            pi = self.posi
            src = self.pos_in[0:1, t0:t0 + TB]
            bsrc = bass.AP(src.tensor, src.offset, [[0, 64], [1, TB]])
            self.ld("posi", pi[:, :], bsrc, W=[S_.b("posi")])
            pf = self.tmp_f[0][0:64, :]
            S_.dve("tensor_copy", out=pf, in_=pi[:, :], R=[S_.b("posi")], W=[S_.b("tmp_f", 0)])
            for which in range(2):
                ang = self.tmp_f[1][0:64, :]
                S_.dve("tensor_scalar", out=ang, in0=pf, scalar1=self.cst_f[0:64, which:which + 1], scalar2=None,
                       op0=ALU.mult, R=[S_.b("tmp_f", 0), S_.b("cst_f")], W=[S_.b("tmp_f", 1)])
                for cs in range(2):
                    red = self.tmp_f[2][0:64, :]
                    S_.dve("tensor_scalar", out=red, in0=ang, scalar1=1.0 / (2 * math.pi), scalar2=(0.25 if cs == 0 else 0.0),
                           op0=ALU.mult, op1=ALU.add, R=[S_.b("tmp_f", 1)], W=[S_.b("tmp_f", 2)])
                    S_.dve("tensor_copy", out=self.posi[:, :], in_=red, R=[S_.b("tmp_f", 2)], W=[S_.b("posi")])
                    tab = self.tmp_f[3][0:64, :]
                    S_.dve("tensor_copy", out=tab, in_=self.posi[:, :], R=[S_.b("posi")], W=[S_.b("tmp_f", 3)])
                    S_.dve("tensor_tensor", out=red, in0=red, in1=tab, op=ALU.subtract, R=[S_.b("tmp_f", 2), S_.b("tmp_f", 3)],
                           W=[S_.b("tmp_f", 2)])
                    S_.dve("scalar_tensor_tensor", out=red, in0=red, scalar=0.5, in1=red, op0=ALU.is_ge, op1=ALU.subtract,
                           R=[S_.b("tmp_f", 2)], W=[S_.b("tmp_f", 2)])
                    S_.act("activation", out=tab, in_=red, func=AF.Sin, scale=-2 * math.pi, R=[S_.b("tmp_f", 2)], W=[S_.b("tmp_f", 3)])
                    if cs == 1:
                        S_.dve("tensor_scalar", out=tab, in0=tab, scalar1=self.cst_f[0:64, 2 + which:3 + which], scalar2=None,
                               op0=ALU.mult, R=[S_.b("tmp_f", 3), S_.b("cst_f")], W=[S_.b("tmp_f", 3)])
                    self.st("tmp_f3", self.ROPE[which * 2 + cs, :, t0:t0 + TB], tab, R=[S_.b("tmp_f", 3)],
                            W=[S_.b("ROPE", tb)])
        for kc in range(16):
            i = self.rot("stg_f", 2)
            for hh in range(2):
                self.ld("stg_f%d" % i, self.stg_f[i][:, :], self.xT_in[kc * 128:(kc + 1) * 128, hh * TB:(hh + 1) * TB],
                        W=[S_.b("stg_f", i)], q=self.q())
                self.st("stg_f%d" % i, self.XF[kc * 128:(kc + 1) * 128, hh * TB:(hh + 1) * TB], self.stg_f[i][:, :],
                        R=[S_.b("stg_f", i)], W=[S_.b("XF", hh)], q=self.q())
                j = self.rot("stg_b", 2)
                S_.dve("tensor_copy", out=self.stg_b[j][:, :], in_=self.stg_f[i][:, :], R=[S_.b("stg_f", i)], W=[S_.b("stg_b", j)])
                self.st("stg_b%d" % j, self.XB[kc * 128:(kc + 1) * 128, hh * TB:(hh + 1) * TB], self.stg_b[j][:, :],
                        R=[S_.b("stg_b", j)], W=[S_.b("XB", hh)], q=self.q())
                i = self.rot("stg_f", 2)

    def load_vecs(self, l):
        S_ = self.S
        self.ld("vecs", self.vecs[:, :], self.w["vecs"][l], W=[S_.b("vecs")])

    def vcol(self, c, rows=128):
        return self.vecs[0:rows, c:c + 1]

    def load_rope(self, which, tb, dst_c, dst_s, bufs):
        S_ = self.S
        t0 = tb * TB
        self.ld(bufs[0].name[0] + "c", dst_c, self.ROPE[which * 2, :, t0:t0 + TB], W=[bufs[0]], R=[S_.b("ROPE", tb)])
        self.ld(bufs[1].name[0] + "s", dst_s, self.ROPE[which * 2 + 1, :, t0:t0 + TB], W=[bufs[1]], R=[S_.b("ROPE", tb)], q="act")

    def rope_evac(self, bk, c, rows, half, cosT, sinT, tbufs, dst, dbuf):
        S_ = self.S
        P = self.ps[bk]
        cs = slice(c * 512, (c + 1) * 512)
        t2 = self.tmp_f[2][0:rows, 0:512]
        t1 = self.tmp_f[3][0:rows, 0:512]
        R0 = [S_.b("ps", bk)] + list(tbufs)
        S_.dve("tensor_tensor", out=t2, in0=P[0:rows, :], in1=cosT[0:rows, cs], op=ALU.mult, R=R0, W=[S_.b("tmp_f", 2)])
        if rows == 48:
            S_.dve("memset", self.tmp_f[3][0:48, 0:512], 0.0, W=[S_.b("tmp_f", 3)])
        S_.dve("tensor_tensor", out=self.tmp_f[3][0:half, 0:512], in0=P[32:32 + half, :], in1=sinT[0:half, cs], op=ALU.mult,
               R=R0, W=[S_.b("tmp_f", 3)])
        S_.dve("tensor_tensor", out=self.tmp_f[3][32:32 + half, 0:512], in0=P[0:half, :], in1=sinT[32:32 + half, cs],
               op=ALU.mult, R=R0, W=[S_.b("tmp_f", 3)])
        S_.dve("tensor_tensor", out=dst, in0=t2, in1=t1, op=ALU.add, R=[S_.b("tmp_f", 2), S_.b("tmp_f", 3)], W=[dbuf])

    def stage_P(self, l, tb):
        S_ = self.S
        t0 = tb * TB
        xv, xb = self.load_xT(self.XB, t0, [S_.b("XB", tb)])
        cosM, sinM = self.FA[0:64, 0:1024], self.FA[0:64, 1024:2048]
        cosD, sinD = self.FA[0:64, 2048:3072], self.FA[0:64, 3072:4096]
        rb = [S_.b("fa_r", i) for i in range(4)]
        self.load_rope(0, tb, cosM, sinM, rb[0:2])
        self.load_rope(1, tb, cosD, sinD, rb[2:4])

        def ev1(pm, pp):
            (g, mt), banks = pm[0], pp[0]
            i = self.rot("stg_f", 2)
            for c in range(2):
                self.evac_copy(self.stg_f[i][:, c * 512:(c + 1) * 512], S_.b("stg_f", i), banks[c])
            dst = self.CQ if g < 2 else self.UP
            row = (g % 2) * 512 + mt * 128
            self.st("stg_f%d" % i, dst[row:row + 128, t0:t0 + TB], self.stg_f[i][:, :], R=[S_.b("stg_f", i)],
                    W=[S_.b("CQ" if g < 2 else "UP", tb)], q=self.q())

        if "p1" not in PSKIP:
            self.linear_fm(self.w["w_p1"][l], 16, 512, range(4), xv, xb, ev1)

        def evk(pm, pp):
            banks = pp[0]
            i = self.rot("stg_b", 2)
            for c in range(2):
                self.rope_evac(banks[c], c, 64, 32, cosM, sinM, rb[0:2], self.stg_b[i][0:64, c * 512:(c + 1) * 512],
                               S_.b("stg_b", i))
            self.st("stg_b%d" % i, self.MKPE[:, t0:t0 + TB], self.stg_b[i][0:64, :], R=[S_.b("stg_b", i)], W=[S_.b("MKPE", tb)])

        if "kpe" not in PSKIP:
            self.linear_fm(self.w["w_kpe"][l], 16, 64, range(1), xv, xb, evk, mw=64)

        def ev2(pm, pp):
            (g, mt), ba = pm[0], pp[0]
            bb = pp[1]
            m = g * 2 + mt // 2
            i = self.rot("stg_f", 2)
            for c in range(2):
                sg = self.tmp_f[0][:, c * 512:(c + 1) * 512]
                S_.act("activation", out=sg, in_=self.ps[bb[c]][:, :], func=AF.Sigmoid, R=[S_.b("ps", bb[c])], W=[S_.b("tmp_f", 0)])
                S_.dve("tensor_tensor", out=self.stg_f[i][:, c * 512:(c + 1) * 512], in0=self.ps[ba[c]][:, :], in1=sg, op=ALU.mult,
                       R=[S_.b("ps", ba[c]), S_.b("tmp_f", 0)], W=[S_.b("stg_f", i)])
            self.st("stg_f%d" % i, self.GL[m * 128:(m + 1) * 128, t0:t0 + TB], self.stg_f[i][:, :], R=[S_.b("stg_f", i)],
                    W=[S_.b("GL", tb)], q=self.q())

        if "p2" not in PSKIP:
            self.linear_fm(self.w["w_p2"][l], 16, 512, range(4), xv, xb, ev2, per_evac=2)

        def ev3(pm, pp):
            (g, mt), banks = pm[0], pp[0]
            tix = g * 4 + mt
            grp, which, h = tix // 16, (tix // 8) % 2, tix % 8
            d = DIL_D[grp]
            i = self.rot("stg_b", 2)
            sb_ = self.stg_b[i]
            for c in range(2):
                P = self.ps[banks[c]]
                cs = slice(c * 512, (c + 1) * 512)
                S_.dve("tensor_copy", out=sb_[64:128, cs], in_=P[64:128, :], R=[S_.b("ps", banks[c])], W=[S_.b("stg_b", i)])
                self.rope_evac(banks[c], c, 64, 32, cosD, sinD, rb[2:4], sb_[0:64, cs], S_.b("stg_b", i))
            dst = (self.DQ if which == 0 else self.DK)[grp, h * 128:(h + 1) * 128, t0:t0 + TB]
            self.st("stg_b%d" % i, dst, sb_[:, :], R=[S_.b("stg_b", i)], W=[S_.b("DQK", tb)], q=self.q())

        self.linear_fm_p3a = ev3
        if "p3a" not in PSKIP:
            self.linear_fm(self.w["w_dqk"][l], 16, 512, range(2 if "short" in PSKIP else 12), xv, xb, ev3)

        def ev4(g, tt, bk):
            grp, hf = g // 2, g % 2
            i = self.rot("pt", 3)
            self.evac_copy(self.pt[i][:, :], S_.b("pt", i), bk)
            self.st("pt%d" % i, self.DV[grp, t0 + tt * 128:t0 + (tt + 1) * 128, hf * 512:(hf + 1) * 512], self.pt[i][:, :],
                    R=[S_.b("pt", i)], W=[S_.b("DV", tb)], q=self.q())

        if "p3b" not in PSKIP:
            self.linear_tm(self.w["w_dv"][l], 16, range(6), xv, xb, ev4)

    def colsum(self, bk, srcs, sbufs):
        S_ = self.S
        n = len(srcs)
        for i, (s, b) in enumerate(zip(srcs, sbufs)):
            S_.pe("matmul", self.ps[bk][:, :], lhsT=self.ones_f[:, :], rhs=s, start=(i == 0), stop=(i == n - 1),
                  R=[S_.b("ones_f"), b], W=[S_.b("ps", bk)])

    def stage_A1(self, l, tb):
        S_ = self.S
        t0 = tb * TB
        cq = self.FA[:, 0:8192].rearrange("p (k t) -> p k t", t=TB)
        for hf in range(2):
            self.ld("fa_cq%d" % hf, cq[:, hf * 4:(hf + 1) * 4, :],
                    self.CQ[hf * 512:(hf + 1) * 512, t0:t0 + TB].rearrange("(k p) t -> p k t", p=128),
                    R=[S_.b("CQ", tb)], W=[S_.b("fa_cq", hf)], q=self.q())
        cosM, sinM = self.FA[0:64, 8192:9216], self.FA[0:64, 9216:10240]
        rb = [S_.b("fa_r", 0), S_.b("fa_r", 1)]
        self.load_rope(0, tb, cosM, sinM, rb)
        xn = self.BFA[:, 0:8192].rearrange("p (k t) -> p k t", t=TB)
        for hf in range(2):
            for c in range(2):
                bk = self.next_ps(1)[0]
                for k in range(4):
                    sq = self.tmp_f[k][:, 0:512]
                    S_.act("activation", out=sq, in_=cq[:, hf * 4 + k, c * 512:(c + 1) * 512], func=AF.Square,
                           R=[S_.b("fa_cq", hf)], W=[S_.b("tmp_f", k)])
                self.colsum(bk, [self.tmp_f[k][:, 0:512] for k in range(4)], [S_.b("tmp_f", k) for k in range(4)])
                rs = self.stg_f[0][:, c * 512:(c + 1) * 512]
                S_.dve("tensor_scalar", out=rs, in0=self.ps[bk][:, :], scalar1=1.0 / 512, scalar2=1e-6, op0=ALU.mult, op1=ALU.add,
                       R=[S_.b("ps", bk)], W=[S_.b("stg_f", 0)])
                S_.act("activation", out=rs, in_=rs, func=AF.Sqrt, R=[S_.b("stg_f", 0)], W=[S_.b("stg_f", 0)])
                S_.dve("reciprocal", out=rs, in_=rs, R=[S_.b("stg_f", 0)], W=[S_.b("stg_f", 0)])
                for k in range(4):
                    gc = self.vcol((V_GQ if hf == 0 else V_GKV) + k)
                    S_.dve("scalar_tensor_tensor", out=xn[:, hf * 4 + k, c * 512:(c + 1) * 512],
                           in0=cq[:, hf * 4 + k, c * 512:(c + 1) * 512], scalar=gc, in1=rs, op0=ALU.mult, op1=ALU.mult,
                           R=[S_.b("fa_cq", hf), S_.b("vecs"), S_.b("stg_f", 0)], W=[S_.b("bfa_xn", hf)])
        xq, xkv = xn[:, 0:4, :], xn[:, 4:8, :]

        def ev_plain(dst_of):
            def ev(pm, pp):
                (g, mt), banks = pm[0], pp[0]
                h = g * 4 + mt
                i = self.rot("stg_b", 2)
                for c in range(2):
                    self.evac_copy(self.stg_b[i][:, c * 512:(c + 1) * 512], S_.b("stg_b", i), banks[c])
                self.st("stg_b%d" % i, dst_of(h), self.stg_b[i][:, :], R=[S_.b("stg_b", i)], W=[S_.b("MQK", tb)], q=self.q())
            return ev

        self.linear_fm(self.w["w_uqn"][l], 4, 512, range(2), xq, [S_.b("bfa_xn", 0)],
                       ev_plain(lambda h: self.MQ[h, 0:128, t0:t0 + TB]))

        def ev_qr(pm, pp):
            (g, mt), banks = pm[0], pp[0]
            h = mt
            i = self.rot("stg_b", 2)
            for c in range(2):
                self.rope_evac(banks[c], c, 64, 32, cosM, sinM, rb, self.stg_b[i][0:64, c * 512:(c + 1) * 512], S_.b("stg_b", i))
            self.st("stg_b%d" % i, self.MQ[h, 128:192, t0:t0 + TB], self.stg_b[i][0:64, :], R=[S_.b("stg_b", i)],
                    W=[S_.b("MQK", tb)], q=self.q())

        self.linear_fm(self.w["w_uqr"][l], 4, 512, range(1), xq, [S_.b("bfa_xn", 0)], ev_qr, mw=64)
        self.linear_fm(self.w["w_ukn"][l], 4, 512, range(2), xkv, [S_.b("bfa_xn", 1)],
                       ev_plain(lambda h: self.MK[h, :, t0:t0 + TB]))

        def ev_v(g, tt, bk):
            i = self.rot("pt", 3)
            self.evac_copy(self.pt[i][:, :], S_.b("pt", i), bk)
            self.st("pt%d" % i, self.MV[t0 + tt * 128:t0 + (tt + 1) * 128, g * 512:(g + 1) * 512], self.pt[i][:, :],
                    R=[S_.b("pt", i)], W=[S_.b("MV", tb)], q=self.q())

        self.linear_tm(self.w["w_ukv"][l], 4, range(2), xkv, [S_.b("bfa_xn", 1)], ev_v)

    def stage_A2(self, l, tb):
        S_ = self.S
        t0 = tb * TB
        nk = t0 + TB
        nkb = nk // 128
        scale = 192 ** -0.5
        kpe = self.BFA[0:64, 0:2048]
        self.ld("bfa_kpe", kpe[:, 0:nk], self.MKPE[:, 0:nk], R=[S_.b("MKPE", 0), S_.b("MKPE", 1)], W=[S_.b("bfa_kpe")])
        masks = self.cst_b[:, 0:2048]
        for h in range(8):
            s = h % 2
            base = 2048 + s * 8192
            qn = self.BFA[:, base:base + 1024]
            qr = self.BFA[0:64, base + 1024:base + 2048]
            kn = self.BFA[:, base + 2048:base + 4096]
            vv = self.BFA[:, base + 4096:base + 6144].rearrange("p (k d) -> p k d", d=128)
            bq, bk_, bv = S_.b("a2_q", s), S_.b("a2_k", s), S_.b("a2_v", s)
            dep = [S_.b("MQK", 0), S_.b("MQK", 1)]
            self.ld("a2_qn%d" % s, qn, self.MQ[h, 0:128, t0:t0 + TB], R=dep, W=[bq])
            self.ld("a2_qr%d" % s, qr, self.MQ[h, 128:192, t0:t0 + TB], R=dep, W=[bq], q="act")
            self.ld("a2_k%d" % s, kn[:, 0:nk], self.MK[h, :, 0:nk], R=dep, W=[bk_])
            self.ld("a2_v%d" % s, vv[:, 0:nkb, :], self.MV[0:nk, h * 128:(h + 1) * 128].rearrange("(k p) d -> p k d", p=128),
                    R=[S_.b("MV", 0), S_.b("MV", 1)], W=[bv], q="act")
            for qc in range(2):
                qb0 = t0 // 128 + qc * 4
                nkeys = qb0 + 4
                ob, zb = self.acc_banks()
                for kb in range(nkeys):
                    sb_ = self.s_bank()
                    qs = slice(qc * 512, (qc + 1) * 512)
                    S_.pe("matmul", self.ps[sb_][:, :], lhsT=kn[:, kb * 128:(kb + 1) * 128], rhs=qn[:, qs], start=True, stop=False,
                          R=[bk_, bq], W=[S_.b("ps", sb_)])
                    S_.pe("matmul", self.ps[sb_][:, :], lhsT=kpe[:, kb * 128:(kb + 1) * 128], rhs=qr[:, qs], start=False, stop=True,
                          R=[S_.b("bfa_kpe"), bq], W=[S_.b("ps", sb_)])
                    pi = self.rot("pt", 3)
                    P = self.pt[pi]
                    S_.act("activation", out=P[:, :], in_=self.ps[sb_][:, :], func=AF.Exp, scale=scale,
                           R=[S_.b("ps", sb_)], W=[S_.b("pt", pi)])
                    if kb >= qb0:
                        i = kb - qb0
                        S_.dve("tensor_tensor", out=P[:, :], in0=P[:, :], in1=masks[:, i * 512:(i + 1) * 512], op=ALU.mult,
                               R=[S_.b("pt", pi), S_.b("cst_b")], W=[S_.b("pt", pi)])
                    S_.pe("matmul", self.ps[ob][:, :], lhsT=vv[:, kb, :], rhs=P[:, :], start=(kb == 0), stop=(kb == nkeys - 1),
                          R=[bv, S_.b("pt", pi)], W=[S_.b("ps", ob)])
                    S_.pe("matmul", self.ps[zb][:, :], lhsT=self.ones_b[:, :], rhs=P[:, :], start=(kb == 0), stop=(kb == nkeys - 1),
                          R=[S_.b("ones_b"), S_.b("pt", pi)], W=[S_.b("ps", zb)])
                rc = self.tmp_f[0][:, 0:512]
                S_.dve("reciprocal", out=rc, in_=self.ps[zb][:, :], R=[S_.b("ps", zb)], W=[S_.b("tmp_f", 0)])
                i = self.rot("stg_b", 2)
                S_.dve("tensor_tensor", out=self.stg_b[i][:, 0:512], in0=self.ps[ob][:, :], in1=rc, op=ALU.mult,
                       R=[S_.b("ps", ob), S_.b("tmp_f", 0)], W=[S_.b("stg_b", i)])
                self.st("stg_b%d" % i, self.ACTS[0, h * 128:(h + 1) * 128, t0 + qc * 512:t0 + (qc + 1) * 512], self.stg_b[i][:, 0:512],
                        R=[S_.b("stg_b", i)], W=[S_.b("ACTS", 0, tb)], q=self.q())

    def stage_B(self, l, tb):
        S_ = self.S
        t0 = tb * TB
        HL = 16
        up = self.FA[:, 0:8 * 1040].rearrange("p (k t) -> p k t", t=1040)
        ic = self.FA[:, 8320:8320 + 4096].rearrange("p (g t) -> p g t", t=TB)
        src = self.icnt_in[0:4, t0:t0 + TB]
        self.ld("fa_ic", ic, bass.AP(src.tensor, src.offset, [[0, 128], [S, 4], [1, TB]]), W=[S_.b("fa_ic")])
        pw = self.wslots[1][:, 0:2048].rearrange("p (g o) -> p g o", o=256)
        S_.dma("pool", "wslot1", self.wslots[1][:, 0:2048], self.w["w_pool"][l], W=[S_.b("wslot", 1)])
        dT = self.BFA[:, 0:8192].rearrange("p (k t) -> p k t", t=TB)
        if tb == 0:
            S_.dve("memset", up[:, :, 0:HL], 0.0, W=[S_.b("fa_up")])
            self.ld("fa_up", up[:, :, HL:HL + TB], self.UP[:, 0:TB].rearrange("(k p) t -> p k t", p=128),
                    R=[S_.b("UP", 0)], W=[S_.b("fa_up")])
        else:
            self.ld("fa_up", up[:, :, :], self.UP[:, t0 - HL:t0 + TB].rearrange("(k p) t -> p k t", p=128),
                    R=[S_.b("UP", 0), S_.b("UP", 1)], W=[S_.b("fa_up")])
        for ct in range(8):
            g = ct // 2
            cur = up[:, ct, :]
            cb = S_.b("fa_up")
            lo = 0
            step = 1
            n = 0
            while step < (2 << g):
                ti = n % 2
                nxt = self.tmp_f[ti][:, :]
                nv = self.FA[:, 12416 + ti * 1040:12416 + (ti + 1) * 1040]
                S_.dve("tensor_tensor", out=nv[:, step:1040], in0=cur[:, step:1040], in1=cur[:, 0:1040 - step], op=ALU.add,
                       R=[cb], W=[S_.b("fa_pp", ti)])
                cur = nv
                cb = S_.b("fa_pp", ti)
                step *= 2
                n += 1
            mean = self.tmp_f[2][:, :]
            S_.dve("tensor_tensor", out=mean, in0=cur[:, HL:HL + TB], in1=ic[:, g, :], op=ALU.mult, R=[cb, S_.b("fa_ic")],
                   W=[S_.b("tmp_f", 2)])
            S_.dve("tensor_tensor", out=dT[:, ct, :], in0=mean, in1=up[:, ct, HL:HL + TB], op=ALU.subtract,
                   R=[S_.b("tmp_f", 2), S_.b("fa_up")], W=[S_.b("bfa_d")])
        for ct in range(8):
            g, hf = ct // 2, ct % 2
            banks = self.next_ps(2)
            for kc in range(2):
                for c in range(2):
                    S_.pe("matmul", self.ps[banks[c]][:, :], lhsT=pw[:, g * 2 + kc, hf * 128:(hf + 1) * 128],
                          rhs=dT[:, g * 2 + kc, c * 512:(c + 1) * 512], start=(kc == 0), stop=(kc == 1),
                          R=[S_.b("wslot", 1), S_.b("bfa_d")], W=[S_.b("ps", banks[c])])
            i = self.rot("stg_b", 2)
            for c in range(2):
                S_.act("activation", out=self.stg_b[i][:, c * 512:(c + 1) * 512], in_=self.ps[banks[c]][:, :], func=AF.Copy,
                       scale=self.vcol(V_PS + ct), R=[S_.b("ps", banks[c]), S_.b("vecs")], W=[S_.b("stg_b", i)])
            self.st("stg_b%d" % i, self.ACTS[1, ct * 128:(ct + 1) * 128, t0:t0 + TB], self.stg_b[i][:, :], R=[S_.b("stg_b", i)],
                    W=[S_.b("ACTS", 1, tb)], q=self.q())

    def ln_stats(self, tiles, tbufs, nfeat, eps, mean, rstd, mbuf):
        S_ = self.S
        b1, b2 = self.next_ps(2)
        self.colsum(b1, tiles, tbufs)
        n = len(tiles)
        for i, (s, b) in enumerate(zip(tiles, tbufs)):
            k = self.rot("sq", 2)
            sq = self.tmp_f[k][:, 0:512]
            S_.act("activation", out=sq, in_=s, func=AF.Square, R=[b], W=[S_.b("tmp_f", k)])
            S_.pe("matmul", self.ps[b2][:, :], lhsT=self.ones_f[:, :], rhs=sq, start=(i == 0), stop=(i == n - 1),
                  R=[S_.b("ones_f"), S_.b("tmp_f", k)], W=[S_.b("ps", b2)])
        S_.dve("tensor_scalar", out=mean, in0=self.ps[b1][:, :], scalar1=1.0 / nfeat, scalar2=None, op0=ALU.mult,
               R=[S_.b("ps", b1)], W=[mbuf])
        m2 = self.tmp_f[2][:, 512:1024]
        S_.dve("tensor_tensor", out=m2, in0=mean, in1=mean, op=ALU.mult, R=[mbuf], W=[S_.b("tmp_f", 2)])
        S_.dve("scalar_tensor_tensor", out=rstd, in0=self.ps[b2][:, :], scalar=1.0 / nfeat, in1=m2, op0=ALU.mult, op1=ALU.subtract,
               R=[S_.b("ps", b2), S_.b("tmp_f", 2)], W=[mbuf])
        S_.dve("tensor_scalar", out=rstd, in0=rstd, scalar1=eps, scalar2=None, op0=ALU.add, R=[mbuf], W=[mbuf])
        S_.act("activation", out=rstd, in_=rstd, func=AF.Sqrt, R=[mbuf], W=[mbuf])
        S_.dve("reciprocal", out=rstd, in_=rstd, R=[mbuf], W=[mbuf])

    def stage_C(self, l, tb):
        S_ = self.S
        t0 = tb * TB
        HL = 30
        acc = self.FA[:, 0:8192].rearrange("p (k t) -> p k t", t=TB)
        for ct in range(8):
            i = ct % 2
            gl = self.FA[:, 8192 + i * 1054:8192 + (i + 1) * 1054]
            gb = S_.b("fa_gl", i)
            if tb == 0:
                S_.dve("memset", gl[:, 0:HL], 0.0, W=[gb])
                self.ld("fa_gl%d" % i, gl[:, HL:HL + TB], self.GL[ct * 128:(ct + 1) * 128, 0:TB], R=[S_.b("GL", 0)], W=[gb], q=self.q())
            else:
                self.ld("fa_gl%d" % i, gl[:, :], self.GL[ct * 128:(ct + 1) * 128, t0 - HL:t0 + TB], R=[S_.b("GL", 0), S_.b("GL", 1)],
                        W=[gb], q=self.q())
            a = acc[:, ct, :]
            ab = S_.b("fa_acc", ct)
            S_.act("activation", out=a, in_=gl[:, 30:30 + TB], func=AF.Identity, scale=self.vcol(V_CW + 30 * 8 + ct),
                   bias=self.vcol(V_CB + ct), R=[gb, S_.b("vecs")], W=[ab])
            for k in range(30):
                S_.dve("scalar_tensor_tensor", out=a, in0=gl[:, k:k + TB], scalar=self.vcol(V_CW + k * 8 + ct), in1=a,
                       op0=ALU.mult, op1=ALU.add, R=[gb, S_.b("vecs"), ab], W=[ab])
        for c in range(2):
            mean = self.FA[:, 10400:10912]
            rstd = self.FA[:, 10912:11424]
            mb = S_.b("fa_mr")
            self.ln_stats([acc[:, ct, c * 512:(c + 1) * 512] for ct in range(8)], [S_.b("fa_acc", ct) for ct in range(8)],
                          1024, 1e-5, mean, rstd, mb)
            for ct in range(8):
                t = self.tmp_f[3][:, 0:512]
                S_.dve("tensor_tensor", out=t, in0=acc[:, ct, c * 512:(c + 1) * 512], in1=mean, op=ALU.subtract,
                       R=[S_.b("fa_acc", ct), mb], W=[S_.b("tmp_f", 3)])
                S_.dve("tensor_tensor", out=t, in0=t, in1=rstd, op=ALU.mult, R=[S_.b("tmp_f", 3), mb], W=[S_.b("tmp_f", 3)])
                i = self.rot("pt", 3)
                S_.act("activation", out=self.pt[i][:, :], in_=t, func=AF.Silu, scale=self.vcol(V_CG + ct), bias=self.vcol(V_CBB + ct),
                       R=[S_.b("tmp_f", 3), S_.b("vecs")], W=[S_.b("pt", i)])
                self.st("pt%d" % i, self.ACTS[2, ct * 128:(ct + 1) * 128, t0 + c * 512:t0 + (c + 1) * 512], self.pt[i][:, :],
                        R=[S_.b("pt", i)], W=[S_.b("ACTS", 2, tb)], q=self.q())

    def stage_D(self, l, tb):
        S_ = self.S
        t0 = tb * TB
        scale = 128 ** -0.5
        dm = self.cst_b[:, 2048:2560]
        diag = self.cst_b[:, 2048 + 128:2048 + 256]
        for h in range(8):
            accU = self.FA[:, 0:1024]
            accZ = self.FA[:, 1024:2048]
            au, az = S_.b("fa_accU"), S_.b("fa_accZ")
            for gi, d in enumerate(DIL_D):
                s = (h * 3 + gi) % 2
                base = s * 8192
                L = S // d
                n = TB // d
                Q = self.BFA[:, base:base + 1024].rearrange("p (j r) -> p r j", r=d)
                K = self.BFA[:, base + 1024:base + 3072].rearrange("p (j r) -> p r j", r=d)
                bq, bk_ = S_.b("d_q", s), S_.b("d_k", s)
                dep = [S_.b("DQK", 0), S_.b("DQK", 1)]
                self.ld("d_q%d" % s, self.BFA[:, base:base + 1024], self.DQ[gi, h * 128:(h + 1) * 128, t0:t0 + TB], R=dep, W=[bq])
                self.ld("d_k%d" % s, self.BFA[:, base + 1024:base + 3072], self.DK[gi, h * 128:(h + 1) * 128, :], R=dep, W=[bk_], q="act")
                units = []
                if d == 16:
                    for r in range(16):
                        qa = 64 * tb
                        nkk = 64 * (tb + 1)
                        units.append((r, qa, 64, [(0, nkk, diag[0:nkk, qa:qa + 64])]))
                else:
                    for r in range(d):
                        for nb in range(n // 128):
                            blk = tb * (n // 128) + nb
                            kt = []
                            if blk > 0:
                                kt.append(((blk - 1) * 128, 128, dm[:, 0:128]))
                            kt.append((blk * 128, 128, dm[:, 128:256]))
                            units.append((r, blk * 128, 128, kt))
                vt = self.BFA[:, base + 3072:base + 3072 + 4096].rearrange("p (k d) -> p k d", d=128)
                bv = S_.b("d_v", s)
                vmap = {}
                for (r, qa, nq, kts) in units:
                    for (klo, nkk, m) in kts:
                        if (r, klo) in vmap:
                            continue
                        vi = len(vmap)
                        vmap[(r, klo)] = vi
                        src = self.DV[gi, :, h * 128:(h + 1) * 128].rearrange("(j r) c -> r j c", r=d)[r, klo:klo + nkk, :]
                        self.ld("d_v%d" % s, vt[0:nkk, vi, :], src, R=[S_.b("DV", 0), S_.b("DV", 1)], W=[bv], q=self.q())
                for u0 in range(0, len(units), 2):
                    grp = units[u0:u0 + 2]
                    sbk = self.s_bank()
                    ub, zb = self.acc_banks()
                    pi = self.rot("pt", 3)
                    P = self.pt[pi]
                    col = 0
                    lay = []
                    for (r, qa, nq, kts) in grp:
                        ql = qa - tb * n
                        for (klo, nkk, m) in kts:
                            S_.pe("matmul", self.ps[sbk][0:nkk, col:col + nq], lhsT=K[:, r, klo:klo + nkk], rhs=Q[:, r, ql:ql + nq],
                                  start=True, stop=True, R=[bk_, bq], W=[S_.b("ps", sbk)])
                            lay.append((r, qa, nq, klo, nkk, m, col))
                            col += nq
                    rows = max(x[4] for x in lay)
                    S_.act("activation", out=P[0:rows, 0:col], in_=self.ps[sbk][0:rows, 0:col], func=AF.Exp, scale=scale,
                           R=[S_.b("ps", sbk)], W=[S_.b("pt", pi)])
                    for (r, qa, nq, klo, nkk, m, c0) in lay:
                        S_.dve("tensor_tensor", out=P[0:nkk, c0:c0 + nq], in0=P[0:nkk, c0:c0 + nq], in1=m, op=ALU.mult,
                               R=[S_.b("pt", pi), S_.b("cst_b")], W=[S_.b("pt", pi)])
                    oc = 0
                    outs = []
                    for ui, (r, qa, nq, kts) in enumerate(grp):
                        mine = [x for x in lay if x[0] == r and x[1] == qa]
                        for j, (r_, qa_, nq_, klo, nkk, m, c0) in enumerate(mine):
                            vi = vmap[(r, klo)]
                            S_.pe("matmul", self.ps[ub][:, oc:oc + nq], lhsT=vt[0:nkk, vi, :], rhs=P[0:nkk, c0:c0 + nq],
                                  start=(j == 0), stop=(j == len(mine) - 1), R=[bv, S_.b("pt", pi)], W=[S_.b("ps", ub)])
                        for j, (r_, qa_, nq_, klo, nkk, m, c0) in enumerate(mine):
                            S_.pe("matmul", self.ps[zb][:, oc:oc + nq], lhsT=self.ones_b[0:nkk, :], rhs=P[0:nkk, c0:c0 + nq],
                                  start=(j == 0), stop=(j == len(mine) - 1), R=[S_.b("ones_b"), S_.b("pt", pi)], W=[S_.b("ps", zb)])
                        outs.append((r, qa, nq, oc))
                        oc += nq
                    for (r, qa, nq, oc_) in outs:
                        ql = qa - tb * n
                        for (accT, ab, bank) in ((accU, au, ub), (accZ, az, zb)):
                            if d == 1:
                                dst = accT[:, ql:ql + nq]
                            else:
                                dst = accT.rearrange("p (j r) -> p r j", r=d)[:, r, ql:ql + nq]
                            src = self.ps[bank][:, oc_:oc_ + nq]
                            if gi == 0:
                                S_.act("copy", out=dst, in_=src, R=[S_.b("ps", bank)], W=[ab])
                            else:
                                S_.dve("tensor_tensor", out=dst, in0=src, in1=dst, op=ALU.add, R=[S_.b("ps", bank), ab], W=[ab])
            rc = self.tmp_f[0][:, :]
            S_.dve("reciprocal", out=rc, in_=accZ, R=[az], W=[S_.b("tmp_f", 0)])
            i = self.rot("stg_b", 2)
            S_.dve("tensor_tensor", out=self.stg_b[i][:, :], in0=accU, in1=rc, op=ALU.mult, R=[au, S_.b("tmp_f", 0)], W=[S_.b("stg_b", i)])
            self.st("stg_b%d" % i, self.ACTS[3, h * 128:(h + 1) * 128, t0:t0 + TB], self.stg_b[i][:, :], R=[S_.b("stg_b", i)],
                    W=[S_.b("ACTS", 3, tb)], q=self.q())

    def stage_M(self, l, tb):
        S_ = self.S
        t0 = tb * TB
        xv, xb = self.load_xT(self.XB, t0, [S_.b("XB", tb)])
        mg = self.FA[:, :].rearrange("p (k t) -> p k t", t=TB)
        for b in range(4):
            s = 0
            av = self.BFA[:, 16384:24576].rearrange("p (k t) -> p k t", t=TB)
            ab = S_.b("bfa_act", s)
            self.ld("bfa_act%d" % s, av, self.ACTS[b, :, t0:t0 + TB].rearrange("(k p) t -> p k t", p=128),
                    R=[S_.b("ACTS", b, tb)], W=[ab], q=self.q())
            for mgp in range(4):
                gate_ps = {}

                def evg(pm, pp):
                    for (g, mt), banks in zip(pm, pp):
                        m = mgp * 4 + mt
                        for c in range(2):
                            k = self.rot("sg", 2)
                            sg = self.stg_f[k][:, c * 512:(c + 1) * 512]
                            S_.act("activation", out=sg, in_=self.ps[banks[c]][:, :], func=AF.Sigmoid, bias=self.vcol(V_BG + b * 16 + m),
                                   R=[S_.b("ps", banks[c]), S_.b("vecs")], W=[S_.b("stg_f", k)])
                            gate_ps[(mt, c)] = (sg, S_.b("stg_f", k))
                        self._proj_tile(l, b, m, mt, av, ab, gate_ps, mg)

                self._proj_w = None
                self.linear_fm(self.w["w_gate"][l], 16, 512, [b * 4 + mgp], xv, xb, evg)
        for m in range(16):
            i = self.rot("stg_b", 2)
            S_.dve("tensor_copy", out=self.stg_b[i][:, :], in_=mg[:, m, :], R=[S_.b("fa_mg", m)], W=[S_.b("stg_b", i)])
            self.st("stg_b%d" % i, self.MG[m * 128:(m + 1) * 128, t0:t0 + TB], self.stg_b[i][:, :], R=[S_.b("stg_b", i)],
                    W=[S_.b("MG", tb)], q=self.q())

    def _proj_tile(self, l, b, m, mt, av, ab, gate_ps, mg):
        S_ = self.S
        if mt == 0:
            wi = self.rot("ws", 2)
            S_.dma("pool", "wslot%d" % wi, self.wslots[wi][:, 0:4096], self.w["w_proj"][l][b * 4 + m // 4], W=[S_.b("wslot", wi)])
            self._proj_w = wi
        wi = self._proj_w
        wv = self.wslots[wi][:, 0:4096].rearrange("p (k c) -> p k c", c=512)
        banks = self.next_ps(2)
        for kc in range(8):
            for c in range(2):
                S_.pe("matmul", self.ps[banks[c]][:, :], lhsT=wv[:, kc, mt * 128:(mt + 1) * 128], rhs=av[:, kc, c * 512:(c + 1) * 512],
                      start=(kc == 0), stop=(kc == 7), R=[S_.b("wslot", wi), ab], W=[S_.b("ps", banks[c])])
        for c in range(2):
            sg, sgb = gate_ps[(mt, c)]
            dst = mg[:, m, c * 512:(c + 1) * 512]
            if b == 0:
                S_.dve("tensor_tensor", out=dst, in0=self.ps[banks[c]][:, :], in1=sg, op=ALU.mult, R=[S_.b("ps", banks[c]), sgb],
                       W=[S_.b("fa_mg", m)])
            else:
                t = self.tmp_f[c][:, 0:512]
                S_.dve("tensor_tensor", out=t, in0=self.ps[banks[c]][:, :], in1=sg, op=ALU.mult, R=[S_.b("ps", banks[c]), sgb],
                       W=[S_.b("tmp_f", c)])
                S_.pool("tensor_tensor", out=dst, in0=dst, in1=t, op=ALU.add, R=[S_.b("tmp_f", c), S_.b("fa_mg", m)], W=[S_.b("fa_mg", m)])

    def stage_resid(self, w_dram, KC, G, ngroups, src_act, act_dep, res, res_dep, tb, nch, c0=0):
        S_ = self.S
        t0 = tb * TB + c0
        W_ = nch * 512
        av = self.BFA[:, 0:KC * W_].rearrange("p (k t) -> p k t", t=W_)
        kh = KC // 2
        for hf in range(2):
            ks = slice(hf * kh, (hf + 1) * kh) if hf == 0 else slice(kh, KC)
            k0, k1 = (0, kh) if hf == 0 else (kh, KC)
            self.ld("bfa_x%d" % hf, av[:, k0:k1, :], src_act[k0 * 128:k1 * 128, t0:t0 + W_].rearrange("(k p) t -> p k t", p=128),
                    R=act_dep, W=[S_.b("bfa_x", hf)], q=self.q())
        xb = [S_.b("bfa_x", 0), S_.b("bfa_x", 1)]

        def ev(pm, pp):
            (g, mt), banks = pm[0], pp[0]
            m = g * (G // 128) + mt
            i = self.rot("stg_f", 2)
            k = self.rot("rs", 2)
            rs = self.tmp_f[k][:, 0:W_]
            self.ld("tmp_f%d" % k, rs, res[m * 128:(m + 1) * 128, t0:t0 + W_], R=res_dep, W=[S_.b("tmp_f", k)], q=self.q())
            for c in range(nch):
                S_.dve("scalar_tensor_tensor", out=self.stg_f[i][:, c * 512:(c + 1) * 512], in0=rs[:, c * 512:(c + 1) * 512], scalar=ALPHA,
                       in1=self.ps[banks[c]][:, :], op0=ALU.mult, op1=ALU.add, R=[S_.b("tmp_f", k), S_.b("ps", banks[c])],
                       W=[S_.b("stg_f", i)])
            self.st("stg_f%d" % i, self.Z[m * 128:(m + 1) * 128, t0:t0 + W_], self.stg_f[i][:, 0:W_], R=[S_.b("stg_f", i)],
                    W=[S_.b("Z", tb)], q=self.q())

        self.linear_fm(w_dram, KC, G, range(ngroups), av, xb, ev, nch=nch)

    def stage_N(self, l, tb, gcol, bcol, dstF, dstB, fkey, final=False):
        S_ = self.S
        for c in range(2):
            t0 = tb * TB + c * 512
            z = self.FA[:, 0:8192].rearrange("p (k t) -> p k t", t=512)
            for hf in range(2):
                self.ld("fa_z%d" % hf, z[:, hf * 8:(hf + 1) * 8, :], self.Z[hf * 1024:(hf + 1) * 1024, t0:t0 + 512].rearrange("(k p) t -> p k t", p=128),
                        R=[S_.b("Z", tb)], W=[S_.b("fa_z", hf)], q=self.q())
            mean = self.FA[:, 8192:8704]
            rstd = self.FA[:, 8704:9216]
            mb = S_.b("fa_mr")
            self.ln_stats([z[:, k, :] for k in range(16)], [S_.b("fa_z", k // 8) for k in range(16)], 2048, 1e-5, mean, rstd, mb)
            for k in range(16):
                t = self.tmp_f[3][:, 0:512]
                S_.dve("tensor_tensor", out=t, in0=z[:, k, :], in1=mean, op=ALU.subtract, R=[S_.b("fa_z", k // 8), mb], W=[S_.b("tmp_f", 3)])
                S_.dve("tensor_tensor", out=t, in0=t, in1=rstd, op=ALU.mult, R=[S_.b("tmp_f", 3), mb], W=[S_.b("tmp_f", 3)])
                i = self.rot("stg_f", 2)
                o = self.stg_f[i][:, 0:512]
                S_.act("activation", out=o, in_=t, func=AF.Identity, scale=self.vcol(gcol + k), bias=self.vcol(bcol + k),
                       R=[S_.b("tmp_f", 3), S_.b("vecs")], W=[S_.b("stg_f", i)])
                if final:
                    self.st("stg_f%d" % i, self.out[k * 128:(k + 1) * 128, t0:t0 + 512], o, R=[S_.b("stg_f", i)], q=self.q(), is_out=True)
                    continue
                self.st("stg_f%d" % i, dstF[k * 128:(k + 1) * 128, t0:t0 + 512], o, R=[S_.b("stg_f", i)], W=[S_.b(fkey + "F", tb)], q=self.q())
                j = self.rot("pt", 3)
                S_.pool("tensor_copy", out=self.pt[j][:, :], in_=o, R=[S_.b("stg_f", i)], W=[S_.b("pt", j)])
                self.st("pt%d" % j, dstB[k * 128:(k + 1) * 128, t0:t0 + 512], self.pt[j][:, :], R=[S_.b("pt", j)], W=[S_.b(fkey + "B", tb)], q=self.q())

    def stage_F1(self, l, tb):
        S_ = self.S
        t0 = tb * TB
        xv, xb = self.load_xT(self.X1B, t0, [S_.b("X1B", tb)])
        def ev(pm, pp):
            g, mt0 = pm[0]
            m = g * 2 + mt0 // 2
            res = []
            for j, ((g_, mt), banks) in enumerate(zip(pm, pp)):
                ct = m + 44 * j
                k = self.rot("hb", 4)
                hb = self.FA[:, k * 1026:(k + 1) * 1026]
                hbb = S_.b("fa_hb", k)
                if tb == 0:
                    S_.dve("memset", hb[:, 0:2], 0.0, W=[hbb])
                else:
                    S_.dve("tensor_copy", out=hb[:, 0:2], in_=self.halo[:, ct, :], R=[S_.b("halo", ct)], W=[hbb])
                for c in range(2):
                    self.evac_copy(hb[:, 2 + c * 512:2 + (c + 1) * 512], hbb, banks[c], eng="act")
                S_.dve("tensor_copy", out=self.halo[:, ct, :], in_=hb[:, 1024:1026], R=[hbb], W=[S_.b("halo", ct)])
                a = self.FA[:, 4200 + k * 1024:4200 + (k + 1) * 1024]
                ab = S_.b("fa_cv", k)
                S_.dve("tensor_scalar", out=a, in0=hb[:, 2:1026], scalar1=self.vcol(V_FW + 2 * 88 + ct), scalar2=self.vcol(V_FB + ct),
                       op0=ALU.mult, op1=ALU.add, R=[hbb, S_.b("vecs")], W=[ab])
                S_.dve("scalar_tensor_tensor", out=a, in0=hb[:, 1:1025], scalar=self.vcol(V_FW + 88 + ct), in1=a, op0=ALU.mult, op1=ALU.add,
                       R=[hbb, S_.b("vecs"), ab], W=[ab])
                S_.dve("scalar_tensor_tensor", out=a, in0=hb[:, 0:1024], scalar=self.vcol(V_FW + ct), in1=a, op0=ALU.mult, op1=ALU.add,
                       R=[hbb, S_.b("vecs"), ab], W=[ab])
                res.append((a, ab))
            (va, vb), (ga, gb) = res
            S_.act("activation", out=ga, in_=ga, func=AF.Silu, R=[gb], W=[gb])
            i = self.rot("stg_b", 2)
            S_.dve("tensor_tensor", out=self.stg_b[i][:, :], in0=va, in1=ga, op=ALU.mult, R=[vb, gb], W=[S_.b("stg_b", i)])
            self.st("stg_b%d" % i, self.HH[m * 128:(m + 1) * 128, t0:t0 + TB], self.stg_b[i][:, :], R=[S_.b("stg_b", i)],
                    W=[S_.b("HH", tb)], q=self.q())

        self.linear_fm(self.w["w_up"][l], 16, 512, range(22), xv, xb, ev, per_evac=2)

    def layer(self, l, last):
        S_ = self.S
        self.load_vecs(l)
        for tb in range(2):
            self.stage_P(l, tb)
            self.stage_A1(l, tb)
            self.stage_A2(l, tb)
            self.stage_B(l, tb)
            self.stage_C(l, tb)
            self.stage_D(l, tb)
            self.stage_M(l, tb)
            self.stage_resid(self.w["w_mix"][l], 16, 512, 4, self.MG, [S_.b("MG", tb)], self.XF, [S_.b("XF", tb)], tb, 2)
            self.stage_N(l, tb, V_L1G, V_L1B, self.X1F, self.X1B, "X1")
        for tb in range(2):
            self.stage_F1(l, tb)
            for c in range(2):
                self.stage_resid(self.w["w_dn"][l], 44, 128, 16, self.HH, [S_.b("HH", tb)], self.X1F, [S_.b("X1F", tb)], tb, 1, c0=c * 512)
            self.stage_N(l, tb, V_L2G, V_L2B, self.XF, self.XB, "X", final=last)


def build(n_layers, debug=(), stages=None):
    nc = bass.Bass("TRN2", target_bir_lowering=False)
    es = ExitStack()
    P = Prog(nc, es, n_layers, debug)
    P.setup()
    if stages is not None:
        P.load_vecs(0)
        for st in stages:
            if st == "none":
                continue
            name, tb = st[:-1], int(st[-1])
            getattr(P, "stage_" + name)(0, tb)
    for l in range(n_layers if stages is None else 0):
        P.layer(l, last=(l == n_layers - 1))
    P.S.finish()
    P.S.emit()
    es.close()
    return nc, P


def make_inputs(inp, b, n_layers):
    cst, icnt = host_consts()
    m = {"xT": np.ascontiguousarray(inp["x"][b].T), "pos": np.ascontiguousarray(inp["positions"][b][None, :]).astype(np.int32),
         "cst": cst, "icnt": icnt}
    return m


def kernel(**inputs):
    inp = {k: np.asarray(v) for k, v in inputs.items()}
    nl = DEPTH
    nc, P = build(nl)
    per_layer = [host_layout(inp, l) for l in range(nl)]
    wts = {k: np.stack([per_layer[l][k] for l in range(nl)], 0) for k in W_SHAPES}
    in_maps = []
    for b in range(4):
        m = make_inputs(inp, b, nl)
        m.update(wts)
        in_maps.append(m)
    res = run_bass_kernel_spmd(nc, in_maps, core_ids=list(range(4)))
    out = np.stack([np.ascontiguousarray(res.results[b]["yT"].T) for b in range(4)], 0)
    return out.astype(np.float32)
```

```python
import math
import numpy as np
from contextlib import ExitStack
import concourse.bass as bass
import concourse.mybir as mybir
from concourse.bass_utils import run_bass_kernel_spmd

F32 = mybir.dt.float32
BF16 = mybir.dt.bfloat16
I32 = mybir.dt.int32
AF = mybir.ActivationFunctionType
ALU = mybir.AluOpType

D = 2048
S = 2048
TB = 1024
DEPTH = 4
FFN = 5632
ALPHA = (2 * DEPTH) ** 0.25
THETA = 500000.0
DIL_D = (1, 4, 16)
PSKIP = set()


class Buf:
    __slots__ = ("name", "lw", "rd_c", "rd_d")

    def __init__(self, name):
        self.name = name
        self.lw = None
        self.rd_c = {}
        self.rd_d = []


class Op:
    __slots__ = ("idx", "eng", "meth", "args", "kw", "deps", "dma", "inc", "sem", "val", "incv")


class Sched:
    def __init__(self, nc, es):
        self.nc = nc
        self.es = es
        self.ops = []
        self.bufs = {}
        self.eng = {"pe": nc.tensor, "act": nc.scalar, "dve": nc.vector, "pool": nc.gpsimd, "sp": nc.sync}
        self.out_dmas = []
        self.last_op = {}
        self.fence_deps = []

    def fence(self):
        self.fence_deps = list(self.last_op.values())

    def b(self, *key):
        bb = self.bufs.get(key)
        if bb is None:
            bb = self.bufs[key] = Buf(key)
        return bb

    def op(self, eng, meth, *args, R=(), W=(), dma=None, **kw):
        o = Op()
        o.idx = len(self.ops)
        o.eng = eng
        o.meth = meth
        o.args = args
        o.kw = kw
        o.dma = dma
        o.inc = False
        o.sem = None
        o.val = 0
        o.incv = 16 if dma is not None else 1
        deps = set()
        for b in R:
            if b.lw is not None:
                deps.add(b.lw)
        for b in W:
            if b.lw is not None:
                deps.add(b.lw)
            deps.update(b.rd_c.values())
            deps.update(b.rd_d)
        if self.fence_deps:
            for b in list(R) + list(W):
                if b.name[0].startswith(("bfa_", "fa_", "a2_", "d_")):
                    deps.update(self.fence_deps)
                    break
        keep = []
        for d in deps:
            p = self.ops[d]
            if p.dma is not None or dma is not None or p.eng != eng:
                keep.append(d)
        o.deps = sorted(keep)
        for b in R:
            if dma is not None:
                b.rd_d.append(o.idx)
            else:
                b.rd_c[eng] = o.idx
        for b in W:
            b.lw = o.idx
            b.rd_c = {}
            b.rd_d = []
        self.ops.append(o)
        if dma is None and meth is not None:
            self.last_op[eng] = o.idx
        return o

    def pe(self, meth, *a, **k):
        return self.op("pe", meth, *a, **k)

    def act(self, meth, *a, **k):
        return self.op("act", meth, *a, **k)

    def dve(self, meth, *a, **k):
        return self.op("dve", meth, *a, **k)

    def pool(self, meth, *a, **k):
        return self.op("pool", meth, *a, **k)

    def dma(self, q, key, out, in_, R=(), W=(), is_out=False, **kw):
        o = self.op(q, "dma_start", out=out, in_=in_, R=R, W=W, dma=key, **kw)
        if is_out:
            self.out_dmas.append(o.idx)
        return o

    def finish(self, eng="sp"):
        o = self.op(eng, None, dma="__fin__")
        o.deps = sorted(set(o.deps) | set(p.idx for p in self.ops if p.dma is not None and p.meth is not None))

    def emit(self):
        nc = self.nc
        for o in self.ops:
            for d in o.deps:
                self.ops[d].inc = True
        sems = {}
        cnt = {}
        for o in self.ops:
            if not o.inc:
                continue
            k = o.eng if o.dma is None else ("d", o.dma)
            if k not in sems:
                sems[k] = self.es.enter_context(nc.semaphore("s%d" % len(sems)))
                cnt[k] = 0
            o.sem = sems[k]
            cnt[k] += o.incv
            o.val = cnt[k]
        waited = {e: {} for e in self.eng}
        for o in self.ops:
            e = self.eng[o.eng]
            wd = waited[o.eng]
            need = {}
            for d in o.deps:
                p = self.ops[d]
                key = id(p.sem)
                if wd.get(key, 0) < p.val:
                    if key not in need or need[key][1] < p.val:
                        need[key] = (p.sem, p.val)
            for key, (sem, val) in need.items():
                e.wait_ge(sem, val)
                wd[key] = val
            if o.meth is None:
                continue
            inst = getattr(e, o.meth)(*o.args, **o.kw)
            if o.inc:
                inst.then_inc(o.sem, o.incv)
        self.n_sems = len(sems)


def tile_w(W, G):
    K, N = W.shape
    KC = K // 128
    return np.ascontiguousarray(W.reshape(KC, 128, N // G, G).transpose(2, 1, 0, 3)).reshape(N // G, 128, KC * G)


def col_vec(v):
    return np.ascontiguousarray(v.reshape(-1, 128).T)


V_BG = 0
V_GQ = 64
V_GKV = 68
V_PS = 72
V_CW = 80
V_CB = 328
V_CG = 336
V_CBB = 344
V_L1G = 352
V_L1B = 368
V_L2G = 384
V_L2B = 400
V_FW = 416
V_FB = 680
NV = 768

DIL_PERM = np.array(list(range(0, 16)) + list(range(32, 48)) + list(range(16, 32)) + list(range(48, 128)))


def host_layout(inp, l):
    w_in = inp["w_in"][l]
    o = {}
    c0 = 0
    cq = w_in[:, 0:512]
    ckv = w_in[:, 512:1024]
    kpe = w_in[:, 1024:1088]
    up = w_in[:, 1088:2112]
    uc = w_in[:, 2112:4160]
    ud = w_in[:, 4160:13376]
    ug = w_in[:, 13376:21568]
    o["w_p1"] = tile_w(np.concatenate([cq, ckv, up], 1), 512)
    o["w_kpe"] = tile_w(kpe, 64)
    a, b = uc[:, :1024], uc[:, 1024:]
    pr = np.stack([a.reshape(D, 8, 128), b.reshape(D, 8, 128)], 2).reshape(D, 2048)
    o["w_p2"] = tile_w(pr, 512)
    ud6 = ud.reshape(D, 3, 3, 8, 128)
    qk = ud6[:, :, 0:2][..., DIL_PERM].reshape(D, 6144)
    o["w_dqk"] = tile_w(qk, 512)
    o["w_dv"] = tile_w(np.ascontiguousarray(ud6[:, :, 2]).reshape(D, 3072), 512)
    o["w_gate"] = tile_w(ug, 512)
    pj = np.concatenate([inp["mla_w_proj"][l], inp["pool_w_proj"][l], inp["conv_w_proj"][l], inp["dil_w_proj"][l]], 1)
    o["w_proj"] = tile_w(pj, 512)
    uq = inp["mla_w_uq"][l].reshape(512, 8, 192)
    o["w_uqn"] = tile_w(np.ascontiguousarray(uq[:, :, :128]).reshape(512, 1024), 512)
    o["w_uqr"] = tile_w(np.ascontiguousarray(uq[:, :, 128:]).reshape(512, 512), 512)
    ukv = inp["mla_w_ukv"][l].reshape(512, 8, 256)
    o["w_ukn"] = tile_w(np.ascontiguousarray(ukv[:, :, :128]).reshape(512, 1024), 512)
    o["w_ukv"] = tile_w(np.ascontiguousarray(ukv[:, :, 128:]).reshape(512, 1024), 512)
    pw = inp["pool_w"][l].reshape(4, 2, 128, 256)
    o["w_pool"] = np.ascontiguousarray(pw.transpose(2, 0, 1, 3)).reshape(128, 2048)
    o["w_mix"] = tile_w(inp["mix_w_out"][l], 512)
    wu = inp["ffn_w_up"][l]
    pr = np.stack([wu[:, :FFN].reshape(D, 44, 128), wu[:, FFN:].reshape(D, 44, 128)], 2).reshape(D, 2 * FFN)
    o["w_up"] = tile_w(pr, 512)
    o["w_dn"] = tile_w(inp["ffn_w_down"][l], 128)
    v = np.zeros((128, NV), np.float32)
    v[:, V_BG:V_BG + 64] = col_vec(inp["b_gate"][l].reshape(-1))
    v[:, V_GQ:V_GQ + 4] = col_vec(inp["mla_gq"][l])
    v[:, V_GKV:V_GKV + 4] = col_vec(inp["mla_gkv"][l])
    v[:, V_PS:V_PS + 8] = col_vec(inp["pool_scale"][l])
    v[:, V_CW:V_CW + 248] = col_vec(inp["conv_dw"][l].reshape(-1))
    v[:, V_CB:V_CB + 8] = col_vec(inp["conv_dw_b"][l])
    v[:, V_CG:V_CG + 8] = col_vec(inp["conv_ln_g"][l])
    v[:, V_CBB:V_CBB + 8] = col_vec(inp["conv_ln_b"][l])
    v[:, V_L1G:V_L1G + 16] = col_vec(inp["ln1_g"][l])
    v[:, V_L1B:V_L1B + 16] = col_vec(inp["ln1_b"][l])
    v[:, V_L2G:V_L2G + 16] = col_vec(inp["ln2_g"][l])
    v[:, V_L2B:V_L2B + 16] = col_vec(inp["ln2_b"][l])
    v[:, V_FW:V_FW + 264] = col_vec(inp["ffn_dw"][l].reshape(-1))
    v[:, V_FB:V_FB + 88] = col_vec(inp["ffn_dw_b"][l])
    o["vecs"] = v
    return o


W_SHAPES = {"w_p1": [4, 128, 8192], "w_kpe": [1, 128, 1024], "w_p2": [4, 128, 8192], "w_dqk": [12, 128, 8192],
            "w_dv": [6, 128, 8192], "w_gate": [16, 128, 8192], "w_proj": [16, 128, 4096], "w_uqn": [2, 128, 2048],
            "w_uqr": [1, 128, 2048], "w_ukn": [2, 128, 2048], "w_ukv": [2, 128, 2048], "w_pool": [128, 2048],
            "w_mix": [4, 128, 8192], "w_up": [22, 128, 8192], "w_dn": [16, 128, 5632], "vecs": [128, NV]}


def host_consts():
    c = np.zeros((128, 2048 + 512 + 8), np.float32)
    i = np.arange(128)
    for kb in range(4):
        for j in range(4):
            blk = np.zeros((128, 128), np.float32) if j < kb else (
                (i[None, :] >= i[:, None]).astype(np.float32) if j == kb else np.ones((128, 128), np.float32))
            c[:, kb * 512 + j * 128: kb * 512 + (j + 1) * 128] = blk
    m = np.concatenate([(i[None, :] <= i[:, None]), (i[None, :] >= i[:, None])], 1).astype(np.float32)
    c[:, 2048:2304] = m
    c[:, 2304:2560] = m
    f64 = THETA ** (-np.arange(32, dtype=np.float32) * 2.0 / 64)
    f32_ = THETA ** (-np.arange(16, dtype=np.float32) * 2.0 / 32)
    c[0:32, 2560] = f64
    c[32:64, 2560] = f64
    c[0:16, 2561] = f32_
    c[32:48, 2561] = f32_
    c[0:32, 2562] = -1
    c[32:64, 2562] = 1
    c[0:16, 2563] = -1
    c[32:48, 2563] = 1
    ic = np.zeros((4, S), np.float32)
    t = np.arange(S)
    for g, w in enumerate((2, 4, 8, 16)):
        ic[g] = 1.0 / np.minimum(t + 1, w)
    return c, ic


class Prog:
    def __init__(self, nc, es, n_layers, debug=()):
        self.nc = nc
        self.es = es
        self.S = Sched(nc, es)
        self.nl = n_layers
        self.debug = debug
        S_ = self.S
        sb = self.sb
        self.ps = [es.enter_context(nc.psum_tensor("ps%d" % i, [128, 512], F32)) for i in range(8)]
        self.ps_i = 0
        self.wslots = [sb("wslot%d" % i, [128, 8192], BF16) for i in range(2)]
        self.ws_i = 0
        self.BFA = sb("BFA", [128, 32768], BF16)
        self.FA = sb("FA", [128, 16384], F32)
        self.vecs = sb("vecs_sb", [128, NV], F32)
        self.cst_f = sb("cst_f", [128, 8], F32)
        self.cst_b = sb("cst_b", [128, 2560], BF16)
        self.ones_b = sb("ones_b", [128, 128], BF16)
        self.ones_f = sb("ones_f", [128, 128], F32)
        self.stg_f = [sb("stg_f%d" % i, [128, 1024], F32) for i in range(2)]
        self.stg_b = [sb("stg_b%d" % i, [128, 1024], BF16) for i in range(2)]
        self.tmp_f = [sb("tmp_f%d" % i, [128, 1024], F32) for i in range(4)]
        self.pt = [sb("pt%d" % i, [128, 512], BF16) for i in range(3)]
        self.halo = sb("halo", [128, 88, 2], F32)
        self.posi = sb("posi", [64, TB], I32)
        self.cnt = {}
        dr = self.dram
        self.xT_in = nc.dram_tensor("xT", [D, S], F32, kind="ExternalInput").ap()
        self.pos_in = nc.dram_tensor("pos", [1, S], I32, kind="ExternalInput").ap()
        self.cst_in = nc.dram_tensor("cst", [128, 2568], F32, kind="ExternalInput").ap()
        self.icnt_in = nc.dram_tensor("icnt", [4, S], F32, kind="ExternalInput").ap()
        self.w = {k: nc.dram_tensor(k, [n_layers] + shp, F32, kind="ExternalInput").ap() for k, shp in W_SHAPES.items()}
        self.out = nc.dram_tensor("yT", [D, S], F32, kind="ExternalOutput").ap()
        self.XF = dr("XF", [D, S], F32)
        self.XB = dr("XB", [D, S], BF16)
        self.X1F = dr("X1F", [D, S], F32)
        self.X1B = dr("X1B", [D, S], BF16)
        self.CQ = dr("CQ", [1024, S], F32)
        self.UP = dr("UP", [1024, S], F32)
        self.GL = dr("GL", [1024, S], F32)
        self.DQ = dr("DQ", [3, 1024, S], BF16)
        self.DK = dr("DK", [3, 1024, S], BF16)
        self.DV = dr("DV", [3, S, 1024], BF16)
        self.MQ = dr("MQ", [8, 192, S], BF16)
        self.MK = dr("MK", [8, 128, S], BF16)
        self.MKPE = dr("MKPE", [64, S], BF16)
        self.MV = dr("MV", [S, 1024], BF16)
        self.ACTS = dr("ACTS", [4, 1024, S], BF16)
        self.MG = dr("MG", [D, S], BF16)
        self.Z = dr("Z", [D, S], F32)
        self.HH = dr("HH", [FFN, S], BF16)
        self.ROPE = dr("ROPE", [4, 64, S], F32)
        self.q_i = 0

    def sb(self, name, shape, dt):
        return self.es.enter_context(self.nc.sbuf_tensor(name, shape, dt))

    def dram(self, name, shape, dt):
        kind = "ExternalOutput" if name in self.debug else "Internal"
        return self.nc.dram_tensor(name, shape, dt, kind=kind).ap()

    def next_ps(self, n):
        if self.ps_i + n > 8:
            self.ps_i = 0
        r = list(range(self.ps_i, self.ps_i + n))
        self.ps_i = (self.ps_i + n) % 8
        return r

    def acc_banks(self):
        i = self.rot("accb", 2)
        return (0, 1) if i == 0 else (2, 3)

    def s_bank(self):
        return 4 + self.rot("sbank", 4)

    def rot(self, name, n):
        i = self.cnt.get(name, 0)
        self.cnt[name] = (i + 1) % n
        return i

    def q(self):
        return "sp"

    def ld(self, key, out, in_, W, R=(), q=None):
        return self.S.dma(q or "sp", key, out, in_, R=list(R), W=list(W))

    def st(self, key, out, in_, R, W=(), q=None, is_out=False):
        return self.S.dma(q or "sp", key, out, in_, R=list(R), W=list(W), is_out=is_out)

    def linear_fm(self, w_dram, KC, G, groups, xT, xbufs, evac, mw=128, per_evac=1, nch=2, blocks=None):
        S_ = self.S
        tpg = G // mw
        loaded = {}
        groups = list(groups)
        blks = blocks if blocks is not None else [(None, xT, xbufs)]

        def load(i):
            wi = self.rot("ws", 2)
            S_.dma("pool", "wslot%d" % wi, self.wslots[wi][:, 0:KC * G], w_dram[groups[i]], W=[S_.b("wslot", wi)])
            loaded[i] = wi

        load(0)
        for i, g in enumerate(groups):
            if i + 1 < len(groups):
                load(i + 1)
            wi = loaded.pop(i)
            wv = self.wslots[wi][:, 0:KC * G].rearrange("p (k c) -> p k c", c=G)
            for mt0 in range(0, tpg, per_evac):
                for (tb, xv_, xb_) in blks:
                    pm, pp = [], []
                    for mt in range(mt0, mt0 + per_evac):
                        banks = self.next_ps(nch)
                        for kc in range(KC):
                            for c in range(nch):
                                S_.pe("matmul", self.ps[banks[c]][0:mw, :], lhsT=wv[:, kc, mt * mw:(mt + 1) * mw],
                                      rhs=xv_[:, kc, c * 512:(c + 1) * 512], start=(kc == 0), stop=(kc == KC - 1),
                                      R=[S_.b("wslot", wi)] + list(xb_), W=[S_.b("ps", banks[c])])
                        pm.append((g, mt))
                        pp.append(banks)
                    if blocks is None:
                        evac(pm, pp)
                    else:
                        evac(pm, pp, tb)

    def linear_tm(self, w_dram, KC, groups, xT, xbufs, evac, blocks=None):
        S_ = self.S
        G = 512
        blks = blocks if blocks is not None else [(None, xT, xbufs)]
        for g in groups:
            wi = self.rot("ws", 2)
            S_.dma("pool", "wslot%d" % wi, self.wslots[wi][:, 0:KC * G], w_dram[g], W=[S_.b("wslot", wi)])
            wv = self.wslots[wi][:, 0:KC * G].rearrange("p (k c) -> p k c", c=G)
            for (tb, xv_, xb_) in blks:
                for tt in range(8):
                    bk = self.next_ps(1)[0]
                    for kc in range(KC):
                        S_.pe("matmul", self.ps[bk][:, :], lhsT=xv_[:, kc, tt * 128:(tt + 1) * 128], rhs=wv[:, kc, :],
                              start=(kc == 0), stop=(kc == KC - 1),
                              R=[S_.b("wslot", wi)] + list(xb_), W=[S_.b("ps", bk)])
                    if blocks is None:
                        evac(g, tt, bk)
                    else:
                        evac(g, tt, bk, tb)

    def evac_copy(self, dst, dbuf, bk, rows=128, eng=None):
        S_ = self.S
        e = eng or ("act" if self.rot("ev", 2) == 0 else "dve")
        if e == "act":
            S_.act("copy", out=dst, in_=self.ps[bk][0:rows, :], R=[S_.b("ps", bk)], W=[dbuf])
        else:
            S_.dve("tensor_copy", out=dst, in_=self.ps[bk][0:rows, :], R=[S_.b("ps", bk)], W=[dbuf])

    def load_xT(self, src, t0, dep=(), slot=0):
        S_ = self.S
        v = self.BFA[:, slot * 16384:(slot + 1) * 16384].rearrange("p (k t) -> p k t", t=TB)
        for hlf in range(2):
            self.ld("bfa_x%d_%d" % (slot, hlf), v[:, hlf * 8:(hlf + 1) * 8, :],
                    src[hlf * 1024:(hlf + 1) * 1024, t0:t0 + TB].rearrange("(k p) t -> p k t", p=128),
                    R=list(dep), W=[S_.b("bfa_x", slot, hlf)])
        return v, [S_.b("bfa_x", slot, 0), S_.b("bfa_x", slot, 1)]

    def setup(self):
        S_ = self.S
        self.ld("cst_f", self.cst_f[:, :], self.cst_in[:, 2560:2568], W=[S_.b("cst_f")])
        S_.dma("pool", "cst_b", self.cst_b[:, :], self.cst_in[:, 0:2560], W=[S_.b("cst_b")])
        S_.dve("memset", self.ones_f[:, :], 1.0, W=[S_.b("ones_f")])
        S_.dve("memset", self.ones_b[:, :], 1.0, W=[S_.b("ones_b")])
        S_.dve("memset", self.halo[:, :, :], 0.0, W=[S_.b("halo", ct) for ct in range(88)])
        for tb in range(2):
            t0 = tb * TB
            pi = self.posi
            src = self.pos_in[0:1, t0:t0 + TB]
            bsrc = bass.AP(src.tensor, src.offset, [[0, 64], [1, TB]])
            self.ld("posi", pi[:, :], bsrc, W=[S_.b("posi")])
            pf = self.tmp_f[0][0:64, :]
            S_.dve("tensor_copy", out=pf, in_=pi[:, :], R=[S_.b("posi")], W=[S_.b("tmp_f", 0)])
            for which in range(2):
                ang = self.tmp_f[1][0:64, :]
                S_.dve("tensor_scalar", out=ang, in0=pf, scalar1=self.cst_f[0:64, which:which + 1], scalar2=None,
                       op0=ALU.mult, R=[S_.b("tmp_f", 0), S_.b("cst_f")], W=[S_.b("tmp_f", 1)])
                for cs in range(2):
                    red = self.tmp_f[2][0:64, :]
                    S_.dve("tensor_scalar", out=red, in0=ang, scalar1=1.0 / (2 * math.pi), scalar2=(0.25 if cs == 0 else 0.0),
                           op0=ALU.mult, op1=ALU.add, R=[S_.b("tmp_f", 1)], W=[S_.b("tmp_f", 2)])
                    S_.dve("tensor_copy", out=self.posi[:, :], in_=red, R=[S_.b("tmp_f", 2)], W=[S_.b("posi")])
                    tab = self.tmp_f[3][0:64, :]
                    S_.dve("tensor_copy", out=tab, in_=self.posi[:, :], R=[S_.b("posi")], W=[S_.b("tmp_f", 3)])
                    S_.dve("tensor_tensor", out=red, in0=red, in1=tab, op=ALU.subtract, R=[S_.b("tmp_f", 2), S_.b("tmp_f", 3)],
                           W=[S_.b("tmp_f", 2)])
                    S_.dve("scalar_tensor_tensor", out=red, in0=red, scalar=0.5, in1=red, op0=ALU.is_ge, op1=ALU.subtract,
                           R=[S_.b("tmp_f", 2)], W=[S_.b("tmp_f", 2)])
                    S_.act("activation", out=tab, in_=red, func=AF.Sin, scale=-2 * math.pi, R=[S_.b("tmp_f", 2)], W=[S_.b("tmp_f", 3)])
                    if cs == 1:
                        S_.dve("tensor_scalar", out=tab, in0=tab, scalar1=self.cst_f[0:64, 2 + which:3 + which], scalar2=None,
                               op0=ALU.mult, R=[S_.b("tmp_f", 3), S_.b("cst_f")], W=[S_.b("tmp_f", 3)])
                    self.st("tmp_f3", self.ROPE[which * 2 + cs, :, t0:t0 + TB], tab, R=[S_.b("tmp_f", 3)],
                            W=[S_.b("ROPE", tb)])
        for kc in range(16):
            i = self.rot("stg_f", 2)
            for hh in range(2):
                self.ld("stg_f%d" % i, self.stg_f[i][:, :], self.xT_in[kc * 128:(kc + 1) * 128, hh * TB:(hh + 1) * TB],
                        W=[S_.b("stg_f", i)], q=self.q())
                self.st("stg_f%d" % i, self.XF[kc * 128:(kc + 1) * 128, hh * TB:(hh + 1) * TB], self.stg_f[i][:, :],
                        R=[S_.b("stg_f", i)], W=[S_.b("XF", hh)], q=self.q())
                j = self.rot("stg_b", 2)
                S_.dve("tensor_copy", out=self.stg_b[j][:, :], in_=self.stg_f[i][:, :], R=[S_.b("stg_f", i)], W=[S_.b("stg_b", j)])
                self.st("stg_b%d" % j, self.XB[kc * 128:(kc + 1) * 128, hh * TB:(hh + 1) * TB], self.stg_b[j][:, :],
                        R=[S_.b("stg_b", j)], W=[S_.b("XB", hh)], q=self.q())
                i = self.rot("stg_f", 2)

    def load_vecs(self, l):
        S_ = self.S
        self.ld("vecs", self.vecs[:, :], self.w["vecs"][l], W=[S_.b("vecs")])

    def vcol(self, c, rows=128):
        return self.vecs[0:rows, c:c + 1]

    def load_rope(self, which, tb, dst_c, dst_s, bufs):
        S_ = self.S
        t0 = tb * TB
        self.ld("k_" + "_".join(map(str, bufs[0].name)), dst_c, self.ROPE[which * 2, :, t0:t0 + TB], W=[bufs[0]], R=[S_.b("ROPE", tb)])
        self.ld("k_" + "_".join(map(str, bufs[1].name)), dst_s, self.ROPE[which * 2 + 1, :, t0:t0 + TB], W=[bufs[1]], R=[S_.b("ROPE", tb)])

    def rope_evac(self, bk, c, rows, half, cosT, sinT, tbufs, dst, dbuf):
        S_ = self.S
        P = self.ps[bk]
        cs = slice(c * 512, (c + 1) * 512)
        t2 = self.tmp_f[2][0:rows, 0:512]
        t1 = self.tmp_f[3][0:rows, 0:512]
        R0 = [S_.b("ps", bk)] + list(tbufs)
        S_.dve("tensor_tensor", out=t2, in0=P[0:rows, :], in1=cosT[0:rows, cs], op=ALU.mult, R=R0, W=[S_.b("tmp_f", 2)])
        if rows == 48:
            S_.dve("memset", self.tmp_f[3][0:48, 0:512], 0.0, W=[S_.b("tmp_f", 3)])
        S_.dve("tensor_tensor", out=self.tmp_f[3][0:half, 0:512], in0=P[32:32 + half, :], in1=sinT[0:half, cs], op=ALU.mult,
               R=R0, W=[S_.b("tmp_f", 3)])
        S_.dve("tensor_tensor", out=self.tmp_f[3][32:32 + half, 0:512], in0=P[0:half, :], in1=sinT[32:32 + half, cs],
               op=ALU.mult, R=R0, W=[S_.b("tmp_f", 3)])
        S_.dve("tensor_tensor", out=dst, in0=t2, in1=t1, op=ALU.add, R=[S_.b("tmp_f", 2), S_.b("tmp_f", 3)], W=[dbuf])

    def stage_P(self, l):
        S_ = self.S
        S_.fence()
        blocks = []
        for tb in range(2):
            xv, xb = self.load_xT(self.XB, tb * TB, [S_.b("XB", tb)], slot=tb)
            blocks.append((tb, xv, xb))

        def tab(which, cs, tb):
            o = (which * 2 + cs) * 2048 + tb * 1024
            return self.FA[0:64, o:o + 1024]

        rbuf = {}
        for which in range(2):
            for tb in range(2):
                bufs = [S_.b("fa_r", which * 2, tb), S_.b("fa_r", which * 2 + 1, tb)]
                self.load_rope(which, tb, tab(which, 0, tb), tab(which, 1, tb), bufs)
                rbuf[(which, tb)] = bufs

        def ev1(pm, pp, tb):
            t0 = tb * TB
            (g, mt), banks = pm[0], pp[0]
            i = self.rot("stg_f", 2)
            for c in range(2):
                self.evac_copy(self.stg_f[i][:, c * 512:(c + 1) * 512], S_.b("stg_f", i), banks[c])
            dst = self.CQ if g < 2 else self.UP
            row = (g % 2) * 512 + mt * 128
            self.st("stg_f%d" % i, dst[row:row + 128, t0:t0 + TB], self.stg_f[i][:, :], R=[S_.b("stg_f", i)],
                    W=[S_.b("CQ" if g < 2 else "UP", tb)])

        self.linear_fm(self.w["w_p1"][l], 16, 512, range(4), None, None, ev1, blocks=blocks)

        def evk(pm, pp, tb):
            t0 = tb * TB
            banks = pp[0]
            i = self.rot("stg_b", 2)
            for c in range(2):
                self.rope_evac(banks[c], c, 64, 32, tab(0, 0, tb), tab(0, 1, tb), rbuf[(0, tb)],
                               self.stg_b[i][0:64, c * 512:(c + 1) * 512], S_.b("stg_b", i))
            self.st("stg_b%d" % i, self.MKPE[:, t0:t0 + TB], self.stg_b[i][0:64, :], R=[S_.b("stg_b", i)], W=[S_.b("MKPE", tb)])

        self.linear_fm(self.w["w_kpe"][l], 16, 64, range(1), None, None, evk, mw=64, blocks=blocks)

        def ev2(pm, pp, tb):
            t0 = tb * TB
            (g, mt), ba = pm[0], pp[0]
            bb = pp[1]
            m = g * 2 + mt // 2
            i = self.rot("stg_f", 2)
            for c in range(2):
                sg = self.tmp_f[0][:, c * 512:(c + 1) * 512]
                S_.act("activation", out=sg, in_=self.ps[bb[c]][:, :], func=AF.Sigmoid, R=[S_.b("ps", bb[c])], W=[S_.b("tmp_f", 0)])
                S_.dve("tensor_tensor", out=self.stg_f[i][:, c * 512:(c + 1) * 512], in0=self.ps[ba[c]][:, :], in1=sg, op=ALU.mult,
                       R=[S_.b("ps", ba[c]), S_.b("tmp_f", 0)], W=[S_.b("stg_f", i)])
            self.st("stg_f%d" % i, self.GL[m * 128:(m + 1) * 128, t0:t0 + TB], self.stg_f[i][:, :], R=[S_.b("stg_f", i)],
                    W=[S_.b("GL", tb)])

        self.linear_fm(self.w["w_p2"][l], 16, 512, range(4), None, None, ev2, per_evac=2, blocks=blocks)

        def ev3(pm, pp, tb):
            t0 = tb * TB
            (g, mt), banks = pm[0], pp[0]
            tix = g * 4 + mt
            grp, which, h = tix // 16, (tix // 8) % 2, tix % 8
            i = self.rot("stg_b", 2)
            sb_ = self.stg_b[i]
            for c in range(2):
                P = self.ps[banks[c]]
                cs = slice(c * 512, (c + 1) * 512)
                S_.dve("tensor_copy", out=sb_[64:128, cs], in_=P[64:128, :], R=[S_.b("ps", banks[c])], W=[S_.b("stg_b", i)])
                self.rope_evac(banks[c], c, 64, 32, tab(1, 0, tb), tab(1, 1, tb), rbuf[(1, tb)], sb_[0:64, cs], S_.b("stg_b", i))
            dst = (self.DQ if which == 0 else self.DK)[grp, h * 128:(h + 1) * 128, t0:t0 + TB]
            self.st("stg_b%d" % i, dst, sb_[:, :], R=[S_.b("stg_b", i)], W=[S_.b("DQK", tb)])

        self.linear_fm(self.w["w_dqk"][l], 16, 512, range(12), None, None, ev3, blocks=blocks)

        def ev4(g, tt, bk, tb):
            t0 = tb * TB
            grp, hf = g // 2, g % 2
            i = self.rot("pt", 3)
            self.evac_copy(self.pt[i][:, :], S_.b("pt", i), bk)
            self.st("pt%d" % i, self.DV[grp, t0 + tt * 128:t0 + (tt + 1) * 128, hf * 512:(hf + 1) * 512], self.pt[i][:, :],
                    R=[S_.b("pt", i)], W=[S_.b("DV", tb)])

        self.linear_tm(self.w["w_dv"][l], 16, range(6), None, None, ev4, blocks=blocks)

    def colsum(self, bk, srcs, sbufs):
        S_ = self.S
        n = len(srcs)
        for i, (s, b) in enumerate(zip(srcs, sbufs)):
            S_.pe("matmul", self.ps[bk][:, :], lhsT=self.ones_f[:, :], rhs=s, start=(i == 0), stop=(i == n - 1),
                  R=[S_.b("ones_f"), b], W=[S_.b("ps", bk)])

    def stage_A1(self, l, tb):
        S_ = self.S
        S_.fence()
        t0 = tb * TB
        cq = self.FA[:, 0:8192].rearrange("p (k t) -> p k t", t=TB)
        for hf in range(2):
            self.ld("fa_cq%d" % hf, cq[:, hf * 4:(hf + 1) * 4, :],
                    self.CQ[hf * 512:(hf + 1) * 512, t0:t0 + TB].rearrange("(k p) t -> p k t", p=128),
                    R=[S_.b("CQ", tb)], W=[S_.b("fa_cq", hf)], q=self.q())
        cosM, sinM = self.FA[0:64, 8192:9216], self.FA[0:64, 9216:10240]
        rb = [S_.b("fa_r", 0), S_.b("fa_r", 1)]
        self.load_rope(0, tb, cosM, sinM, rb)
        xn = self.BFA[:, 0:8192].rearrange("p (k t) -> p k t", t=TB)
        for hf in range(2):
            for c in range(2):
                bk = self.next_ps(1)[0]
                for k in range(4):
                    sq = self.tmp_f[k][:, 0:512]
                    S_.act("activation", out=sq, in_=cq[:, hf * 4 + k, c * 512:(c + 1) * 512], func=AF.Square,
                           R=[S_.b("fa_cq", hf)], W=[S_.b("tmp_f", k)])
                self.colsum(bk, [self.tmp_f[k][:, 0:512] for k in range(4)], [S_.b("tmp_f", k) for k in range(4)])
                rs = self.stg_f[0][:, c * 512:(c + 1) * 512]
                S_.dve("tensor_scalar", out=rs, in0=self.ps[bk][:, :], scalar1=1.0 / 512, scalar2=1e-6, op0=ALU.mult, op1=ALU.add,
                       R=[S_.b("ps", bk)], W=[S_.b("stg_f", 0)])
                S_.act("activation", out=rs, in_=rs, func=AF.Sqrt, R=[S_.b("stg_f", 0)], W=[S_.b("stg_f", 0)])
                S_.dve("reciprocal", out=rs, in_=rs, R=[S_.b("stg_f", 0)], W=[S_.b("stg_f", 0)])
                for k in range(4):
                    gc = self.vcol((V_GQ if hf == 0 else V_GKV) + k)
                    S_.dve("scalar_tensor_tensor", out=xn[:, hf * 4 + k, c * 512:(c + 1) * 512],
                           in0=cq[:, hf * 4 + k, c * 512:(c + 1) * 512], scalar=gc, in1=rs, op0=ALU.mult, op1=ALU.mult,
                           R=[S_.b("fa_cq", hf), S_.b("vecs"), S_.b("stg_f", 0)], W=[S_.b("bfa_xn", hf)])
        xq, xkv = xn[:, 0:4, :], xn[:, 4:8, :]

        def ev_plain(dst_of):
            def ev(pm, pp):
                (g, mt), banks = pm[0], pp[0]
                h = g * 4 + mt
                i = self.rot("stg_b", 2)
                for c in range(2):
                    self.evac_copy(self.stg_b[i][:, c * 512:(c + 1) * 512], S_.b("stg_b", i), banks[c])
                self.st("stg_b%d" % i, dst_of(h), self.stg_b[i][:, :], R=[S_.b("stg_b", i)], W=[S_.b("MQK", tb)], q=self.q())
            return ev

        self.linear_fm(self.w["w_uqn"][l], 4, 512, range(2), xq, [S_.b("bfa_xn", 0)],
                       ev_plain(lambda h: self.MQ[h, 0:128, t0:t0 + TB]))

        def ev_qr(pm, pp):
            (g, mt), banks = pm[0], pp[0]
            h = mt
            i = self.rot("stg_b", 2)
            for c in range(2):
                self.rope_evac(banks[c], c, 64, 32, cosM, sinM, rb, self.stg_b[i][0:64, c * 512:(c + 1) * 512], S_.b("stg_b", i))
            self.st("stg_b%d" % i, self.MQ[h, 128:192, t0:t0 + TB], self.stg_b[i][0:64, :], R=[S_.b("stg_b", i)],
                    W=[S_.b("MQK", tb)], q=self.q())

        self.linear_fm(self.w["w_uqr"][l], 4, 512, range(1), xq, [S_.b("bfa_xn", 0)], ev_qr, mw=64)
        self.linear_fm(self.w["w_ukn"][l], 4, 512, range(2), xkv, [S_.b("bfa_xn", 1)],
                       ev_plain(lambda h: self.MK[h, :, t0:t0 + TB]))

        def ev_v(g, tt, bk):
            i = self.rot("pt", 3)
            self.evac_copy(self.pt[i][:, :], S_.b("pt", i), bk)
            self.st("pt%d" % i, self.MV[t0 + tt * 128:t0 + (tt + 1) * 128, g * 512:(g + 1) * 512], self.pt[i][:, :],
                    R=[S_.b("pt", i)], W=[S_.b("MV", tb)], q=self.q())

        self.linear_tm(self.w["w_ukv"][l], 4, range(2), xkv, [S_.b("bfa_xn", 1)], ev_v)

    def stage_A2(self, l, tb):
        S_ = self.S
        S_.fence()
        t0 = tb * TB
        nk = t0 + TB
        nkb = nk // 128
        scale = 192 ** -0.5
        kpe = self.BFA[0:64, 0:2048]
        self.ld("bfa_kpe", kpe[:, 0:nk], self.MKPE[:, 0:nk], R=[S_.b("MKPE", 0), S_.b("MKPE", 1)], W=[S_.b("bfa_kpe")])
        masks = self.cst_b[:, 0:2048]
        for h in range(8):
            s = h % 2
            base = 2048 + s * 8192
            qn = self.BFA[:, base:base + 1024]
            qr = self.BFA[0:64, base + 1024:base + 2048]
            kn = self.BFA[:, base + 2048:base + 4096]
            vv = self.BFA[:, base + 4096:base + 6144].rearrange("p (k d) -> p k d", d=128)
            bq, bk_, bv = S_.b("a2_q", s), S_.b("a2_k", s), S_.b("a2_v", s)
            dep = [S_.b("MQK", 0), S_.b("MQK", 1)]
            self.ld("a2_qn%d" % s, qn, self.MQ[h, 0:128, t0:t0 + TB], R=dep, W=[bq])
            self.ld("a2_qr%d" % s, qr, self.MQ[h, 128:192, t0:t0 + TB], R=dep, W=[bq])
            self.ld("a2_k%d" % s, kn[:, 0:nk], self.MK[h, :, 0:nk], R=dep, W=[bk_])
            self.ld("a2_v%d" % s, vv[:, 0:nkb, :], self.MV[0:nk, h * 128:(h + 1) * 128].rearrange("(k p) d -> p k d", p=128),
                    R=[S_.b("MV", 0), S_.b("MV", 1)], W=[bv])
            for qc in range(2):
                qb0 = t0 // 128 + qc * 4
                nkeys = qb0 + 4
                ob, zb = self.acc_banks()
                for kb in range(nkeys):
                    sb_ = self.s_bank()
                    qs = slice(qc * 512, (qc + 1) * 512)
                    S_.pe("matmul", self.ps[sb_][:, :], lhsT=kn[:, kb * 128:(kb + 1) * 128], rhs=qn[:, qs], start=True, stop=False,
                          R=[bk_, bq], W=[S_.b("ps", sb_)])
                    S_.pe("matmul", self.ps[sb_][:, :], lhsT=kpe[:, kb * 128:(kb + 1) * 128], rhs=qr[:, qs], start=False, stop=True,
                          R=[S_.b("bfa_kpe"), bq], W=[S_.b("ps", sb_)])
                    pi = self.rot("pt", 3)
                    P = self.pt[pi]
                    S_.act("activation", out=P[:, :], in_=self.ps[sb_][:, :], func=AF.Exp, scale=scale,
                           R=[S_.b("ps", sb_)], W=[S_.b("pt", pi)])
                    if kb >= qb0:
                        i = kb - qb0
                        S_.dve("tensor_tensor", out=P[:, :], in0=P[:, :], in1=masks[:, i * 512:(i + 1) * 512], op=ALU.mult,
                               R=[S_.b("pt", pi), S_.b("cst_b")], W=[S_.b("pt", pi)])
                    S_.pe("matmul", self.ps[ob][:, :], lhsT=vv[:, kb, :], rhs=P[:, :], start=(kb == 0), stop=(kb == nkeys - 1),
                          R=[bv, S_.b("pt", pi)], W=[S_.b("ps", ob)])
                    S_.pe("matmul", self.ps[zb][:, :], lhsT=self.ones_b[:, :], rhs=P[:, :], start=(kb == 0), stop=(kb == nkeys - 1),
                          R=[S_.b("ones_b"), S_.b("pt", pi)], W=[S_.b("ps", zb)])
                rc = self.tmp_f[0][:, 0:512]
                S_.dve("reciprocal", out=rc, in_=self.ps[zb][:, :], R=[S_.b("ps", zb)], W=[S_.b("tmp_f", 0)])
                i = self.rot("stg_b", 2)
                S_.dve("tensor_tensor", out=self.stg_b[i][:, 0:512], in0=self.ps[ob][:, :], in1=rc, op=ALU.mult,
                       R=[S_.b("ps", ob), S_.b("tmp_f", 0)], W=[S_.b("stg_b", i)])
                self.st("stg_b%d" % i, self.ACTS[0, h * 128:(h + 1) * 128, t0 + qc * 512:t0 + (qc + 1) * 512], self.stg_b[i][:, 0:512],
                        R=[S_.b("stg_b", i)], W=[S_.b("ACTS", 0, tb)], q=self.q())

    def stage_B(self, l, tb):
        S_ = self.S
        S_.fence()
        t0 = tb * TB
        HL = 16
        up = self.FA[:, 0:8 * 1040].rearrange("p (k t) -> p k t", t=1040)
        ic = self.FA[:, 8320:8320 + 4096].rearrange("p (g t) -> p g t", t=TB)
        src = self.icnt_in[0:4, t0:t0 + TB]
        self.ld("fa_ic", ic, bass.AP(src.tensor, src.offset, [[0, 128], [S, 4], [1, TB]]), W=[S_.b("fa_ic")])
        pw = self.wslots[1][:, 0:2048].rearrange("p (g o) -> p g o", o=256)
        S_.dma("pool", "wslot1", self.wslots[1][:, 0:2048], self.w["w_pool"][l], W=[S_.b("wslot", 1)])
        dT = self.BFA[:, 0:8192].rearrange("p (k t) -> p k t", t=TB)
        if tb == 0:
            S_.dve("memset", up[:, :, 0:HL], 0.0, W=[S_.b("fa_up")])
            self.ld("fa_up", up[:, :, HL:HL + TB], self.UP[:, 0:TB].rearrange("(k p) t -> p k t", p=128),
                    R=[S_.b("UP", 0)], W=[S_.b("fa_up")])
        else:
            self.ld("fa_up", up[:, :, :], self.UP[:, t0 - HL:t0 + TB].rearrange("(k p) t -> p k t", p=128),
                    R=[S_.b("UP", 0), S_.b("UP", 1)], W=[S_.b("fa_up")])
        for ct in range(8):
            g = ct // 2
            cur = up[:, ct, :]
            cb = S_.b("fa_up")
            lo = 0
            step = 1
            n = 0
            while step < (2 << g):
                ti = n % 2
                nxt = self.tmp_f[ti][:, :]
                nv = self.FA[:, 12416 + ti * 1040:12416 + (ti + 1) * 1040]
                S_.dve("tensor_tensor", out=nv[:, step:1040], in0=cur[:, step:1040], in1=cur[:, 0:1040 - step], op=ALU.add,
                       R=[cb], W=[S_.b("fa_pp", ti)])
                cur = nv
                cb = S_.b("fa_pp", ti)
                step *= 2
                n += 1
            mean = self.tmp_f[2][:, :]
            S_.dve("tensor_tensor", out=mean, in0=cur[:, HL:HL + TB], in1=ic[:, g, :], op=ALU.mult, R=[cb, S_.b("fa_ic")],
                   W=[S_.b("tmp_f", 2)])
            S_.dve("tensor_tensor", out=dT[:, ct, :], in0=mean, in1=up[:, ct, HL:HL + TB], op=ALU.subtract,
                   R=[S_.b("tmp_f", 2), S_.b("fa_up")], W=[S_.b("bfa_d")])
        for ct in range(8):
            g, hf = ct // 2, ct % 2
            banks = self.next_ps(2)
            for kc in range(2):
                for c in range(2):
                    S_.pe("matmul", self.ps[banks[c]][:, :], lhsT=pw[:, g * 2 + kc, hf * 128:(hf + 1) * 128],
                          rhs=dT[:, g * 2 + kc, c * 512:(c + 1) * 512], start=(kc == 0), stop=(kc == 1),
                          R=[S_.b("wslot", 1), S_.b("bfa_d")], W=[S_.b("ps", banks[c])])
            i = self.rot("stg_b", 2)
            for c in range(2):
                S_.act("activation", out=self.stg_b[i][:, c * 512:(c + 1) * 512], in_=self.ps[banks[c]][:, :], func=AF.Copy,
                       scale=self.vcol(V_PS + ct), R=[S_.b("ps", banks[c]), S_.b("vecs")], W=[S_.b("stg_b", i)])
            self.st("stg_b%d" % i, self.ACTS[1, ct * 128:(ct + 1) * 128, t0:t0 + TB], self.stg_b[i][:, :], R=[S_.b("stg_b", i)],
                    W=[S_.b("ACTS", 1, tb)], q=self.q())

    def ln_stats(self, tiles, tbufs, nfeat, eps, mean, rstd, mbuf):
        S_ = self.S
        b1, b2 = self.next_ps(2)
        self.colsum(b1, tiles, tbufs)
        n = len(tiles)
        for i, (s, b) in enumerate(zip(tiles, tbufs)):
            k = self.rot("sq", 2)
            sq = self.tmp_f[k][:, 0:512]
            S_.act("activation", out=sq, in_=s, func=AF.Square, R=[b], W=[S_.b("tmp_f", k)])
            S_.pe("matmul", self.ps[b2][:, :], lhsT=self.ones_f[:, :], rhs=sq, start=(i == 0), stop=(i == n - 1),
                  R=[S_.b("ones_f"), S_.b("tmp_f", k)], W=[S_.b("ps", b2)])
        S_.dve("tensor_scalar", out=mean, in0=self.ps[b1][:, :], scalar1=1.0 / nfeat, scalar2=None, op0=ALU.mult,
               R=[S_.b("ps", b1)], W=[mbuf])
        m2 = self.tmp_f[2][:, 512:1024]
        S_.dve("tensor_tensor", out=m2, in0=mean, in1=mean, op=ALU.mult, R=[mbuf], W=[S_.b("tmp_f", 2)])
        S_.dve("scalar_tensor_tensor", out=rstd, in0=self.ps[b2][:, :], scalar=1.0 / nfeat, in1=m2, op0=ALU.mult, op1=ALU.subtract,
               R=[S_.b("ps", b2), S_.b("tmp_f", 2)], W=[mbuf])
        S_.dve("tensor_scalar", out=rstd, in0=rstd, scalar1=eps, scalar2=None, op0=ALU.add, R=[mbuf], W=[mbuf])
        S_.act("activation", out=rstd, in_=rstd, func=AF.Sqrt, R=[mbuf], W=[mbuf])
        S_.dve("reciprocal", out=rstd, in_=rstd, R=[mbuf], W=[mbuf])

    def stage_C(self, l, tb):
        S_ = self.S
        S_.fence()
        t0 = tb * TB
        HL = 30
        acc = self.FA[:, 0:8192].rearrange("p (k t) -> p k t", t=TB)
        for ct in range(8):
            i = ct % 2
            gl = self.FA[:, 8192 + i * 1054:8192 + (i + 1) * 1054]
            gb = S_.b("fa_gl", i)
            if tb == 0:
                S_.dve("memset", gl[:, 0:HL], 0.0, W=[gb])
                self.ld("fa_gl%d" % i, gl[:, HL:HL + TB], self.GL[ct * 128:(ct + 1) * 128, 0:TB], R=[S_.b("GL", 0)], W=[gb], q=self.q())
            else:
                self.ld("fa_gl%d" % i, gl[:, :], self.GL[ct * 128:(ct + 1) * 128, t0 - HL:t0 + TB], R=[S_.b("GL", 0), S_.b("GL", 1)],
                        W=[gb], q=self.q())
            a = acc[:, ct, :]
            ab = S_.b("fa_acc", ct)
            S_.act("activation", out=a, in_=gl[:, 30:30 + TB], func=AF.Identity, scale=self.vcol(V_CW + 30 * 8 + ct),
                   bias=self.vcol(V_CB + ct), R=[gb, S_.b("vecs")], W=[ab])
            for k in range(30):
                S_.dve("scalar_tensor_tensor", out=a, in0=gl[:, k:k + TB], scalar=self.vcol(V_CW + k * 8 + ct), in1=a,
                       op0=ALU.mult, op1=ALU.add, R=[gb, S_.b("vecs"), ab], W=[ab])
        for c in range(2):
            mean = self.FA[:, 10400:10912]
            rstd = self.FA[:, 10912:11424]
            mb = S_.b("fa_mr")
            self.ln_stats([acc[:, ct, c * 512:(c + 1) * 512] for ct in range(8)], [S_.b("fa_acc", ct) for ct in range(8)],
                          1024, 1e-5, mean, rstd, mb)
            for ct in range(8):
                t = self.tmp_f[3][:, 0:512]
                S_.dve("tensor_tensor", out=t, in0=acc[:, ct, c * 512:(c + 1) * 512], in1=mean, op=ALU.subtract,
                       R=[S_.b("fa_acc", ct), mb], W=[S_.b("tmp_f", 3)])
                S_.dve("tensor_tensor", out=t, in0=t, in1=rstd, op=ALU.mult, R=[S_.b("tmp_f", 3), mb], W=[S_.b("tmp_f", 3)])
                i = self.rot("pt", 3)
                S_.act("activation", out=self.pt[i][:, :], in_=t, func=AF.Silu, scale=self.vcol(V_CG + ct), bias=self.vcol(V_CBB + ct),
                       R=[S_.b("tmp_f", 3), S_.b("vecs")], W=[S_.b("pt", i)])
                self.st("pt%d" % i, self.ACTS[2, ct * 128:(ct + 1) * 128, t0 + c * 512:t0 + (c + 1) * 512], self.pt[i][:, :],
                        R=[S_.b("pt", i)], W=[S_.b("ACTS", 2, tb)], q=self.q())

    def stage_D(self, l, tb):
        S_ = self.S
        S_.fence()
        t0 = tb * TB
        scale = 128 ** -0.5
        dm = self.cst_b[:, 2048:2560]
        diag = self.cst_b[:, 2048 + 128:2048 + 256]
        for h in range(8):
            accU = self.FA[:, 0:1024]
            accZ = self.FA[:, 1024:2048]
            au, az = S_.b("fa_accU"), S_.b("fa_accZ")
            for gi, d in enumerate(DIL_D):
                s = (h * 3 + gi) % 2
                base = s * 8192
                L = S // d
                n = TB // d
                Q = self.BFA[:, base:base + 1024].rearrange("p (j r) -> p r j", r=d)
                K = self.BFA[:, base + 1024:base + 3072].rearrange("p (j r) -> p r j", r=d)
                bq, bk_ = S_.b("d_q", s), S_.b("d_k", s)
                dep = [S_.b("DQK", 0), S_.b("DQK", 1)]
                self.ld("d_q%d" % s, self.BFA[:, base:base + 1024], self.DQ[gi, h * 128:(h + 1) * 128, t0:t0 + TB], R=dep, W=[bq])
                self.ld("d_k%d" % s, self.BFA[:, base + 1024:base + 3072], self.DK[gi, h * 128:(h + 1) * 128, :], R=dep, W=[bk_])
                units = []
                if d == 16:
                    for r in range(16):
                        qa = 64 * tb
                        nkk = 64 * (tb + 1)
                        units.append((r, qa, 64, [(0, nkk, diag[0:nkk, qa:qa + 64])]))
                else:
                    for r in range(d):
                        for nb in range(n // 128):
                            blk = tb * (n // 128) + nb
                            kt = []
                            if blk > 0:
                                kt.append(((blk - 1) * 128, 128, dm[:, 0:128]))
                            kt.append((blk * 128, 128, dm[:, 128:256]))
                            units.append((r, blk * 128, 128, kt))
                bv = S_.b("d_v", s)
                vdep = [S_.b("DV", 0), S_.b("DV", 1)]
                vsrc = self.DV[gi, :, h * 128:(h + 1) * 128].rearrange("(jt p r) c -> p r jt c", p=128, r=d)
                if d == 16:
                    jt_lo, njt = 0, 1
                    nkk = 64 * (tb + 1)
                    vt = self.BFA[:, base + 3072:base + 3072 + 16 * 128].rearrange("p (r jt c) -> p r jt c", r=16, jt=1)
                    self.ld("d_v%d" % s, vt[0:nkk, :, 0, :], vsrc[0:nkk, :, 0, :], R=vdep, W=[bv])
                else:
                    blk0 = tb * (n // 128)
                    jt_lo = max(0, blk0 - 1)
                    njt = blk0 + n // 128 - jt_lo
                    vt = self.BFA[:, base + 3072:base + 3072 + d * njt * 128].rearrange("p (r jt c) -> p r jt c", r=d, jt=njt)
                    for r in range(d):
                        self.ld("d_v%d" % s, vt[:, r, :, :], vsrc[:, r, jt_lo:jt_lo + njt, :], R=vdep, W=[bv])
                vmap = {}
                for (r, qa, nq, kts) in units:
                    for (klo, nkk_, m) in kts:
                        vmap[(r, klo)] = (r, klo // 128 - jt_lo)
                for u0 in range(0, len(units), 2):
                    grp = units[u0:u0 + 2]
                    sbk = self.s_bank()
                    ub, zb = self.acc_banks()
                    pi = self.rot("pt", 3)
                    P = self.pt[pi]
                    col = 0
                    lay = []
                    for (r, qa, nq, kts) in grp:
                        ql = qa - tb * n
                        for (klo, nkk, m) in kts:
                            S_.pe("matmul", self.ps[sbk][0:nkk, col:col + nq], lhsT=K[:, r, klo:klo + nkk], rhs=Q[:, r, ql:ql + nq],
                                  start=True, stop=True, R=[bk_, bq], W=[S_.b("ps", sbk)])
                            lay.append((r, qa, nq, klo, nkk, m, col))
                            col += nq
                    rows = max(x[4] for x in lay)
                    S_.act("activation", out=P[0:rows, 0:col], in_=self.ps[sbk][0:rows, 0:col], func=AF.Exp, scale=scale,
                           R=[S_.b("ps", sbk)], W=[S_.b("pt", pi)])
                    for (r, qa, nq, klo, nkk, m, c0) in lay:
                        S_.dve("tensor_tensor", out=P[0:nkk, c0:c0 + nq], in0=P[0:nkk, c0:c0 + nq], in1=m, op=ALU.mult,
                               R=[S_.b("pt", pi), S_.b("cst_b")], W=[S_.b("pt", pi)])
                    oc = 0
                    outs = []
                    for ui, (r, qa, nq, kts) in enumerate(grp):
                        mine = [x for x in lay if x[0] == r and x[1] == qa]
                        for j, (r_, qa_, nq_, klo, nkk, m, c0) in enumerate(mine):
                            vr, vj = vmap[(r, klo)]
                            S_.pe("matmul", self.ps[ub][:, oc:oc + nq], lhsT=vt[0:nkk, vr, vj, :], rhs=P[0:nkk, c0:c0 + nq],
                                  start=(j == 0), stop=(j == len(mine) - 1), R=[bv, S_.b("pt", pi)], W=[S_.b("ps", ub)])
                        for j, (r_, qa_, nq_, klo, nkk, m, c0) in enumerate(mine):
                            S_.pe("matmul", self.ps[zb][:, oc:oc + nq], lhsT=self.ones_b[0:nkk, :], rhs=P[0:nkk, c0:c0 + nq],
                                  start=(j == 0), stop=(j == len(mine) - 1), R=[S_.b("ones_b"), S_.b("pt", pi)], W=[S_.b("ps", zb)])
                        outs.append((r, qa, nq, oc))
                        oc += nq
                    for (r, qa, nq, oc_) in outs:
                        ql = qa - tb * n
                        for (accT, ab, bank) in ((accU, au, ub), (accZ, az, zb)):
                            if d == 1:
                                dst = accT[:, ql:ql + nq]
                            else:
                                dst = accT.rearrange("p (j r) -> p r j", r=d)[:, r, ql:ql + nq]
                            src = self.ps[bank][:, oc_:oc_ + nq]
                            if gi == 0:
                                S_.act("copy", out=dst, in_=src, R=[S_.b("ps", bank)], W=[ab])
                            else:
                                S_.dve("tensor_tensor", out=dst, in0=src, in1=dst, op=ALU.add, R=[S_.b("ps", bank), ab], W=[ab])
            rc = self.tmp_f[0][:, :]
            S_.dve("reciprocal", out=rc, in_=accZ, R=[az], W=[S_.b("tmp_f", 0)])
            i = self.rot("stg_b", 2)
            S_.dve("tensor_tensor", out=self.stg_b[i][:, :], in0=accU, in1=rc, op=ALU.mult, R=[au, S_.b("tmp_f", 0)], W=[S_.b("stg_b", i)])
            self.st("stg_b%d" % i, self.ACTS[3, h * 128:(h + 1) * 128, t0:t0 + TB], self.stg_b[i][:, :], R=[S_.b("stg_b", i)],
                    W=[S_.b("ACTS", 3, tb)], q=self.q())

    def stage_M(self, l, tb):
        S_ = self.S
        S_.fence()
        t0 = tb * TB
        xv, xb = self.load_xT(self.XB, t0, [S_.b("XB", tb)])
        mg = self.FA[:, :].rearrange("p (k t) -> p k t", t=TB)
        for b in range(4):
            s = 0
            av = self.BFA[:, 16384:24576].rearrange("p (k t) -> p k t", t=TB)
            ab = S_.b("bfa_act", s)
            self.ld("bfa_act%d" % s, av, self.ACTS[b, :, t0:t0 + TB].rearrange("(k p) t -> p k t", p=128),
                    R=[S_.b("ACTS", b, tb)], W=[ab], q=self.q())
            for mgp in range(4):
                gate_ps = {}

                def evg(pm, pp):
                    for (g, mt), banks in zip(pm, pp):
                        m = mgp * 4 + mt
                        for c in range(2):
                            k = self.rot("sg", 2)
                            sg = self.stg_f[k][:, c * 512:(c + 1) * 512]
                            S_.act("activation", out=sg, in_=self.ps[banks[c]][:, :], func=AF.Sigmoid, bias=self.vcol(V_BG + b * 16 + m),
                                   R=[S_.b("ps", banks[c]), S_.b("vecs")], W=[S_.b("stg_f", k)])
                            gate_ps[(mt, c)] = (sg, S_.b("stg_f", k))
                        self._proj_tile(l, b, m, mt, av, ab, gate_ps, mg)

                self._proj_w = None
                self.linear_fm(self.w["w_gate"][l], 16, 512, [b * 4 + mgp], xv, xb, evg)
        for m in range(16):
            i = self.rot("stg_b", 2)
            S_.dve("tensor_copy", out=self.stg_b[i][:, :], in_=mg[:, m, :], R=[S_.b("fa_mg", m)], W=[S_.b("stg_b", i)])
            self.st("stg_b%d" % i, self.MG[m * 128:(m + 1) * 128, t0:t0 + TB], self.stg_b[i][:, :], R=[S_.b("stg_b", i)],
                    W=[S_.b("MG", tb)], q=self.q())

    def _proj_tile(self, l, b, m, mt, av, ab, gate_ps, mg):
        S_ = self.S
        if mt == 0:
            wi = self.rot("ws", 2)
            S_.dma("pool", "wslot%d" % wi, self.wslots[wi][:, 0:4096], self.w["w_proj"][l][b * 4 + m // 4], W=[S_.b("wslot", wi)])
            self._proj_w = wi
        wi = self._proj_w
        wv = self.wslots[wi][:, 0:4096].rearrange("p (k c) -> p k c", c=512)
        banks = self.next_ps(2)
        for kc in range(8):
            for c in range(2):
                S_.pe("matmul", self.ps[banks[c]][:, :], lhsT=wv[:, kc, mt * 128:(mt + 1) * 128], rhs=av[:, kc, c * 512:(c + 1) * 512],
                      start=(kc == 0), stop=(kc == 7), R=[S_.b("wslot", wi), ab], W=[S_.b("ps", banks[c])])
        for c in range(2):
            sg, sgb = gate_ps[(mt, c)]
            dst = mg[:, m, c * 512:(c + 1) * 512]
            if b == 0:
                S_.dve("tensor_tensor", out=dst, in0=self.ps[banks[c]][:, :], in1=sg, op=ALU.mult, R=[S_.b("ps", banks[c]), sgb],
                       W=[S_.b("fa_mg", m)])
            else:
                t = self.tmp_f[c][:, 0:512]
                S_.dve("tensor_tensor", out=t, in0=self.ps[banks[c]][:, :], in1=sg, op=ALU.mult, R=[S_.b("ps", banks[c]), sgb],
                       W=[S_.b("tmp_f", c)])
                S_.pool("tensor_tensor", out=dst, in0=dst, in1=t, op=ALU.add, R=[S_.b("tmp_f", c), S_.b("fa_mg", m)], W=[S_.b("fa_mg", m)])

    def stage_resid(self, w_dram, KC, G, ngroups, src_act, act_dep, res, res_dep, tb, nch, c0=0):
        S_ = self.S
        S_.fence()
        t0 = tb * TB + c0
        W_ = nch * 512
        av = self.BFA[:, 0:KC * W_].rearrange("p (k t) -> p k t", t=W_)
        kh = KC // 2
        for hf in range(2):
            ks = slice(hf * kh, (hf + 1) * kh) if hf == 0 else slice(kh, KC)
            k0, k1 = (0, kh) if hf == 0 else (kh, KC)
            self.ld("bfa_rx%d" % hf, av[:, k0:k1, :], src_act[k0 * 128:k1 * 128, t0:t0 + W_].rearrange("(k p) t -> p k t", p=128),
                    R=act_dep, W=[S_.b("bfa_rx", hf)])
        xb = [S_.b("bfa_rx", 0), S_.b("bfa_rx", 1)]

        def ev(pm, pp):
            (g, mt), banks = pm[0], pp[0]
            m = g * (G // 128) + mt
            i = self.rot("stg_f", 2)
            k = self.rot("rs", 2)
            rs = self.tmp_f[k][:, 0:W_]
            self.ld("tmp_f%d" % k, rs, res[m * 128:(m + 1) * 128, t0:t0 + W_], R=res_dep, W=[S_.b("tmp_f", k)], q=self.q())
            for c in range(nch):
                S_.dve("scalar_tensor_tensor", out=self.stg_f[i][:, c * 512:(c + 1) * 512], in0=rs[:, c * 512:(c + 1) * 512], scalar=ALPHA,
                       in1=self.ps[banks[c]][:, :], op0=ALU.mult, op1=ALU.add, R=[S_.b("tmp_f", k), S_.b("ps", banks[c])],
                       W=[S_.b("stg_f", i)])
            self.st("stg_f%d" % i, self.Z[m * 128:(m + 1) * 128, t0:t0 + W_], self.stg_f[i][:, 0:W_], R=[S_.b("stg_f", i)],
                    W=[S_.b("Z", tb)], q=self.q())

        self.linear_fm(w_dram, KC, G, range(ngroups), av, xb, ev, nch=nch)

    def stage_N(self, l, tb, gcol, bcol, dstF, dstB, fkey, final=False):
        S_ = self.S
        S_.fence()
        for c in range(2):
            t0 = tb * TB + c * 512
            z = self.FA[:, 0:8192].rearrange("p (k t) -> p k t", t=512)
            for hf in range(2):
                self.ld("fa_z%d" % hf, z[:, hf * 8:(hf + 1) * 8, :], self.Z[hf * 1024:(hf + 1) * 1024, t0:t0 + 512].rearrange("(k p) t -> p k t", p=128),
                        R=[S_.b("Z", tb)], W=[S_.b("fa_z", hf)], q=self.q())
            mean = self.FA[:, 8192:8704]
            rstd = self.FA[:, 8704:9216]
            mb = S_.b("fa_mr")
            self.ln_stats([z[:, k, :] for k in range(16)], [S_.b("fa_z", k // 8) for k in range(16)], 2048, 1e-5, mean, rstd, mb)
            for k in range(16):
                t = self.tmp_f[3][:, 0:512]
                S_.dve("tensor_tensor", out=t, in0=z[:, k, :], in1=mean, op=ALU.subtract, R=[S_.b("fa_z", k // 8), mb], W=[S_.b("tmp_f", 3)])
                S_.dve("tensor_tensor", out=t, in0=t, in1=rstd, op=ALU.mult, R=[S_.b("tmp_f", 3), mb], W=[S_.b("tmp_f", 3)])
                i = self.rot("stg_f", 2)
                o = self.stg_f[i][:, 0:512]
                S_.act("activation", out=o, in_=t, func=AF.Identity, scale=self.vcol(gcol + k), bias=self.vcol(bcol + k),
                       R=[S_.b("tmp_f", 3), S_.b("vecs")], W=[S_.b("stg_f", i)])
                if final:
                    self.st("stg_f%d" % i, self.out[k * 128:(k + 1) * 128, t0:t0 + 512], o, R=[S_.b("stg_f", i)], q=self.q(), is_out=True)
                    continue
                self.st("stg_f%d" % i, dstF[k * 128:(k + 1) * 128, t0:t0 + 512], o, R=[S_.b("stg_f", i)], W=[S_.b(fkey + "F", tb)], q=self.q())
                j = self.rot("pt", 3)
                S_.pool("tensor_copy", out=self.pt[j][:, :], in_=o, R=[S_.b("stg_f", i)], W=[S_.b("pt", j)])
                self.st("pt%d" % j, dstB[k * 128:(k + 1) * 128, t0:t0 + 512], self.pt[j][:, :], R=[S_.b("pt", j)], W=[S_.b(fkey + "B", tb)], q=self.q())

    def stage_F1(self, l):
        S_ = self.S
        S_.fence()
        blocks = []
        for tb_ in range(2):
            xv, xb = self.load_xT(self.X1B, tb_ * TB, [S_.b("X1B", tb_)], slot=tb_)
            blocks.append((tb_, xv, xb))

        def ev(pm, pp, tb):
            t0 = tb * TB
            g, mt0 = pm[0]
            m = g * 2 + mt0 // 2
            res = []
            for j, ((g_, mt), banks) in enumerate(zip(pm, pp)):
                ct = m + 44 * j
                k = self.rot("hb", 4)
                hb = self.FA[:, k * 1026:(k + 1) * 1026]
                hbb = S_.b("fa_hb", k)
                if tb == 0:
                    S_.dve("memset", hb[:, 0:2], 0.0, W=[hbb])
                else:
                    S_.dve("tensor_copy", out=hb[:, 0:2], in_=self.halo[:, ct, :], R=[S_.b("halo", ct)], W=[hbb])
                for c in range(2):
                    self.evac_copy(hb[:, 2 + c * 512:2 + (c + 1) * 512], hbb, banks[c], eng="act")
                S_.dve("tensor_copy", out=self.halo[:, ct, :], in_=hb[:, 1024:1026], R=[hbb], W=[S_.b("halo", ct)])
                a = self.FA[:, 4200 + k * 1024:4200 + (k + 1) * 1024]
                ab = S_.b("fa_cv", k)
                S_.dve("tensor_scalar", out=a, in0=hb[:, 2:1026], scalar1=self.vcol(V_FW + 2 * 88 + ct), scalar2=self.vcol(V_FB + ct),
                       op0=ALU.mult, op1=ALU.add, R=[hbb, S_.b("vecs")], W=[ab])
                S_.dve("scalar_tensor_tensor", out=a, in0=hb[:, 1:1025], scalar=self.vcol(V_FW + 88 + ct), in1=a, op0=ALU.mult, op1=ALU.add,
                       R=[hbb, S_.b("vecs"), ab], W=[ab])
                S_.dve("scalar_tensor_tensor", out=a, in0=hb[:, 0:1024], scalar=self.vcol(V_FW + ct), in1=a, op0=ALU.mult, op1=ALU.add,
                       R=[hbb, S_.b("vecs"), ab], W=[ab])
                res.append((a, ab))
            (va, vb), (ga, gb) = res
            S_.act("activation", out=ga, in_=ga, func=AF.Silu, R=[gb], W=[gb])
            i = self.rot("stg_b", 2)
            S_.dve("tensor_tensor", out=self.stg_b[i][:, :], in0=va, in1=ga, op=ALU.mult, R=[vb, gb], W=[S_.b("stg_b", i)])
            self.st("stg_b%d" % i, self.HH[m * 128:(m + 1) * 128, t0:t0 + TB], self.stg_b[i][:, :], R=[S_.b("stg_b", i)],
                    W=[S_.b("HH", tb)])

        self.linear_fm(self.w["w_up"][l], 16, 512, range(22), None, None, ev, per_evac=2, blocks=blocks)

    def layer(self, l, last):
        S_ = self.S
        S_.fence()
        self.load_vecs(l)
        self.stage_P(l)
        for tb in range(2):
            self.stage_A1(l, tb)
            self.stage_A2(l, tb)
            self.stage_B(l, tb)
            self.stage_C(l, tb)
            self.stage_D(l, tb)
            self.stage_M(l, tb)
            self.stage_resid(self.w["w_mix"][l], 16, 512, 4, self.MG, [S_.b("MG", tb)], self.XF, [S_.b("XF", tb)], tb, 2)
            self.stage_N(l, tb, V_L1G, V_L1B, self.X1F, self.X1B, "X1")
        self.stage_F1(l)
        for tb in range(2):
            for c in range(2):
                self.stage_resid(self.w["w_dn"][l], 44, 128, 16, self.HH, [S_.b("HH", tb)], self.X1F, [S_.b("X1F", tb)], tb, 1, c0=c * 512)
            self.stage_N(l, tb, V_L2G, V_L2B, self.XF, self.XB, "X", final=last)


def build(n_layers, debug=(), stages=None):
    nc = bass.Bass("TRN2", target_bir_lowering=False)
    es = ExitStack()
    P = Prog(nc, es, n_layers, debug)
    P.setup()
    if stages is not None:
        P.load_vecs(0)
        for st in stages:
            if st == "none":
                continue
            name, tb = st[:-1], int(st[-1])
            if name in ("P", "F1"):
                getattr(P, "stage_" + name)(0)
            else:
                getattr(P, "stage_" + name)(0, tb)
    for l in range(n_layers if stages is None else 0):
        P.layer(l, last=(l == n_layers - 1))
    P.S.finish()
    P.S.emit()
    es.close()
    return nc, P


def make_inputs(inp, b, n_layers):
    cst, icnt = host_consts()
    m = {"xT": np.ascontiguousarray(inp["x"][b].T), "pos": np.ascontiguousarray(inp["positions"][b][None, :]).astype(np.int32),
         "cst": cst, "icnt": icnt}
    return m


def kernel(**inputs):
    inp = {k: np.asarray(v) for k, v in inputs.items()}
    nl = DEPTH
    nc, P = build(nl)
    per_layer = [host_layout(inp, l) for l in range(nl)]
    wts = {k: np.stack([per_layer[l][k] for l in range(nl)], 0) for k in W_SHAPES}
    in_maps = []
    for b in range(4):
        m = make_inputs(inp, b, nl)
        m.update(wts)
        in_maps.append(m)
    res = run_bass_kernel_spmd(nc, in_maps, core_ids=list(range(4)))
    out = np.stack([np.ascontiguousarray(res.results[b]["yT"].T) for b in range(4)], 0)
    return out.astype(np.float32)
```

```python
import math
import numpy as np
from contextlib import ExitStack
import concourse.bass as bass
import concourse.mybir as mybir
from concourse.bass_utils import run_bass_kernel_spmd

F32 = mybir.dt.float32
BF16 = mybir.dt.bfloat16
I32 = mybir.dt.int32
AF = mybir.ActivationFunctionType
ALU = mybir.AluOpType

D = 2048
S = 2048
TB = 1024
DEPTH = 4
FFN = 5632
ALPHA = (2 * DEPTH) ** 0.25
THETA = 500000.0
DIL_D = (1, 4, 16)
PSKIP = set()


class Buf:
    __slots__ = ("name", "lw", "rd_c", "rd_d")

    def __init__(self, name):
        self.name = name
        self.lw = None
        self.rd_c = {}
        self.rd_d = []


class Op:
    __slots__ = ("idx", "eng", "meth", "args", "kw", "deps", "dma", "inc", "sem", "val", "incv")


class Sched:
    def __init__(self, nc, es):
        self.nc = nc
        self.es = es
        self.ops = []
        self.bufs = {}
        self.eng = {"pe": nc.tensor, "act": nc.scalar, "dve": nc.vector, "pool": nc.gpsimd, "sp": nc.sync}
        self.out_dmas = []
        self.last_op = {}
        self.fence_deps = []

    def fence(self):
        self.fence_deps = list(self.last_op.values())

    def b(self, *key):
        bb = self.bufs.get(key)
        if bb is None:
            bb = self.bufs[key] = Buf(key)
        return bb

    def op(self, eng, meth, *args, R=(), W=(), dma=None, **kw):
        o = Op()
        o.idx = len(self.ops)
        o.eng = eng
        o.meth = meth
        o.args = args
        o.kw = kw
        o.dma = dma
        o.inc = False
        o.sem = None
        o.val = 0
        o.incv = 16 if dma is not None else 1
        deps = set()
        for b in R:
            if b.lw is not None:
                deps.add(b.lw)
        for b in W:
            if b.lw is not None:
                deps.add(b.lw)
            deps.update(b.rd_c.values())
            deps.update(b.rd_d)
        if self.fence_deps:
            for b in list(R) + list(W):
                if b.name[0].startswith(("bfa_", "fa_", "a2_", "d_")):
                    deps.update(self.fence_deps)
                    break
        keep = []
        for d in deps:
            p = self.ops[d]
            if p.dma is not None or dma is not None or p.eng != eng:
                keep.append(d)
        o.deps = sorted(keep)
        for b in R:
            if dma is not None:
                b.rd_d.append(o.idx)
            else:
                b.rd_c[eng] = o.idx
        for b in W:
            b.lw = o.idx
            b.rd_c = {}
            b.rd_d = []
        self.ops.append(o)
        if dma is None and meth is not None:
            self.last_op[eng] = o.idx
        return o

    def pe(self, meth, *a, **k):
        return self.op("pe", meth, *a, **k)

    def act(self, meth, *a, **k):
        return self.op("act", meth, *a, **k)

    def dve(self, meth, *a, **k):
        return self.op("dve", meth, *a, **k)

    def pool(self, meth, *a, **k):
        return self.op("pool", meth, *a, **k)

    def dma(self, q, key, out, in_, R=(), W=(), is_out=False, **kw):
        o = self.op(q, "dma_start", out=out, in_=in_, R=R, W=W, dma=key, **kw)
        if is_out:
            self.out_dmas.append(o.idx)
        return o

    def finish(self, eng="sp"):
        o = self.op(eng, None, dma="__fin__")
        o.deps = sorted(set(o.deps) | set(p.idx for p in self.ops if p.dma is not None and p.meth is not None))

    def emit(self):
        nc = self.nc
        for o in self.ops:
            for d in o.deps:
                self.ops[d].inc = True
        sems = {}
        cnt = {}
        for o in self.ops:
            if not o.inc:
                continue
            k = o.eng if o.dma is None else ("d", o.dma)
            if k not in sems:
                sems[k] = self.es.enter_context(nc.semaphore("s%d" % len(sems)))
                cnt[k] = 0
            o.sem = sems[k]
            cnt[k] += o.incv
            o.val = cnt[k]
        waited = {e: {} for e in self.eng}
        for o in self.ops:
            e = self.eng[o.eng]
            wd = waited[o.eng]
            need = {}
            for d in o.deps:
                p = self.ops[d]
                key = id(p.sem)
                if wd.get(key, 0) < p.val:
                    if key not in need or need[key][1] < p.val:
                        need[key] = (p.sem, p.val)
            for key, (sem, val) in need.items():
                e.wait_ge(sem, val)
                wd[key] = val
            if o.meth is None:
                continue
            inst = getattr(e, o.meth)(*o.args, **o.kw)
            if o.inc:
                inst.then_inc(o.sem, o.incv)
        self.n_sems = len(sems)


def tile_w(W, G):
    K, N = W.shape
    KC = K // 128
    return np.ascontiguousarray(W.reshape(KC, 128, N // G, G).transpose(2, 1, 0, 3)).reshape(N // G, 128, KC * G)


def col_vec(v):
    return np.ascontiguousarray(v.reshape(-1, 128).T)


V_BG = 0
V_GQ = 64
V_GKV = 68
V_PS = 72
V_CW = 80
V_CB = 328
V_CG = 336
V_CBB = 344
V_L1G = 352
V_L1B = 368
V_L2G = 384
V_L2B = 400
V_FW = 416
V_FB = 680
NV = 768

DIL_PERM = np.array(list(range(0, 16)) + list(range(32, 48)) + list(range(16, 32)) + list(range(48, 128)))


def host_layout(inp, l):
    w_in = inp["w_in"][l]
    o = {}
    c0 = 0
    cq = w_in[:, 0:512]
    ckv = w_in[:, 512:1024]
    kpe = w_in[:, 1024:1088]
    up = w_in[:, 1088:2112]
    uc = w_in[:, 2112:4160]
    ud = w_in[:, 4160:13376]
    ug = w_in[:, 13376:21568]
    o["w_p1"] = tile_w(np.concatenate([cq, ckv, up], 1), 512)
    o["w_kpe"] = tile_w(kpe, 64)
    a, b = uc[:, :1024], uc[:, 1024:]
    pr = np.stack([a.reshape(D, 8, 128), b.reshape(D, 8, 128)], 2).reshape(D, 2048)
    o["w_p2"] = tile_w(pr, 512)
    ud6 = ud.reshape(D, 3, 3, 8, 128)
    qk = ud6[:, :, 0:2][..., DIL_PERM].reshape(D, 6144)
    o["w_dqk"] = tile_w(qk, 512)
    o["w_dv"] = tile_w(np.ascontiguousarray(ud6[:, :, 2]).reshape(D, 3072), 512)
    o["w_gate"] = tile_w(ug, 512)
    pj = np.concatenate([inp["mla_w_proj"][l], inp["pool_w_proj"][l], inp["conv_w_proj"][l], inp["dil_w_proj"][l]], 1)
    o["w_proj"] = tile_w(pj, 512)
    uq = inp["mla_w_uq"][l].reshape(512, 8, 192)
    o["w_uqn"] = tile_w(np.ascontiguousarray(uq[:, :, :128]).reshape(512, 1024), 512)
    o["w_uqr"] = tile_w(np.ascontiguousarray(uq[:, :, 128:]).reshape(512, 512), 512)
    ukv = inp["mla_w_ukv"][l].reshape(512, 8, 256)
    o["w_ukn"] = tile_w(np.ascontiguousarray(ukv[:, :, :128]).reshape(512, 1024), 512)
    o["w_ukv"] = tile_w(np.ascontiguousarray(ukv[:, :, 128:]).reshape(512, 1024), 512)
    pw = inp["pool_w"][l].reshape(4, 2, 128, 256)
    o["w_pool"] = np.ascontiguousarray(pw.transpose(2, 0, 1, 3)).reshape(128, 2048)
    o["w_mix"] = tile_w(inp["mix_w_out"][l], 512)
    wu = inp["ffn_w_up"][l]
    pr = np.stack([wu[:, :FFN].reshape(D, 44, 128), wu[:, FFN:].reshape(D, 44, 128)], 2).reshape(D, 2 * FFN)
    o["w_up"] = tile_w(pr, 512)
    o["w_dn"] = tile_w(inp["ffn_w_down"][l], 128)
    v = np.zeros((128, NV), np.float32)
    v[:, V_BG:V_BG + 64] = col_vec(inp["b_gate"][l].reshape(-1))
    v[:, V_GQ:V_GQ + 4] = col_vec(inp["mla_gq"][l])
    v[:, V_GKV:V_GKV + 4] = col_vec(inp["mla_gkv"][l])
    v[:, V_PS:V_PS + 8] = col_vec(inp["pool_scale"][l])
    v[:, V_CW:V_CW + 248] = col_vec(inp["conv_dw"][l].reshape(-1))
    v[:, V_CB:V_CB + 8] = col_vec(inp["conv_dw_b"][l])
    v[:, V_CG:V_CG + 8] = col_vec(inp["conv_ln_g"][l])
    v[:, V_CBB:V_CBB + 8] = col_vec(inp["conv_ln_b"][l])
    v[:, V_L1G:V_L1G + 16] = col_vec(inp["ln1_g"][l])
    v[:, V_L1B:V_L1B + 16] = col_vec(inp["ln1_b"][l])
    v[:, V_L2G:V_L2G + 16] = col_vec(inp["ln2_g"][l])
    v[:, V_L2B:V_L2B + 16] = col_vec(inp["ln2_b"][l])
    v[:, V_FW:V_FW + 264] = col_vec(inp["ffn_dw"][l].reshape(-1))
    v[:, V_FB:V_FB + 88] = col_vec(inp["ffn_dw_b"][l])
    o["vecs"] = v
    return o


W_SHAPES = {"w_p1": [4, 128, 8192], "w_kpe": [1, 128, 1024], "w_p2": [4, 128, 8192], "w_dqk": [12, 128, 8192],
            "w_dv": [6, 128, 8192], "w_gate": [16, 128, 8192], "w_proj": [16, 128, 4096], "w_uqn": [2, 128, 2048],
            "w_uqr": [1, 128, 2048], "w_ukn": [2, 128, 2048], "w_ukv": [2, 128, 2048], "w_pool": [128, 2048],
            "w_mix": [4, 128, 8192], "w_up": [22, 128, 8192], "w_dn": [16, 128, 5632], "vecs": [128, NV]}


def host_consts():
    c = np.zeros((128, 2048 + 512 + 8), np.float32)
    i = np.arange(128)
    for kb in range(4):
        for j in range(4):
            blk = np.zeros((128, 128), np.float32) if j < kb else (
                (i[None, :] >= i[:, None]).astype(np.float32) if j == kb else np.ones((128, 128), np.float32))
            c[:, kb * 512 + j * 128: kb * 512 + (j + 1) * 128] = blk
    m = np.concatenate([(i[None, :] <= i[:, None]), (i[None, :] >= i[:, None])], 1).astype(np.float32)
    c[:, 2048:2304] = m
    c[:, 2304:2560] = m
    f64 = THETA ** (-np.arange(32, dtype=np.float32) * 2.0 / 64)
    f32_ = THETA ** (-np.arange(16, dtype=np.float32) * 2.0 / 32)
    c[0:32, 2560] = f64
    c[32:64, 2560] = f64
    c[0:16, 2561] = f32_
    c[32:48, 2561] = f32_
    c[0:32, 2562] = -1
    c[32:64, 2562] = 1
    c[0:16, 2563] = -1
    c[32:48, 2563] = 1
    ic = np.zeros((4, S), np.float32)
    t = np.arange(S)
    for g, w in enumerate((2, 4, 8, 16)):
        ic[g] = 1.0 / np.minimum(t + 1, w)
    return c, ic


class Prog:
    def __init__(self, nc, es, n_layers, debug=()):
        self.nc = nc
        self.es = es
        self.S = Sched(nc, es)
        self.nl = n_layers
        self.debug = debug
        S_ = self.S
        sb = self.sb
        self.ps = [es.enter_context(nc.psum_tensor("ps%d" % i, [128, 512], F32)) for i in range(8)]
        self.ps_i = 0
        self.wslots = [sb("wslot%d" % i, [128, 8192], BF16) for i in range(2)]
        self.ws_i = 0
        self.BFA = sb("BFA", [128, 32768], BF16)
        self.FA = sb("FA", [128, 16384], F32)
        self.vecs = sb("vecs_sb", [128, NV], F32)
        self.cst_f = sb("cst_f", [128, 8], F32)
        self.cst_b = sb("cst_b", [128, 2560], BF16)
        self.ones_b = sb("ones_b", [128, 128], BF16)
        self.ones_f = sb("ones_f", [128, 128], F32)
        self.stg_f = [sb("stg_f%d" % i, [128, 1024], F32) for i in range(2)]
        self.stg_b = [sb("stg_b%d" % i, [128, 1024], BF16) for i in range(2)]
        self.tmp_f = [sb("tmp_f%d" % i, [128, 1024], F32) for i in range(4)]
        self.pt = [sb("pt%d" % i, [128, 512], BF16) for i in range(3)]
        self.halo = sb("halo", [128, 88, 2], F32)
        self.posi = sb("posi", [64, TB], I32)
        self.cnt = {}
        dr = self.dram
        self.xT_in = nc.dram_tensor("xT", [D, S], F32, kind="ExternalInput").ap()
        self.pos_in = nc.dram_tensor("pos", [1, S], I32, kind="ExternalInput").ap()
        self.cst_in = nc.dram_tensor("cst", [128, 2568], F32, kind="ExternalInput").ap()
        self.icnt_in = nc.dram_tensor("icnt", [4, S], F32, kind="ExternalInput").ap()
        self.w = {k: nc.dram_tensor(k, [n_layers] + shp, F32, kind="ExternalInput").ap() for k, shp in W_SHAPES.items()}
        self.out = nc.dram_tensor("yT", [D, S], F32, kind="ExternalOutput").ap()
        self.XF = dr("XF", [D, S], F32)
        self.XB = dr("XB", [D, S], BF16)
        self.X1F = dr("X1F", [D, S], F32)
        self.X1B = dr("X1B", [D, S], BF16)
        self.CQ = dr("CQ", [1024, S], F32)
        self.UP = dr("UP", [1024, S], F32)
        self.GL = dr("GL", [1024, S], F32)
        self.DQ = dr("DQ", [3, 1024, S], BF16)
        self.DK = dr("DK", [3, 1024, S], BF16)
        self.DV = dr("DV", [3, S, 1024], BF16)
        self.MQ = dr("MQ", [8, 192, S], BF16)
        self.MK = dr("MK", [8, 128, S], BF16)
        self.MKPE = dr("MKPE", [64, S], BF16)
        self.MV = dr("MV", [S, 1024], BF16)
        self.ACTS = dr("ACTS", [4, 1024, S], BF16)
        self.MG = dr("MG", [D, S], BF16)
        self.Z = dr("Z", [D, S], F32)
        self.HH = dr("HH", [FFN, S], BF16)
        self.ROPE = dr("ROPE", [4, 64, S], F32)
        self.q_i = 0

    def sb(self, name, shape, dt):
        return self.es.enter_context(self.nc.sbuf_tensor(name, shape, dt))

    def dram(self, name, shape, dt):
        kind = "ExternalOutput" if name in self.debug else "Internal"
        return self.nc.dram_tensor(name, shape, dt, kind=kind).ap()

    def next_ps(self, n):
        if self.ps_i + n > 8:
            self.ps_i = 0
        r = list(range(self.ps_i, self.ps_i + n))
        self.ps_i = (self.ps_i + n) % 8
        return r

    def acc_banks(self):
        i = self.rot("accb", 2)
        return (0, 1) if i == 0 else (2, 3)

    def s_bank(self):
        return 4 + self.rot("sbank", 4)

    def rot(self, name, n):
        i = self.cnt.get(name, 0)
        self.cnt[name] = (i + 1) % n
        return i

    def q(self):
        return "sp"

    def ld(self, key, out, in_, W, R=(), q=None):
        return self.S.dma(q or "sp", key, out, in_, R=list(R), W=list(W))

    def st(self, key, out, in_, R, W=(), q=None, is_out=False):
        return self.S.dma(q or "sp", key, out, in_, R=list(R), W=list(W), is_out=is_out)

    def linear_fm(self, w_dram, KC, G, groups, xT, xbufs, evac, mw=128, per_evac=1, nch=2, blocks=None):
        S_ = self.S
        tpg = G // mw
        loaded = {}
        groups = list(groups)
        blks = blocks if blocks is not None else [(None, xT, xbufs)]

        def load(i):
            wi = self.rot("ws", 2)
            S_.dma("pool", "wslot%d" % wi, self.wslots[wi][:, 0:KC * G], w_dram[groups[i]], W=[S_.b("wslot", wi)])
            loaded[i] = wi

        load(0)
        for i, g in enumerate(groups):
            if i + 1 < len(groups):
                load(i + 1)
            wi = loaded.pop(i)
            wv = self.wslots[wi][:, 0:KC * G].rearrange("p (k c) -> p k c", c=G)
            for mt0 in range(0, tpg, per_evac):
                for (tb, xv_, xb_) in blks:
                    pm, pp = [], []
                    for mt in range(mt0, mt0 + per_evac):
                        banks = self.next_ps(nch)
                        for kc in range(KC):
                            for c in range(nch):
                                S_.pe("matmul", self.ps[banks[c]][0:mw, :], lhsT=wv[:, kc, mt * mw:(mt + 1) * mw],
                                      rhs=(xv_(kc, c) if callable(xv_) else xv_[:, kc, c * 512:(c + 1) * 512]), start=(kc == 0), stop=(kc == KC - 1),
                                      R=[S_.b("wslot", wi)] + list(xb_), W=[S_.b("ps", banks[c])])
                        pm.append((g, mt))
                        pp.append(banks)
                    if blocks is None:
                        evac(pm, pp)
                    else:
                        evac(pm, pp, tb)

    def linear_tm(self, w_dram, KC, groups, xT, xbufs, evac, blocks=None):
        S_ = self.S
        G = 512
        blks = blocks if blocks is not None else [(None, xT, xbufs)]
        for g in groups:
            wi = self.rot("ws", 2)
            S_.dma("pool", "wslot%d" % wi, self.wslots[wi][:, 0:KC * G], w_dram[g], W=[S_.b("wslot", wi)])
            wv = self.wslots[wi][:, 0:KC * G].rearrange("p (k c) -> p k c", c=G)
            for (tb, xv_, xb_) in blks:
                for tt in range(8):
                    bk = self.next_ps(1)[0]
                    for kc in range(KC):
                        S_.pe("matmul", self.ps[bk][:, :], lhsT=xv_[:, kc, tt * 128:(tt + 1) * 128], rhs=wv[:, kc, :],
                              start=(kc == 0), stop=(kc == KC - 1),
                              R=[S_.b("wslot", wi)] + list(xb_), W=[S_.b("ps", bk)])
                    if blocks is None:
                        evac(g, tt, bk)
                    else:
                        evac(g, tt, bk, tb)

    def evac_copy(self, dst, dbuf, bk, rows=128, eng=None):
        S_ = self.S
        e = eng or ("act" if self.rot("ev", 2) == 0 else "dve")
        if e == "act":
            S_.act("copy", out=dst, in_=self.ps[bk][0:rows, :], R=[S_.b("ps", bk)], W=[dbuf])
        else:
            S_.dve("tensor_copy", out=dst, in_=self.ps[bk][0:rows, :], R=[S_.b("ps", bk)], W=[dbuf])

    def load_xT(self, src, t0, dep=(), slot=0):
        S_ = self.S
        v = self.BFA[:, slot * 16384:(slot + 1) * 16384].rearrange("p (k t) -> p k t", t=TB)
        for hlf in range(2):
            self.ld("bfa_x%d_%d" % (slot, hlf), v[:, hlf * 8:(hlf + 1) * 8, :],
                    src[hlf * 1024:(hlf + 1) * 1024, t0:t0 + TB].rearrange("(k p) t -> p k t", p=128),
                    R=list(dep), W=[S_.b("bfa_x", slot, hlf)])
        return v, [S_.b("bfa_x", slot, 0), S_.b("bfa_x", slot, 1)]

    def setup(self):
        S_ = self.S
        self.ld("cst_f", self.cst_f[:, :], self.cst_in[:, 2560:2568], W=[S_.b("cst_f")])
        S_.dma("pool", "cst_b", self.cst_b[:, :], self.cst_in[:, 0:2560], W=[S_.b("cst_b")])
        S_.dve("memset", self.ones_f[:, :], 1.0, W=[S_.b("ones_f")])
        S_.dve("memset", self.ones_b[:, :], 1.0, W=[S_.b("ones_b")])
        S_.dve("memset", self.halo[:, :, :], 0.0, W=[S_.b("halo", ct) for ct in range(88)])
        for tb in range(2):
            t0 = tb * TB
            pi = self.posi
            src = self.pos_in[0:1, t0:t0 + TB]
            bsrc = bass.AP(src.tensor, src.offset, [[0, 64], [1, TB]])
            self.ld("posi", pi[:, :], bsrc, W=[S_.b("posi")])
            pf = self.tmp_f[0][0:64, :]
            S_.dve("tensor_copy", out=pf, in_=pi[:, :], R=[S_.b("posi")], W=[S_.b("tmp_f", 0)])
            for which in range(2):
                ang = self.tmp_f[1][0:64, :]
                S_.dve("tensor_scalar", out=ang, in0=pf, scalar1=self.cst_f[0:64, which:which + 1], scalar2=None,
                       op0=ALU.mult, R=[S_.b("tmp_f", 0), S_.b("cst_f")], W=[S_.b("tmp_f", 1)])
                for cs in range(2):
                    red = self.tmp_f[2][0:64, :]
                    S_.dve("tensor_scalar", out=red, in0=ang, scalar1=1.0 / (2 * math.pi), scalar2=(0.25 if cs == 0 else 0.0),
                           op0=ALU.mult, op1=ALU.add, R=[S_.b("tmp_f", 1)], W=[S_.b("tmp_f", 2)])
                    S_.dve("tensor_copy", out=self.posi[:, :], in_=red, R=[S_.b("tmp_f", 2)], W=[S_.b("posi")])
                    tab = self.tmp_f[3][0:64, :]
                    S_.dve("tensor_copy", out=tab, in_=self.posi[:, :], R=[S_.b("posi")], W=[S_.b("tmp_f", 3)])
                    S_.dve("tensor_tensor", out=red, in0=red, in1=tab, op=ALU.subtract, R=[S_.b("tmp_f", 2), S_.b("tmp_f", 3)],
                           W=[S_.b("tmp_f", 2)])
                    S_.dve("scalar_tensor_tensor", out=red, in0=red, scalar=0.5, in1=red, op0=ALU.is_ge, op1=ALU.subtract,
                           R=[S_.b("tmp_f", 2)], W=[S_.b("tmp_f", 2)])
                    S_.act("activation", out=tab, in_=red, func=AF.Sin, scale=-2 * math.pi, R=[S_.b("tmp_f", 2)], W=[S_.b("tmp_f", 3)])
                    if cs == 1:
                        S_.dve("tensor_scalar", out=tab, in0=tab, scalar1=self.cst_f[0:64, 2 + which:3 + which], scalar2=None,
                               op0=ALU.mult, R=[S_.b("tmp_f", 3), S_.b("cst_f")], W=[S_.b("tmp_f", 3)])
                    self.st("tmp_f3", self.ROPE[which * 2 + cs, :, t0:t0 + TB], tab, R=[S_.b("tmp_f", 3)],
                            W=[S_.b("ROPE", tb)])
        for kc in range(16):
            i = self.rot("stg_f", 2)
            for hh in range(2):
                self.ld("stg_f%d" % i, self.stg_f[i][:, :], self.xT_in[kc * 128:(kc + 1) * 128, hh * TB:(hh + 1) * TB],
                        W=[S_.b("stg_f", i)], q=self.q())
                self.st("stg_f%d" % i, self.XF[kc * 128:(kc + 1) * 128, hh * TB:(hh + 1) * TB], self.stg_f[i][:, :],
                        R=[S_.b("stg_f", i)], W=[S_.b("XF", hh)], q=self.q())
                j = self.rot("stg_b", 2)
                S_.dve("tensor_copy", out=self.stg_b[j][:, :], in_=self.stg_f[i][:, :], R=[S_.b("stg_f", i)], W=[S_.b("stg_b", j)])
                self.st("stg_b%d" % j, self.XB[kc * 128:(kc + 1) * 128, hh * TB:(hh + 1) * TB], self.stg_b[j][:, :],
                        R=[S_.b("stg_b", j)], W=[S_.b("XB", hh)], q=self.q())
                i = self.rot("stg_f", 2)

    def load_vecs(self, l):
        S_ = self.S
        self.ld("vecs", self.vecs[:, :], self.w["vecs"][l], W=[S_.b("vecs")])

    def vcol(self, c, rows=128):
        return self.vecs[0:rows, c:c + 1]

    def load_rope(self, which, tb, dst_c, dst_s, bufs):
        S_ = self.S
        t0 = tb * TB
        self.ld("k_" + "_".join(map(str, bufs[0].name)), dst_c, self.ROPE[which * 2, :, t0:t0 + TB], W=[bufs[0]], R=[S_.b("ROPE", tb)])
        self.ld("k_" + "_".join(map(str, bufs[1].name)), dst_s, self.ROPE[which * 2 + 1, :, t0:t0 + TB], W=[bufs[1]], R=[S_.b("ROPE", tb)])

    def rope_evac(self, bk, c, rows, half, cosT, sinT, tbufs, dst, dbuf):
        S_ = self.S
        P = self.ps[bk]
        cs = slice(c * 512, (c + 1) * 512)
        t2 = self.tmp_f[2][0:rows, 0:512]
        t1 = self.tmp_f[3][0:rows, 0:512]
        R0 = [S_.b("ps", bk)] + list(tbufs)
        S_.dve("tensor_tensor", out=t2, in0=P[0:rows, :], in1=cosT[0:rows, cs], op=ALU.mult, R=R0, W=[S_.b("tmp_f", 2)])
        if rows == 48:
            S_.dve("memset", self.tmp_f[3][0:48, 0:512], 0.0, W=[S_.b("tmp_f", 3)])
        S_.dve("tensor_tensor", out=self.tmp_f[3][0:half, 0:512], in0=P[32:32 + half, :], in1=sinT[0:half, cs], op=ALU.mult,
               R=R0, W=[S_.b("tmp_f", 3)])
        S_.dve("tensor_tensor", out=self.tmp_f[3][32:32 + half, 0:512], in0=P[0:half, :], in1=sinT[32:32 + half, cs],
               op=ALU.mult, R=R0, W=[S_.b("tmp_f", 3)])
        S_.dve("tensor_tensor", out=dst, in0=t2, in1=t1, op=ALU.add, R=[S_.b("tmp_f", 2), S_.b("tmp_f", 3)], W=[dbuf])

    def stage_P(self, l):
        S_ = self.S
        S_.fence()
        blocks = []
        for tb in range(2):
            xv, xb = self.load_xT(self.XB, tb * TB, [S_.b("XB", tb)], slot=tb)
            blocks.append((tb, xv, xb))

        def tab(which, cs, tb):
            o = (which * 2 + cs) * 2048 + tb * 1024
            return self.FA[0:64, o:o + 1024]

        rbuf = {}
        for which in range(2):
            for tb in range(2):
                bufs = [S_.b("fa_r", which * 2, tb), S_.b("fa_r", which * 2 + 1, tb)]
                self.load_rope(which, tb, tab(which, 0, tb), tab(which, 1, tb), bufs)
                rbuf[(which, tb)] = bufs

        def ev1(pm, pp, tb):
            t0 = tb * TB
            (g, mt), banks = pm[0], pp[0]
            i = self.rot("stg_f", 2)
            for c in range(2):
                self.evac_copy(self.stg_f[i][:, c * 512:(c + 1) * 512], S_.b("stg_f", i), banks[c])
            dst = self.CQ if g < 2 else self.UP
            row = (g % 2) * 512 + mt * 128
            self.st("stg_f%d" % i, dst[row:row + 128, t0:t0 + TB], self.stg_f[i][:, :], R=[S_.b("stg_f", i)],
                    W=[S_.b("CQ" if g < 2 else "UP", tb)])

        self.linear_fm(self.w["w_p1"][l], 16, 512, range(4), None, None, ev1, blocks=blocks)

        def evk(pm, pp, tb):
            t0 = tb * TB
            banks = pp[0]
            i = self.rot("stg_b", 2)
            for c in range(2):
                self.rope_evac(banks[c], c, 64, 32, tab(0, 0, tb), tab(0, 1, tb), rbuf[(0, tb)],
                               self.stg_b[i][0:64, c * 512:(c + 1) * 512], S_.b("stg_b", i))
            self.st("stg_b%d" % i, self.MKPE[:, t0:t0 + TB], self.stg_b[i][0:64, :], R=[S_.b("stg_b", i)], W=[S_.b("MKPE", tb)])

        self.linear_fm(self.w["w_kpe"][l], 16, 64, range(1), None, None, evk, mw=64, blocks=blocks)

        def ev2(pm, pp, tb):
            t0 = tb * TB
            (g, mt), ba = pm[0], pp[0]
            bb = pp[1]
            m = g * 2 + mt // 2
            i = self.rot("stg_f", 2)
            for c in range(2):
                sg = self.tmp_f[0][:, c * 512:(c + 1) * 512]
                S_.act("activation", out=sg, in_=self.ps[bb[c]][:, :], func=AF.Sigmoid, R=[S_.b("ps", bb[c])], W=[S_.b("tmp_f", 0)])
                S_.dve("tensor_tensor", out=self.stg_f[i][:, c * 512:(c + 1) * 512], in0=self.ps[ba[c]][:, :], in1=sg, op=ALU.mult,
                       R=[S_.b("ps", ba[c]), S_.b("tmp_f", 0)], W=[S_.b("stg_f", i)])
            self.st("stg_f%d" % i, self.GL[m * 128:(m + 1) * 128, t0:t0 + TB], self.stg_f[i][:, :], R=[S_.b("stg_f", i)],
                    W=[S_.b("GL", tb)])

        self.linear_fm(self.w["w_p2"][l], 16, 512, range(4), None, None, ev2, per_evac=2, blocks=blocks)

        def ev3(pm, pp, tb):
            t0 = tb * TB
            (g, mt), banks = pm[0], pp[0]
            tix = g * 4 + mt
            grp, which, h = tix // 16, (tix // 8) % 2, tix % 8
            i = self.rot("stg_b", 2)
            sb_ = self.stg_b[i]
            for c in range(2):
                P = self.ps[banks[c]]
                cs = slice(c * 512, (c + 1) * 512)
                S_.dve("tensor_copy", out=sb_[64:128, cs], in_=P[64:128, :], R=[S_.b("ps", banks[c])], W=[S_.b("stg_b", i)])
                self.rope_evac(banks[c], c, 64, 32, tab(1, 0, tb), tab(1, 1, tb), rbuf[(1, tb)], sb_[0:64, cs], S_.b("stg_b", i))
            dst = (self.DQ if which == 0 else self.DK)[grp, h * 128:(h + 1) * 128, t0:t0 + TB]
            self.st("stg_b%d" % i, dst, sb_[:, :], R=[S_.b("stg_b", i)], W=[S_.b("DQK", tb)])

        self.linear_fm(self.w["w_dqk"][l], 16, 512, range(12), None, None, ev3, blocks=blocks)

        def ev4(g, tt, bk, tb):
            t0 = tb * TB
            grp, hf = g // 2, g % 2
            i = self.rot("pt", 3)
            self.evac_copy(self.pt[i][:, :], S_.b("pt", i), bk)
            self.st("pt%d" % i, self.DV[grp, t0 + tt * 128:t0 + (tt + 1) * 128, hf * 512:(hf + 1) * 512], self.pt[i][:, :],
                    R=[S_.b("pt", i)], W=[S_.b("DV", tb)])

        self.linear_tm(self.w["w_dv"][l], 16, range(6), None, None, ev4, blocks=blocks)

    def colsum(self, bk, srcs, sbufs):
        S_ = self.S
        n = len(srcs)
        for i, (s, b) in enumerate(zip(srcs, sbufs)):
            S_.pe("matmul", self.ps[bk][:, :], lhsT=self.ones_f[:, :], rhs=s, start=(i == 0), stop=(i == n - 1),
                  R=[S_.b("ones_f"), b], W=[S_.b("ps", bk)])

    def stage_A1(self, l, tb):
        S_ = self.S
        S_.fence()
        t0 = tb * TB
        cq = self.FA[:, 0:8192].rearrange("p (k t) -> p k t", t=TB)
        for hf in range(2):
            self.ld("fa_cq%d" % hf, cq[:, hf * 4:(hf + 1) * 4, :],
                    self.CQ[hf * 512:(hf + 1) * 512, t0:t0 + TB].rearrange("(k p) t -> p k t", p=128),
                    R=[S_.b("CQ", tb)], W=[S_.b("fa_cq", hf)], q=self.q())
        cosM, sinM = self.FA[0:64, 8192:9216], self.FA[0:64, 9216:10240]
        rb = [S_.b("fa_r", 0), S_.b("fa_r", 1)]
        self.load_rope(0, tb, cosM, sinM, rb)
        xn = self.BFA[:, 0:8192].rearrange("p (k t) -> p k t", t=TB)
        for hf in range(2):
            for c in range(2):
                bk = self.next_ps(1)[0]
                for k in range(4):
                    sq = self.tmp_f[k][:, 0:512]
                    S_.act("activation", out=sq, in_=cq[:, hf * 4 + k, c * 512:(c + 1) * 512], func=AF.Square,
                           R=[S_.b("fa_cq", hf)], W=[S_.b("tmp_f", k)])
                self.colsum(bk, [self.tmp_f[k][:, 0:512] for k in range(4)], [S_.b("tmp_f", k) for k in range(4)])
                rs = self.stg_f[0][:, c * 512:(c + 1) * 512]
                S_.dve("tensor_scalar", out=rs, in0=self.ps[bk][:, :], scalar1=1.0 / 512, scalar2=1e-6, op0=ALU.mult, op1=ALU.add,
                       R=[S_.b("ps", bk)], W=[S_.b("stg_f", 0)])
                S_.act("activation", out=rs, in_=rs, func=AF.Sqrt, R=[S_.b("stg_f", 0)], W=[S_.b("stg_f", 0)])
                S_.dve("reciprocal", out=rs, in_=rs, R=[S_.b("stg_f", 0)], W=[S_.b("stg_f", 0)])
                for k in range(4):
                    gc = self.vcol((V_GQ if hf == 0 else V_GKV) + k)
                    S_.dve("scalar_tensor_tensor", out=xn[:, hf * 4 + k, c * 512:(c + 1) * 512],
                           in0=cq[:, hf * 4 + k, c * 512:(c + 1) * 512], scalar=gc, in1=rs, op0=ALU.mult, op1=ALU.mult,
                           R=[S_.b("fa_cq", hf), S_.b("vecs"), S_.b("stg_f", 0)], W=[S_.b("bfa_xn", hf)])
        xq, xkv = xn[:, 0:4, :], xn[:, 4:8, :]

        def ev_plain(dst_of):
            def ev(pm, pp):
                (g, mt), banks = pm[0], pp[0]
                h = g * 4 + mt
                i = self.rot("stg_b", 2)
                for c in range(2):
                    self.evac_copy(self.stg_b[i][:, c * 512:(c + 1) * 512], S_.b("stg_b", i), banks[c])
                self.st("stg_b%d" % i, dst_of(h), self.stg_b[i][:, :], R=[S_.b("stg_b", i)], W=[S_.b("MQK", tb)], q=self.q())
            return ev

        self.linear_fm(self.w["w_uqn"][l], 4, 512, range(2), xq, [S_.b("bfa_xn", 0)],
                       ev_plain(lambda h: self.MQ[h, 0:128, t0:t0 + TB]))

        def ev_qr(pm, pp):
            (g, mt), banks = pm[0], pp[0]
            h = mt
            i = self.rot("stg_b", 2)
            for c in range(2):
                self.rope_evac(banks[c], c, 64, 32, cosM, sinM, rb, self.stg_b[i][0:64, c * 512:(c + 1) * 512], S_.b("stg_b", i))
            self.st("stg_b%d" % i, self.MQ[h, 128:192, t0:t0 + TB], self.stg_b[i][0:64, :], R=[S_.b("stg_b", i)],
                    W=[S_.b("MQK", tb)], q=self.q())

        self.linear_fm(self.w["w_uqr"][l], 4, 512, range(1), xq, [S_.b("bfa_xn", 0)], ev_qr, mw=64)
        self.linear_fm(self.w["w_ukn"][l], 4, 512, range(2), xkv, [S_.b("bfa_xn", 1)],
                       ev_plain(lambda h: self.MK[h, :, t0:t0 + TB]))

        def ev_v(g, tt, bk):
            i = self.rot("pt", 3)
            self.evac_copy(self.pt[i][:, :], S_.b("pt", i), bk)
            self.st("pt%d" % i, self.MV[t0 + tt * 128:t0 + (tt + 1) * 128, g * 512:(g + 1) * 512], self.pt[i][:, :],
                    R=[S_.b("pt", i)], W=[S_.b("MV", tb)], q=self.q())

        self.linear_tm(self.w["w_ukv"][l], 4, range(2), xkv, [S_.b("bfa_xn", 1)], ev_v)

    def stage_A2(self, l, tb):
        S_ = self.S
        S_.fence()
        t0 = tb * TB
        nk = t0 + TB
        nkb = nk // 128
        scale = 192 ** -0.5
        kpe = self.BFA[0:64, 0:2048]
        self.ld("bfa_kpe", kpe[:, 0:nk], self.MKPE[:, 0:nk], R=[S_.b("MKPE", 0), S_.b("MKPE", 1)], W=[S_.b("bfa_kpe")])
        masks = self.cst_b[:, 0:2048]
        for h in range(8):
            s = h % 2
            base = 2048 + s * 8192
            qn = self.BFA[:, base:base + 1024]
            qr = self.BFA[0:64, base + 1024:base + 2048]
            kn = self.BFA[:, base + 2048:base + 4096]
            vv = self.BFA[:, base + 4096:base + 6144].rearrange("p (k d) -> p k d", d=128)
            bq, bk_, bv = S_.b("a2_q", s), S_.b("a2_k", s), S_.b("a2_v", s)
            dep = [S_.b("MQK", 0), S_.b("MQK", 1)]
            self.ld("a2_qn%d" % s, qn, self.MQ[h, 0:128, t0:t0 + TB], R=dep, W=[bq])
            self.ld("a2_qr%d" % s, qr, self.MQ[h, 128:192, t0:t0 + TB], R=dep, W=[bq])
            self.ld("a2_k%d" % s, kn[:, 0:nk], self.MK[h, :, 0:nk], R=dep, W=[bk_])
            self.ld("a2_v%d" % s, vv[:, 0:nkb, :], self.MV[0:nk, h * 128:(h + 1) * 128].rearrange("(k p) d -> p k d", p=128),
                    R=[S_.b("MV", 0), S_.b("MV", 1)], W=[bv])
            for qc in range(2):
                qb0 = t0 // 128 + qc * 4
                nkeys = qb0 + 4
                ob, zb = self.acc_banks()
                for kb in range(nkeys):
                    sb_ = self.s_bank()
                    qs = slice(qc * 512, (qc + 1) * 512)
                    S_.pe("matmul", self.ps[sb_][:, :], lhsT=kn[:, kb * 128:(kb + 1) * 128], rhs=qn[:, qs], start=True, stop=False,
                          R=[bk_, bq], W=[S_.b("ps", sb_)])
                    S_.pe("matmul", self.ps[sb_][:, :], lhsT=kpe[:, kb * 128:(kb + 1) * 128], rhs=qr[:, qs], start=False, stop=True,
                          R=[S_.b("bfa_kpe"), bq], W=[S_.b("ps", sb_)])
                    pi = self.rot("pt", 3)
                    P = self.pt[pi]
                    S_.act("activation", out=P[:, :], in_=self.ps[sb_][:, :], func=AF.Exp, scale=scale,
                           R=[S_.b("ps", sb_)], W=[S_.b("pt", pi)])
                    if kb >= qb0:
                        i = kb - qb0
                        S_.dve("tensor_tensor", out=P[:, :], in0=P[:, :], in1=masks[:, i * 512:(i + 1) * 512], op=ALU.mult,
                               R=[S_.b("pt", pi), S_.b("cst_b")], W=[S_.b("pt", pi)])
                    S_.pe("matmul", self.ps[ob][:, :], lhsT=vv[:, kb, :], rhs=P[:, :], start=(kb == 0), stop=(kb == nkeys - 1),
                          R=[bv, S_.b("pt", pi)], W=[S_.b("ps", ob)])
                    S_.pe("matmul", self.ps[zb][:, :], lhsT=self.ones_b[:, :], rhs=P[:, :], start=(kb == 0), stop=(kb == nkeys - 1),
                          R=[S_.b("ones_b"), S_.b("pt", pi)], W=[S_.b("ps", zb)])
                rc = self.tmp_f[0][:, 0:512]
                S_.dve("reciprocal", out=rc, in_=self.ps[zb][:, :], R=[S_.b("ps", zb)], W=[S_.b("tmp_f", 0)])
                i = self.rot("stg_b", 2)
                S_.dve("tensor_tensor", out=self.stg_b[i][:, 0:512], in0=self.ps[ob][:, :], in1=rc, op=ALU.mult,
                       R=[S_.b("ps", ob), S_.b("tmp_f", 0)], W=[S_.b("stg_b", i)])
                self.st("stg_b%d" % i, self.ACTS[0, h * 128:(h + 1) * 128, t0 + qc * 512:t0 + (qc + 1) * 512], self.stg_b[i][:, 0:512],
                        R=[S_.b("stg_b", i)], W=[S_.b("ACTS", 0, tb)], q=self.q())

    def stage_B(self, l, tb):
        S_ = self.S
        S_.fence()
        t0 = tb * TB
        HL = 16
        up = self.FA[:, 0:8 * 1040].rearrange("p (k t) -> p k t", t=1040)
        ic = self.FA[:, 8320:8320 + 4096].rearrange("p (g t) -> p g t", t=TB)
        src = self.icnt_in[0:4, t0:t0 + TB]
        self.ld("fa_ic", ic, bass.AP(src.tensor, src.offset, [[0, 128], [S, 4], [1, TB]]), W=[S_.b("fa_ic")])
        pw = self.wslots[1][:, 0:2048].rearrange("p (g o) -> p g o", o=256)
        S_.dma("pool", "wslot1", self.wslots[1][:, 0:2048], self.w["w_pool"][l], W=[S_.b("wslot", 1)])
        dT = self.BFA[:, 0:8192].rearrange("p (k t) -> p k t", t=TB)
        if tb == 0:
            S_.dve("memset", up[:, :, 0:HL], 0.0, W=[S_.b("fa_up")])
            self.ld("fa_up", up[:, :, HL:HL + TB], self.UP[:, 0:TB].rearrange("(k p) t -> p k t", p=128),
                    R=[S_.b("UP", 0)], W=[S_.b("fa_up")])
        else:
            self.ld("fa_up", up[:, :, :], self.UP[:, t0 - HL:t0 + TB].rearrange("(k p) t -> p k t", p=128),
                    R=[S_.b("UP", 0), S_.b("UP", 1)], W=[S_.b("fa_up")])
        for ct in range(8):
            g = ct // 2
            cur = up[:, ct, :]
            cb = S_.b("fa_up")
            lo = 0
            step = 1
            n = 0
            while step < (2 << g):
                ti = n % 2
                nxt = self.tmp_f[ti][:, :]
                nv = self.FA[:, 12416 + ti * 1040:12416 + (ti + 1) * 1040]
                S_.dve("tensor_tensor", out=nv[:, step:1040], in0=cur[:, step:1040], in1=cur[:, 0:1040 - step], op=ALU.add,
                       R=[cb], W=[S_.b("fa_pp", ti)])
                cur = nv
                cb = S_.b("fa_pp", ti)
                step *= 2
                n += 1
            mean = self.tmp_f[2][:, :]
            S_.dve("tensor_tensor", out=mean, in0=cur[:, HL:HL + TB], in1=ic[:, g, :], op=ALU.mult, R=[cb, S_.b("fa_ic")],
                   W=[S_.b("tmp_f", 2)])
            S_.dve("tensor_tensor", out=dT[:, ct, :], in0=mean, in1=up[:, ct, HL:HL + TB], op=ALU.subtract,
                   R=[S_.b("tmp_f", 2), S_.b("fa_up")], W=[S_.b("bfa_d")])
        for ct in range(8):
            g, hf = ct // 2, ct % 2
            banks = self.next_ps(2)
            for kc in range(2):
                for c in range(2):
                    S_.pe("matmul", self.ps[banks[c]][:, :], lhsT=pw[:, g * 2 + kc, hf * 128:(hf + 1) * 128],
                          rhs=dT[:, g * 2 + kc, c * 512:(c + 1) * 512], start=(kc == 0), stop=(kc == 1),
                          R=[S_.b("wslot", 1), S_.b("bfa_d")], W=[S_.b("ps", banks[c])])
            i = self.rot("stg_b", 2)
            for c in range(2):
                S_.act("activation", out=self.stg_b[i][:, c * 512:(c + 1) * 512], in_=self.ps[banks[c]][:, :], func=AF.Copy,
                       scale=self.vcol(V_PS + ct), R=[S_.b("ps", banks[c]), S_.b("vecs")], W=[S_.b("stg_b", i)])
            self.st("stg_b%d" % i, self.ACTS[1, ct * 128:(ct + 1) * 128, t0:t0 + TB], self.stg_b[i][:, :], R=[S_.b("stg_b", i)],
                    W=[S_.b("ACTS", 1, tb)], q=self.q())

    def ln_stats(self, tiles, tbufs, nfeat, eps, mean, rstd, mbuf):
        S_ = self.S
        b1, b2 = self.next_ps(2)
        self.colsum(b1, tiles, tbufs)
        n = len(tiles)
        for i, (s, b) in enumerate(zip(tiles, tbufs)):
            k = self.rot("sq", 2)
            sq = self.tmp_f[k][:, 0:512]
            S_.act("activation", out=sq, in_=s, func=AF.Square, R=[b], W=[S_.b("tmp_f", k)])
            S_.pe("matmul", self.ps[b2][:, :], lhsT=self.ones_f[:, :], rhs=sq, start=(i == 0), stop=(i == n - 1),
                  R=[S_.b("ones_f"), S_.b("tmp_f", k)], W=[S_.b("ps", b2)])
        S_.dve("tensor_scalar", out=mean, in0=self.ps[b1][:, :], scalar1=1.0 / nfeat, scalar2=None, op0=ALU.mult,
               R=[S_.b("ps", b1)], W=[mbuf])
        m2 = self.tmp_f[2][:, 512:1024]
        S_.dve("tensor_tensor", out=m2, in0=mean, in1=mean, op=ALU.mult, R=[mbuf], W=[S_.b("tmp_f", 2)])
        S_.dve("scalar_tensor_tensor", out=rstd, in0=self.ps[b2][:, :], scalar=1.0 / nfeat, in1=m2, op0=ALU.mult, op1=ALU.subtract,
               R=[S_.b("ps", b2), S_.b("tmp_f", 2)], W=[mbuf])
        S_.dve("tensor_scalar", out=rstd, in0=rstd, scalar1=eps, scalar2=None, op0=ALU.add, R=[mbuf], W=[mbuf])
        S_.act("activation", out=rstd, in_=rstd, func=AF.Sqrt, R=[mbuf], W=[mbuf])
        S_.dve("reciprocal", out=rstd, in_=rstd, R=[mbuf], W=[mbuf])

    def stage_C(self, l, tb):
        S_ = self.S
        S_.fence()
        t0 = tb * TB
        HL = 30
        acc = self.FA[:, 0:8192].rearrange("p (k t) -> p k t", t=TB)
        for ct in range(8):
            i = ct % 2
            gl = self.FA[:, 8192 + i * 1054:8192 + (i + 1) * 1054]
            gb = S_.b("fa_gl", i)
            if tb == 0:
                S_.dve("memset", gl[:, 0:HL], 0.0, W=[gb])
                self.ld("fa_gl%d" % i, gl[:, HL:HL + TB], self.GL[ct * 128:(ct + 1) * 128, 0:TB], R=[S_.b("GL", 0)], W=[gb], q=self.q())
            else:
                self.ld("fa_gl%d" % i, gl[:, :], self.GL[ct * 128:(ct + 1) * 128, t0 - HL:t0 + TB], R=[S_.b("GL", 0), S_.b("GL", 1)],
                        W=[gb], q=self.q())
            a = acc[:, ct, :]
            ab = S_.b("fa_acc", ct)
            S_.act("activation", out=a, in_=gl[:, 30:30 + TB], func=AF.Identity, scale=self.vcol(V_CW + 30 * 8 + ct),
                   bias=self.vcol(V_CB + ct), R=[gb, S_.b("vecs")], W=[ab])
            for k in range(30):
                S_.dve("scalar_tensor_tensor", out=a, in0=gl[:, k:k + TB], scalar=self.vcol(V_CW + k * 8 + ct), in1=a,
                       op0=ALU.mult, op1=ALU.add, R=[gb, S_.b("vecs"), ab], W=[ab])
        for c in range(2):
            mean = self.FA[:, 10400:10912]
            rstd = self.FA[:, 10912:11424]
            mb = S_.b("fa_mr")
            self.ln_stats([acc[:, ct, c * 512:(c + 1) * 512] for ct in range(8)], [S_.b("fa_acc", ct) for ct in range(8)],
                          1024, 1e-5, mean, rstd, mb)
            for ct in range(8):
                t = self.tmp_f[3][:, 0:512]
                S_.dve("tensor_tensor", out=t, in0=acc[:, ct, c * 512:(c + 1) * 512], in1=mean, op=ALU.subtract,
                       R=[S_.b("fa_acc", ct), mb], W=[S_.b("tmp_f", 3)])
                S_.dve("tensor_tensor", out=t, in0=t, in1=rstd, op=ALU.mult, R=[S_.b("tmp_f", 3), mb], W=[S_.b("tmp_f", 3)])
                i = self.rot("pt", 3)
                S_.act("activation", out=self.pt[i][:, :], in_=t, func=AF.Silu, scale=self.vcol(V_CG + ct), bias=self.vcol(V_CBB + ct),
                       R=[S_.b("tmp_f", 3), S_.b("vecs")], W=[S_.b("pt", i)])
                self.st("pt%d" % i, self.ACTS[2, ct * 128:(ct + 1) * 128, t0 + c * 512:t0 + (c + 1) * 512], self.pt[i][:, :],
                        R=[S_.b("pt", i)], W=[S_.b("ACTS", 2, tb)], q=self.q())

    def stage_D(self, l, tb):
        S_ = self.S
        S_.fence()
        t0 = tb * TB
        scale = 128 ** -0.5
        dm = self.cst_b[:, 2048:2560]
        diag = self.cst_b[:, 2048 + 128:2048 + 256]
        for h in range(8):
            accU = self.FA[:, 0:1024]
            accZ = self.FA[:, 1024:2048]
            au, az = S_.b("fa_accU"), S_.b("fa_accZ")
            for gi, d in enumerate(DIL_D):
                s = (h * 3 + gi) % 2
                base = s * 8192
                L = S // d
                n = TB // d
                Q = self.BFA[:, base:base + 1024].rearrange("p (j r) -> p r j", r=d)
                K = self.BFA[:, base + 1024:base + 3072].rearrange("p (j r) -> p r j", r=d)
                bq, bk_ = S_.b("d_q", s), S_.b("d_k", s)
                dep = [S_.b("DQK", 0), S_.b("DQK", 1)]
                self.ld("d_q%d" % s, self.BFA[:, base:base + 1024], self.DQ[gi, h * 128:(h + 1) * 128, t0:t0 + TB], R=dep, W=[bq])
                self.ld("d_k%d" % s, self.BFA[:, base + 1024:base + 3072], self.DK[gi, h * 128:(h + 1) * 128, :], R=dep, W=[bk_])
                units = []
                if d == 16:
                    for r in range(16):
                        qa = 64 * tb
                        nkk = 64 * (tb + 1)
                        units.append((r, qa, 64, [(0, nkk, diag[0:nkk, qa:qa + 64])]))
                else:
                    for r in range(d):
                        for nb in range(n // 128):
                            blk = tb * (n // 128) + nb
                            kt = []
                            if blk > 0:
                                kt.append(((blk - 1) * 128, 128, dm[:, 0:128]))
                            kt.append((blk * 128, 128, dm[:, 128:256]))
                            units.append((r, blk * 128, 128, kt))
                bv = S_.b("d_v", s)
                vdep = [S_.b("DV", 0), S_.b("DV", 1)]
                vsrc = self.DV[gi, :, h * 128:(h + 1) * 128].rearrange("(jt p r) c -> p r jt c", p=128, r=d)
                if d == 16:
                    jt_lo, njt = 0, 1
                    nkk = 64 * (tb + 1)
                    vt = self.BFA[:, base + 3072:base + 3072 + 16 * 128].rearrange("p (r jt c) -> p r jt c", r=16, jt=1)
                    self.ld("d_v%d" % s, vt[0:nkk, :, 0, :], vsrc[0:nkk, :, 0, :], R=vdep, W=[bv])
                else:
                    blk0 = tb * (n // 128)
                    jt_lo = max(0, blk0 - 1)
                    njt = blk0 + n // 128 - jt_lo
                    vt = self.BFA[:, base + 3072:base + 3072 + d * njt * 128].rearrange("p (r jt c) -> p r jt c", r=d, jt=njt)
                    for r in range(d):
                        self.ld("d_v%d" % s, vt[:, r, :, :], vsrc[:, r, jt_lo:jt_lo + njt, :], R=vdep, W=[bv])
                vmap = {}
                for (r, qa, nq, kts) in units:
                    for (klo, nkk_, m) in kts:
                        vmap[(r, klo)] = (r, klo // 128 - jt_lo)
                for u0 in range(0, len(units), 2):
                    grp = units[u0:u0 + 2]
                    sbk = self.s_bank()
                    ub, zb = self.acc_banks()
                    pi = self.rot("pt", 3)
                    P = self.pt[pi]
                    col = 0
                    lay = []
                    for (r, qa, nq, kts) in grp:
                        ql = qa - tb * n
                        for (klo, nkk, m) in kts:
                            S_.pe("matmul", self.ps[sbk][0:nkk, col:col + nq], lhsT=K[:, r, klo:klo + nkk], rhs=Q[:, r, ql:ql + nq],
                                  start=True, stop=True, R=[bk_, bq], W=[S_.b("ps", sbk)])
                            lay.append((r, qa, nq, klo, nkk, m, col))
                            col += nq
                    rows = max(x[4] for x in lay)
                    S_.act("activation", out=P[0:rows, 0:col], in_=self.ps[sbk][0:rows, 0:col], func=AF.Exp, scale=scale,
                           R=[S_.b("ps", sbk)], W=[S_.b("pt", pi)])
                    for (r, qa, nq, klo, nkk, m, c0) in lay:
                        S_.dve("tensor_tensor", out=P[0:nkk, c0:c0 + nq], in0=P[0:nkk, c0:c0 + nq], in1=m, op=ALU.mult,
                               R=[S_.b("pt", pi), S_.b("cst_b")], W=[S_.b("pt", pi)])
                    oc = 0
                    outs = []
                    for ui, (r, qa, nq, kts) in enumerate(grp):
                        mine = [x for x in lay if x[0] == r and x[1] == qa]
                        for j, (r_, qa_, nq_, klo, nkk, m, c0) in enumerate(mine):
                            vr, vj = vmap[(r, klo)]
                            S_.pe("matmul", self.ps[ub][:, oc:oc + nq], lhsT=vt[0:nkk, vr, vj, :], rhs=P[0:nkk, c0:c0 + nq],
                                  start=(j == 0), stop=(j == len(mine) - 1), R=[bv, S_.b("pt", pi)], W=[S_.b("ps", ub)])
                        for j, (r_, qa_, nq_, klo, nkk, m, c0) in enumerate(mine):
                            S_.pe("matmul", self.ps[zb][:, oc:oc + nq], lhsT=self.ones_b[0:nkk, :], rhs=P[0:nkk, c0:c0 + nq],
                                  start=(j == 0), stop=(j == len(mine) - 1), R=[S_.b("ones_b"), S_.b("pt", pi)], W=[S_.b("ps", zb)])
                        outs.append((r, qa, nq, oc))
                        oc += nq
                    for (r, qa, nq, oc_) in outs:
                        ql = qa - tb * n
                        for (accT, ab, bank) in ((accU, au, ub), (accZ, az, zb)):
                            if d == 1:
                                dst = accT[:, ql:ql + nq]
                            else:
                                dst = accT.rearrange("p (j r) -> p r j", r=d)[:, r, ql:ql + nq]
                            src = self.ps[bank][:, oc_:oc_ + nq]
                            if gi == 0:
                                S_.act("copy", out=dst, in_=src, R=[S_.b("ps", bank)], W=[ab])
                            else:
                                S_.dve("tensor_tensor", out=dst, in0=src, in1=dst, op=ALU.add, R=[S_.b("ps", bank), ab], W=[ab])
            rc = self.tmp_f[0][:, :]
            S_.dve("reciprocal", out=rc, in_=accZ, R=[az], W=[S_.b("tmp_f", 0)])
            i = self.rot("stg_b", 2)
            S_.dve("tensor_tensor", out=self.stg_b[i][:, :], in0=accU, in1=rc, op=ALU.mult, R=[au, S_.b("tmp_f", 0)], W=[S_.b("stg_b", i)])
            self.st("stg_b%d" % i, self.ACTS[3, h * 128:(h + 1) * 128, t0:t0 + TB], self.stg_b[i][:, :], R=[S_.b("stg_b", i)],
                    W=[S_.b("ACTS", 3, tb)], q=self.q())

    def stage_M(self, l, tb):
        S_ = self.S
        S_.fence()
        t0 = tb * TB
        xv, xb = self.load_xT(self.XB, t0, [S_.b("XB", tb)])
        mg = self.FA[:, :].rearrange("p (k t) -> p k t", t=TB)
        for b in range(4):
            s = 0
            av = self.BFA[:, 16384:24576].rearrange("p (k t) -> p k t", t=TB)
            ab = S_.b("bfa_act", s)
            self.ld("bfa_act%d" % s, av, self.ACTS[b, :, t0:t0 + TB].rearrange("(k p) t -> p k t", p=128),
                    R=[S_.b("ACTS", b, tb)], W=[ab], q=self.q())
            for mgp in range(4):
                gate_ps = {}

                def evg(pm, pp):
                    for (g, mt), banks in zip(pm, pp):
                        m = mgp * 4 + mt
                        for c in range(2):
                            k = self.rot("sg", 2)
                            sg = self.stg_f[k][:, c * 512:(c + 1) * 512]
                            S_.act("activation", out=sg, in_=self.ps[banks[c]][:, :], func=AF.Sigmoid, bias=self.vcol(V_BG + b * 16 + m),
                                   R=[S_.b("ps", banks[c]), S_.b("vecs")], W=[S_.b("stg_f", k)])
                            gate_ps[(mt, c)] = (sg, S_.b("stg_f", k))
                        self._proj_tile(l, b, m, mt, av, ab, gate_ps, mg)

                self._proj_w = None
                self.linear_fm(self.w["w_gate"][l], 16, 512, [b * 4 + mgp], xv, xb, evg)
        for m in range(16):
            i = self.rot("stg_b", 2)
            S_.dve("tensor_copy", out=self.stg_b[i][:, :], in_=mg[:, m, :], R=[S_.b("fa_mg", m)], W=[S_.b("stg_b", i)])
            self.st("stg_b%d" % i, self.MG[m * 128:(m + 1) * 128, t0:t0 + TB], self.stg_b[i][:, :], R=[S_.b("stg_b", i)],
                    W=[S_.b("MG", tb)], q=self.q())

    def _proj_tile(self, l, b, m, mt, av, ab, gate_ps, mg):
        S_ = self.S
        if mt == 0:
            wi = self.rot("ws", 2)
            S_.dma("pool", "wslot%d" % wi, self.wslots[wi][:, 0:4096], self.w["w_proj"][l][b * 4 + m // 4], W=[S_.b("wslot", wi)])
            self._proj_w = wi
        wi = self._proj_w
        wv = self.wslots[wi][:, 0:4096].rearrange("p (k c) -> p k c", c=512)
        banks = self.next_ps(2)
        for kc in range(8):
            for c in range(2):
                S_.pe("matmul", self.ps[banks[c]][:, :], lhsT=wv[:, kc, mt * 128:(mt + 1) * 128], rhs=av[:, kc, c * 512:(c + 1) * 512],
                      start=(kc == 0), stop=(kc == 7), R=[S_.b("wslot", wi), ab], W=[S_.b("ps", banks[c])])
        for c in range(2):
            sg, sgb = gate_ps[(mt, c)]
            dst = mg[:, m, c * 512:(c + 1) * 512]
            if b == 0:
                S_.dve("tensor_tensor", out=dst, in0=self.ps[banks[c]][:, :], in1=sg, op=ALU.mult, R=[S_.b("ps", banks[c]), sgb],
                       W=[S_.b("fa_mg", m)])
            else:
                t = self.tmp_f[c][:, 0:512]
                S_.dve("tensor_tensor", out=t, in0=self.ps[banks[c]][:, :], in1=sg, op=ALU.mult, R=[S_.b("ps", banks[c]), sgb],
                       W=[S_.b("tmp_f", c)])
                S_.dve("tensor_tensor", out=dst, in0=dst, in1=t, op=ALU.add, R=[S_.b("tmp_f", c), S_.b("fa_mg", m)], W=[S_.b("fa_mg", m)])

    def stage_resid(self, w_dram, KC, G, ngroups, src_act, act_dep, res, res_dep, tb, nch, c0=0):
        S_ = self.S
        S_.fence()
        t0 = tb * TB + c0
        W_ = nch * 512
        kcap = min(KC, 32768 // W_)
        av = self.BFA[:, 0:kcap * W_].rearrange("p (k t) -> p k t", t=W_)
        xb = []
        kh = kcap // 2
        for hf, (k0, k1) in enumerate(((0, kh), (kh, kcap))):
            self.ld("bfa_rx%d" % hf, av[:, k0:k1, :], src_act[k0 * 128:k1 * 128, t0:t0 + W_].rearrange("(k p) t -> p k t", p=128),
                    R=act_dep, W=[S_.b("bfa_rx", hf)])
            xb.append(S_.b("bfa_rx", hf))
        av2 = None
        if KC > kcap:
            kx = KC - kcap
            av2 = self.FA[:, 0:kx * W_ // 2].bitcast(BF16).rearrange("p (k t) -> p k t", t=W_)
            self.ld("fa_rx", av2, src_act[kcap * 128:KC * 128, t0:t0 + W_].rearrange("(k p) t -> p k t", p=128),
                    R=act_dep, W=[S_.b("fa_rx")])
            xb.append(S_.b("fa_rx"))

        def xget(kc, c):
            if kc < kcap:
                return av[:, kc, c * 512:(c + 1) * 512]
            return av2[:, kc - kcap, c * 512:(c + 1) * 512]

        def ev(pm, pp):
            (g, mt), banks = pm[0], pp[0]
            m = g * (G // 128) + mt
            i = self.rot("stg_f", 2)
            k = self.rot("rs", 2)
            rs = self.tmp_f[k][:, 0:W_]
            self.ld("tmp_f%d" % k, rs, res[m * 128:(m + 1) * 128, t0:t0 + W_], R=res_dep, W=[S_.b("tmp_f", k)], q=self.q())
            for c in range(nch):
                S_.dve("scalar_tensor_tensor", out=self.stg_f[i][:, c * 512:(c + 1) * 512], in0=rs[:, c * 512:(c + 1) * 512], scalar=ALPHA,
                       in1=self.ps[banks[c]][:, :], op0=ALU.mult, op1=ALU.add, R=[S_.b("tmp_f", k), S_.b("ps", banks[c])],
                       W=[S_.b("stg_f", i)])
            self.st("stg_f%d" % i, self.Z[m * 128:(m + 1) * 128, t0:t0 + W_], self.stg_f[i][:, 0:W_], R=[S_.b("stg_f", i)],
                    W=[S_.b("Z", tb)], q=self.q())

        self.linear_fm(w_dram, KC, G, range(ngroups), xget, xb, ev, nch=nch)

    def stage_N(self, l, tb, gcol, bcol, dstF, dstB, fkey, final=False):
        S_ = self.S
        S_.fence()
        for c in range(2):
            t0 = tb * TB + c * 512
            z = self.FA[:, 0:8192].rearrange("p (k t) -> p k t", t=512)
            for hf in range(2):
                self.ld("fa_z%d" % hf, z[:, hf * 8:(hf + 1) * 8, :], self.Z[hf * 1024:(hf + 1) * 1024, t0:t0 + 512].rearrange("(k p) t -> p k t", p=128),
                        R=[S_.b("Z", tb)], W=[S_.b("fa_z", hf)], q=self.q())
            mean = self.FA[:, 8192:8704]
            rstd = self.FA[:, 8704:9216]
            mb = S_.b("fa_mr")
            self.ln_stats([z[:, k, :] for k in range(16)], [S_.b("fa_z", k // 8) for k in range(16)], 2048, 1e-5, mean, rstd, mb)
            for k in range(16):
                t = self.tmp_f[3][:, 0:512]
                S_.dve("tensor_tensor", out=t, in0=z[:, k, :], in1=mean, op=ALU.subtract, R=[S_.b("fa_z", k // 8), mb], W=[S_.b("tmp_f", 3)])
                S_.dve("tensor_tensor", out=t, in0=t, in1=rstd, op=ALU.mult, R=[S_.b("tmp_f", 3), mb], W=[S_.b("tmp_f", 3)])
                i = self.rot("stg_f", 2)
                o = self.stg_f[i][:, 0:512]
                S_.act("activation", out=o, in_=t, func=AF.Identity, scale=self.vcol(gcol + k), bias=self.vcol(bcol + k),
                       R=[S_.b("tmp_f", 3), S_.b("vecs")], W=[S_.b("stg_f", i)])
                if final:
                    self.st("stg_f%d" % i, self.out[k * 128:(k + 1) * 128, t0:t0 + 512], o, R=[S_.b("stg_f", i)], q=self.q(), is_out=True)
                    continue
                self.st("stg_f%d" % i, dstF[k * 128:(k + 1) * 128, t0:t0 + 512], o, R=[S_.b("stg_f", i)], W=[S_.b(fkey + "F", tb)], q=self.q())
                j = self.rot("pt", 3)
                S_.act("activation", out=self.pt[j][:, :], in_=t, func=AF.Identity, scale=self.vcol(gcol + k), bias=self.vcol(bcol + k),
                       R=[S_.b("tmp_f", 3), S_.b("vecs")], W=[S_.b("pt", j)])
                self.st("pt%d" % j, dstB[k * 128:(k + 1) * 128, t0:t0 + 512], self.pt[j][:, :], R=[S_.b("pt", j)], W=[S_.b(fkey + "B", tb)], q=self.q())

    def stage_F1(self, l):
        S_ = self.S
        S_.fence()
        blocks = []
        for tb_ in range(2):
            xv, xb = self.load_xT(self.X1B, tb_ * TB, [S_.b("X1B", tb_)], slot=tb_)
            blocks.append((tb_, xv, xb))

        def ev(pm, pp, tb):
            t0 = tb * TB
            g, mt0 = pm[0]
            m = g * 2 + mt0 // 2
            res = []
            for j, ((g_, mt), banks) in enumerate(zip(pm, pp)):
                ct = m + 44 * j
                k = self.rot("hb", 4)
                hb = self.FA[:, k * 1026:(k + 1) * 1026]
                hbb = S_.b("fa_hb", k)
                if tb == 0:
                    S_.dve("memset", hb[:, 0:2], 0.0, W=[hbb])
                else:
                    S_.dve("tensor_copy", out=hb[:, 0:2], in_=self.halo[:, ct, :], R=[S_.b("halo", ct)], W=[hbb])
                for c in range(2):
                    self.evac_copy(hb[:, 2 + c * 512:2 + (c + 1) * 512], hbb, banks[c], eng="act")
                S_.dve("tensor_copy", out=self.halo[:, ct, :], in_=hb[:, 1024:1026], R=[hbb], W=[S_.b("halo", ct)])
                a = self.FA[:, 4200 + k * 1024:4200 + (k + 1) * 1024]
                ab = S_.b("fa_cv", k)
                S_.dve("tensor_scalar", out=a, in0=hb[:, 2:1026], scalar1=self.vcol(V_FW + 2 * 88 + ct), scalar2=self.vcol(V_FB + ct),
                       op0=ALU.mult, op1=ALU.add, R=[hbb, S_.b("vecs")], W=[ab])
                S_.dve("scalar_tensor_tensor", out=a, in0=hb[:, 1:1025], scalar=self.vcol(V_FW + 88 + ct), in1=a, op0=ALU.mult, op1=ALU.add,
                       R=[hbb, S_.b("vecs"), ab], W=[ab])
                S_.dve("scalar_tensor_tensor", out=a, in0=hb[:, 0:1024], scalar=self.vcol(V_FW + ct), in1=a, op0=ALU.mult, op1=ALU.add,
                       R=[hbb, S_.b("vecs"), ab], W=[ab])
                res.append((a, ab))
            (va, vb), (ga, gb) = res
            S_.act("activation", out=ga, in_=ga, func=AF.Silu, R=[gb], W=[gb])
            i = self.rot("stg_b", 2)
            S_.dve("tensor_tensor", out=self.stg_b[i][:, :], in0=va, in1=ga, op=ALU.mult, R=[vb, gb], W=[S_.b("stg_b", i)])
            self.st("stg_b%d" % i, self.HH[m * 128:(m + 1) * 128, t0:t0 + TB], self.stg_b[i][:, :], R=[S_.b("stg_b", i)],
                    W=[S_.b("HH", tb)])

        self.linear_fm(self.w["w_up"][l], 16, 512, range(22), None, None, ev, per_evac=2, blocks=blocks)

    def layer(self, l, last):
        S_ = self.S
        S_.fence()
        self.load_vecs(l)
        self.stage_P(l)
        for tb in range(2):
            self.stage_A1(l, tb)
            self.stage_A2(l, tb)
            self.stage_B(l, tb)
            self.stage_C(l, tb)
            self.stage_D(l, tb)
            self.stage_M(l, tb)
            self.stage_resid(self.w["w_mix"][l], 16, 512, 4, self.MG, [S_.b("MG", tb)], self.XF, [S_.b("XF", tb)], tb, 2)
            self.stage_N(l, tb, V_L1G, V_L1B, self.X1F, self.X1B, "X1")
        self.stage_F1(l)
        for tb in range(2):
            self.stage_resid(self.w["w_dn"][l], 44, 128, 16, self.HH, [S_.b("HH", tb)], self.X1F, [S_.b("X1F", tb)], tb, 2)
            self.stage_N(l, tb, V_L2G, V_L2B, self.XF, self.XB, "X", final=last)


def build(n_layers, debug=(), stages=None):
    nc = bass.Bass("TRN2", target_bir_lowering=False)
    es = ExitStack()
    P = Prog(nc, es, n_layers, debug)
    P.setup()
    if stages is not None:
        P.load_vecs(0)
        for st in stages:
            if st == "none":
                continue
            name, tb = st[:-1], int(st[-1])
            if name in ("P", "F1"):
                getattr(P, "stage_" + name)(0)
            else:
                getattr(P, "stage_" + name)(0, tb)
    for l in range(n_layers if stages is None else 0):
        P.layer(l, last=(l == n_layers - 1))
    P.S.finish()
    P.S.emit()
    es.close()
    return nc, P


def make_inputs(inp, b, n_layers):
    cst, icnt = host_consts()
    m = {"xT": np.ascontiguousarray(inp["x"][b].T), "pos": np.ascontiguousarray(inp["positions"][b][None, :]).astype(np.int32),
         "cst": cst, "icnt": icnt}
    return m


def kernel(**inputs):
    inp = {k: np.asarray(v) for k, v in inputs.items()}
    nl = DEPTH
    nc, P = build(nl)
    per_layer = [host_layout(inp, l) for l in range(nl)]
    wts = {k: np.stack([per_layer[l][k] for l in range(nl)], 0) for k in W_SHAPES}
    in_maps = []
    for b in range(4):
        m = make_inputs(inp, b, nl)
        m.update(wts)
        in_maps.append(m)
    res = run_bass_kernel_spmd(nc, in_maps, core_ids=list(range(4)))
    out = np.stack([np.ascontiguousarray(res.results[b]["yT"].T) for b in range(4)], 0)
    return out.astype(np.float32)
```

```python
import math
import numpy as np
from contextlib import ExitStack
import concourse.bass as bass
import concourse.mybir as mybir
from concourse.bass_utils import run_bass_kernel_spmd

F32 = mybir.dt.float32
BF16 = mybir.dt.bfloat16
I32 = mybir.dt.int32
AF = mybir.ActivationFunctionType
ALU = mybir.AluOpType

D = 2048
S = 2048
TB = 1024
DEPTH = 4
FFN = 5632
ALPHA = (2 * DEPTH) ** 0.25
THETA = 500000.0
DIL_D = (1, 4, 16)
PSKIP = set()


class Buf:
    __slots__ = ("name", "lw", "rd_c", "rd_d")

    def __init__(self, name):
        self.name = name
        self.lw = None
        self.rd_c = {}
        self.rd_d = []


class Op:
    __slots__ = ("idx", "eng", "meth", "args", "kw", "deps", "dma", "inc", "sem", "val", "incv")


class Sched:
    def __init__(self, nc, es):
        self.nc = nc
        self.es = es
        self.ops = []
        self.bufs = {}
        self.eng = {"pe": nc.tensor, "act": nc.scalar, "dve": nc.vector, "pool": nc.gpsimd, "sp": nc.sync}
        self.out_dmas = []
        self.last_op = {}
        self.fence_deps = []

    def fence(self):
        self.fence_deps = list(self.last_op.values())

    def b(self, *key):
        bb = self.bufs.get(key)
        if bb is None:
            bb = self.bufs[key] = Buf(key)
        return bb

    def op(self, eng, meth, *args, R=(), W=(), dma=None, **kw):
        o = Op()
        o.idx = len(self.ops)
        o.eng = eng
        o.meth = meth
        o.args = args
        o.kw = kw
        o.dma = dma
        o.inc = False
        o.sem = None
        o.val = 0
        o.incv = 16 if dma is not None else 1
        deps = set()
        for b in R:
            if b.lw is not None:
                deps.add(b.lw)
        for b in W:
            if b.lw is not None:
                deps.add(b.lw)
            deps.update(b.rd_c.values())
            deps.update(b.rd_d)
        if self.fence_deps:
            for b in list(R) + list(W):
                if b.name[0].startswith(("bfa_", "fa_", "a2_", "d_")):
                    deps.update(self.fence_deps)
                    break
        keep = []
        for d in deps:
            p = self.ops[d]
            if p.dma is not None or dma is not None or p.eng != eng:
                keep.append(d)
        o.deps = sorted(keep)
        for b in R:
            if dma is not None:
                b.rd_d.append(o.idx)
            else:
                b.rd_c[eng] = o.idx
        for b in W:
            b.lw = o.idx
            b.rd_c = {}
            b.rd_d = []
        self.ops.append(o)
        if dma is None and meth is not None:
            self.last_op[eng] = o.idx
        return o

    def pe(self, meth, *a, **k):
        return self.op("pe", meth, *a, **k)

    def act(self, meth, *a, **k):
        return self.op("act", meth, *a, **k)

    def dve(self, meth, *a, **k):
        return self.op("dve", meth, *a, **k)

    def pool(self, meth, *a, **k):
        return self.op("pool", meth, *a, **k)

    def dma(self, q, key, out, in_, R=(), W=(), is_out=False, **kw):
        o = self.op(q, "dma_start", out=out, in_=in_, R=R, W=W, dma=key, **kw)
        if is_out:
            self.out_dmas.append(o.idx)
        return o

    def finish(self, eng="sp"):
        o = self.op(eng, None, dma="__fin__")
        o.deps = sorted(set(o.deps) | set(p.idx for p in self.ops if p.dma is not None and p.meth is not None))

    def emit(self):
        nc = self.nc
        for o in self.ops:
            for d in o.deps:
                self.ops[d].inc = True
        sems = {}
        cnt = {}
        for o in self.ops:
            if not o.inc:
                continue
            k = o.eng if o.dma is None else ("d", o.dma)
            if k not in sems:
                sems[k] = self.es.enter_context(nc.semaphore("s%d" % len(sems)))
                cnt[k] = 0
            o.sem = sems[k]
            cnt[k] += o.incv
            o.val = cnt[k]
        waited = {e: {} for e in self.eng}
        for o in self.ops:
            e = self.eng[o.eng]
            wd = waited[o.eng]
            need = {}
            for d in o.deps:
                p = self.ops[d]
                key = id(p.sem)
                if wd.get(key, 0) < p.val:
                    if key not in need or need[key][1] < p.val:
                        need[key] = (p.sem, p.val)
            for key, (sem, val) in need.items():
                e.wait_ge(sem, val)
                wd[key] = val
            if o.meth is None:
                continue
            inst = getattr(e, o.meth)(*o.args, **o.kw)
            if o.inc:
                inst.then_inc(o.sem, o.incv)
        self.n_sems = len(sems)


def tile_w(W, G):
    K, N = W.shape
    KC = K // 128
    return np.ascontiguousarray(W.reshape(KC, 128, N // G, G).transpose(2, 1, 0, 3)).reshape(N // G, 128, KC * G)


def col_vec(v):
    return np.ascontiguousarray(v.reshape(-1, 128).T)


V_BG = 0
V_GQ = 64
V_GKV = 68
V_PS = 72
V_CW = 80
V_CB = 328
V_CG = 336
V_CBB = 344
V_L1G = 352
V_L1B = 368
V_L2G = 384
V_L2B = 400
V_FW = 416
V_FB = 680
NV = 768

DIL_PERM = np.array(list(range(0, 16)) + list(range(32, 48)) + list(range(16, 32)) + list(range(48, 128)))


def host_layout(inp, l):
    w_in = inp["w_in"][l]
    o = {}
    c0 = 0
    cq = w_in[:, 0:512]
    ckv = w_in[:, 512:1024]
    kpe = w_in[:, 1024:1088]
    up = w_in[:, 1088:2112]
    uc = w_in[:, 2112:4160]
    ud = w_in[:, 4160:13376]
    ug = w_in[:, 13376:21568]
    o["w_p1"] = tile_w(np.concatenate([cq, ckv, up], 1), 512)
    o["w_kpe"] = tile_w(kpe, 64)
    a, b = uc[:, :1024], uc[:, 1024:]
    pr = np.stack([a.reshape(D, 8, 128), b.reshape(D, 8, 128)], 2).reshape(D, 2048)
    o["w_p2"] = tile_w(pr, 512)
    ud6 = ud.reshape(D, 3, 3, 8, 128)
    qk = ud6[:, :, 0:2][..., DIL_PERM].reshape(D, 6144)
    o["w_dqk"] = tile_w(qk, 512)
    o["w_dv"] = tile_w(np.ascontiguousarray(ud6[:, :, 2]).reshape(D, 3072), 512)
    o["w_gate"] = tile_w(ug, 256)
    pj = np.concatenate([inp["mla_w_proj"][l], inp["pool_w_proj"][l], inp["conv_w_proj"][l], inp["dil_w_proj"][l]], 1)
    o["w_proj"] = tile_w(pj, 256)
    uq = inp["mla_w_uq"][l].reshape(512, 8, 192)
    o["w_uqn"] = tile_w(np.ascontiguousarray(uq[:, :, :128]).reshape(512, 1024), 512)
    o["w_uqr"] = tile_w(np.ascontiguousarray(uq[:, :, 128:]).reshape(512, 512), 512)
    ukv = inp["mla_w_ukv"][l].reshape(512, 8, 256)
    o["w_ukn"] = tile_w(np.ascontiguousarray(ukv[:, :, :128]).reshape(512, 1024), 512)
    o["w_ukv"] = tile_w(np.ascontiguousarray(ukv[:, :, 128:]).reshape(512, 1024), 512)
    pw = inp["pool_w"][l].reshape(4, 2, 128, 256)
    o["w_pool"] = np.ascontiguousarray(pw.transpose(2, 0, 1, 3)).reshape(128, 2048)
    o["w_mix"] = tile_w(inp["mix_w_out"][l], 512)
    wu = inp["ffn_w_up"][l]
    pr = np.stack([wu[:, :FFN].reshape(D, 44, 128), wu[:, FFN:].reshape(D, 44, 128)], 2).reshape(D, 2 * FFN)
    o["w_up"] = tile_w(pr, 512)
    o["w_dn"] = tile_w(inp["ffn_w_down"][l], 128)
    v = np.zeros((128, NV), np.float32)
    v[:, V_BG:V_BG + 64] = col_vec(inp["b_gate"][l].reshape(-1))
    v[:, V_GQ:V_GQ + 4] = col_vec(inp["mla_gq"][l])
    v[:, V_GKV:V_GKV + 4] = col_vec(inp["mla_gkv"][l])
    v[:, V_PS:V_PS + 8] = col_vec(inp["pool_scale"][l])
    v[:, V_CW:V_CW + 248] = col_vec(inp["conv_dw"][l].reshape(-1))
    v[:, V_CB:V_CB + 8] = col_vec(inp["conv_dw_b"][l])
    v[:, V_CG:V_CG + 8] = col_vec(inp["conv_ln_g"][l])
    v[:, V_CBB:V_CBB + 8] = col_vec(inp["conv_ln_b"][l])
    v[:, V_L1G:V_L1G + 16] = col_vec(inp["ln1_g"][l])
    v[:, V_L1B:V_L1B + 16] = col_vec(inp["ln1_b"][l])
    v[:, V_L2G:V_L2G + 16] = col_vec(inp["ln2_g"][l])
    v[:, V_L2B:V_L2B + 16] = col_vec(inp["ln2_b"][l])
    v[:, V_FW:V_FW + 264] = col_vec(inp["ffn_dw"][l].reshape(-1))
    v[:, V_FB:V_FB + 88] = col_vec(inp["ffn_dw_b"][l])
    o["vecs"] = v
    return o


W_SHAPES = {"w_p1": [4, 128, 8192], "w_kpe": [1, 128, 1024], "w_p2": [4, 128, 8192], "w_dqk": [12, 128, 8192],
            "w_dv": [6, 128, 8192], "w_gate": [32, 128, 4096], "w_proj": [32, 128, 2048], "w_uqn": [2, 128, 2048],
            "w_uqr": [1, 128, 2048], "w_ukn": [2, 128, 2048], "w_ukv": [2, 128, 2048], "w_pool": [128, 2048],
            "w_mix": [4, 128, 8192], "w_up": [22, 128, 8192], "w_dn": [16, 128, 5632], "vecs": [128, NV]}


def host_consts():
    c = np.zeros((128, 2048 + 512 + 8), np.float32)
    i = np.arange(128)
    for kb in range(4):
        for j in range(4):
            blk = np.zeros((128, 128), np.float32) if j < kb else (
                (i[None, :] >= i[:, None]).astype(np.float32) if j == kb else np.ones((128, 128), np.float32))
            c[:, kb * 512 + j * 128: kb * 512 + (j + 1) * 128] = blk
    m = np.concatenate([(i[None, :] <= i[:, None]), (i[None, :] >= i[:, None])], 1).astype(np.float32)
    c[:, 2048:2304] = m
    c[:, 2304:2560] = m
    f64 = THETA ** (-np.arange(32, dtype=np.float32) * 2.0 / 64)
    f32_ = THETA ** (-np.arange(16, dtype=np.float32) * 2.0 / 32)
    c[0:32, 2560] = f64
    c[32:64, 2560] = f64
    c[0:16, 2561] = f32_
    c[32:48, 2561] = f32_
    c[0:32, 2562] = -1
    c[32:64, 2562] = 1
    c[0:16, 2563] = -1
    c[32:48, 2563] = 1
    ic = np.zeros((4, S), np.float32)
    t = np.arange(S)
    for g, w in enumerate((2, 4, 8, 16)):
        ic[g] = 1.0 / np.minimum(t + 1, w)
    return c, ic


class Prog:
    def __init__(self, nc, es, n_layers, debug=()):
        self.nc = nc
        self.es = es
        self.S = Sched(nc, es)
        self.nl = n_layers
        self.debug = debug
        S_ = self.S
        sb = self.sb
        self.ps = [es.enter_context(nc.psum_tensor("ps%d" % i, [128, 512], F32)) for i in range(8)]
        self.ps_i = 0
        self.wslots = [sb("wslot%d" % i, [128, 8192], BF16) for i in range(2)]
        self.ws_i = 0
        self.BFA = sb("BFA", [128, 32768], BF16)
        self.FA = sb("FA", [128, 16384], F32)
        self.vecs = sb("vecs_sb", [128, NV], F32)
        self.cst_f = sb("cst_f", [128, 8], F32)
        self.cst_b = sb("cst_b", [128, 2560], BF16)
        self.ones_b = sb("ones_b", [128, 128], BF16)
        self.ones_f = sb("ones_f", [128, 128], F32)
        self.stg_f = [sb("stg_f%d" % i, [128, 1024], F32) for i in range(2)]
        self.stg_b = [sb("stg_b%d" % i, [128, 1024], BF16) for i in range(2)]
        self.tmp_f = [sb("tmp_f%d" % i, [128, 1024], F32) for i in range(4)]
        self.pt = [sb("pt%d" % i, [128, 512], BF16) for i in range(3)]
        self.halo = sb("halo", [128, 88, 2], F32)
        self.posi = sb("posi", [64, TB], I32)
        self.cnt = {}
        dr = self.dram
        self.xT_in = nc.dram_tensor("xT", [D, S], F32, kind="ExternalInput").ap()
        self.pos_in = nc.dram_tensor("pos", [1, S], I32, kind="ExternalInput").ap()
        self.cst_in = nc.dram_tensor("cst", [128, 2568], F32, kind="ExternalInput").ap()
        self.icnt_in = nc.dram_tensor("icnt", [4, S], F32, kind="ExternalInput").ap()
        self.w = {k: nc.dram_tensor(k, [n_layers] + shp, F32, kind="ExternalInput").ap() for k, shp in W_SHAPES.items()}
        self.out = nc.dram_tensor("yT", [D, S], F32, kind="ExternalOutput").ap()
        self.XF = dr("XF", [D, S], F32)
        self.XB = dr("XB", [D, S], BF16)
        self.X1F = dr("X1F", [D, S], F32)
        self.X1B = dr("X1B", [D, S], BF16)
        self.CQ = dr("CQ", [1024, S], F32)
        self.UP = dr("UP", [1024, S], F32)
        self.GL = dr("GL", [1024, S], F32)
        self.DQ = dr("DQ", [3, 1024, S], BF16)
        self.DK = dr("DK", [3, 1024, S], BF16)
        self.DV = dr("DV", [3, S, 1024], BF16)
        self.MQ = dr("MQ", [8, 192, S], BF16)
        self.MK = dr("MK", [8, 128, S], BF16)
        self.MKPE = dr("MKPE", [64, S], BF16)
        self.MV = dr("MV", [S, 1024], BF16)
        self.ACTS = dr("ACTS", [4, 1024, S], BF16)
        self.MG = dr("MG", [D, S], BF16)
        self.Z = dr("Z", [D, S], F32)
        self.HH = dr("HH", [FFN, S], BF16)
        self.ROPE = dr("ROPE", [4, 64, S], F32)
        self.q_i = 0

    def sb(self, name, shape, dt):
        return self.es.enter_context(self.nc.sbuf_tensor(name, shape, dt))

    def dram(self, name, shape, dt):
        kind = "ExternalOutput" if name in self.debug else "Internal"
        return self.nc.dram_tensor(name, shape, dt, kind=kind).ap()

    def next_ps(self, n):
        if self.ps_i + n > 8:
            self.ps_i = 0
        r = list(range(self.ps_i, self.ps_i + n))
        self.ps_i = (self.ps_i + n) % 8
        return r

    def acc_banks(self):
        i = self.rot("accb", 2)
        return (0, 1) if i == 0 else (2, 3)

    def s_bank(self):
        return 4 + self.rot("sbank", 4)

    def rot(self, name, n):
        i = self.cnt.get(name, 0)
        self.cnt[name] = (i + 1) % n
        return i

    def q(self):
        return "sp"

    def ld(self, key, out, in_, W, R=(), q=None):
        return self.S.dma(q or "sp", key, out, in_, R=list(R), W=list(W))

    def st(self, key, out, in_, R, W=(), q=None, is_out=False):
        return self.S.dma(q or "sp", key, out, in_, R=list(R), W=list(W), is_out=is_out)

    def linear_fm(self, w_dram, KC, G, groups, xT, xbufs, evac, mw=128, per_evac=1, nch=2, blocks=None):
        S_ = self.S
        tpg = G // mw
        loaded = {}
        groups = list(groups)
        blks = blocks if blocks is not None else [(None, xT, xbufs)]

        def load(i):
            wi = self.rot("ws", 2)
            S_.dma("pool", "wslot%d" % wi, self.wslots[wi][:, 0:KC * G], w_dram[groups[i]], W=[S_.b("wslot", wi)])
            loaded[i] = wi

        load(0)
        for i, g in enumerate(groups):
            if i + 1 < len(groups):
                load(i + 1)
            wi = loaded.pop(i)
            wv = self.wslots[wi][:, 0:KC * G].rearrange("p (k c) -> p k c", c=G)
            for mt0 in range(0, tpg, per_evac):
                for (tb, xv_, xb_) in blks:
                    pm, pp = [], []
                    for mt in range(mt0, mt0 + per_evac):
                        banks = self.next_ps(nch)
                        for kc in range(KC):
                            for c in range(nch):
                                S_.pe("matmul", self.ps[banks[c]][0:mw, :], lhsT=wv[:, kc, mt * mw:(mt + 1) * mw],
                                      rhs=(xv_(kc, c) if callable(xv_) else xv_[:, kc, c * 512:(c + 1) * 512]), start=(kc == 0), stop=(kc == KC - 1),
                                      R=[S_.b("wslot", wi)] + list(xb_), W=[S_.b("ps", banks[c])])
                        pm.append((g, mt))
                        pp.append(banks)
                    if blocks is None:
                        evac(pm, pp)
                    else:
                        evac(pm, pp, tb)

    def linear_tm(self, w_dram, KC, groups, xT, xbufs, evac, blocks=None):
        S_ = self.S
        G = 512
        blks = blocks if blocks is not None else [(None, xT, xbufs)]
        for g in groups:
            wi = self.rot("ws", 2)
            S_.dma("pool", "wslot%d" % wi, self.wslots[wi][:, 0:KC * G], w_dram[g], W=[S_.b("wslot", wi)])
            wv = self.wslots[wi][:, 0:KC * G].rearrange("p (k c) -> p k c", c=G)
            for (tb, xv_, xb_) in blks:
                for tt in range(8):
                    bk = self.next_ps(1)[0]
                    for kc in range(KC):
                        S_.pe("matmul", self.ps[bk][:, :], lhsT=xv_[:, kc, tt * 128:(tt + 1) * 128], rhs=wv[:, kc, :],
                              start=(kc == 0), stop=(kc == KC - 1),
                              R=[S_.b("wslot", wi)] + list(xb_), W=[S_.b("ps", bk)])
                    if blocks is None:
                        evac(g, tt, bk)
                    else:
                        evac(g, tt, bk, tb)

    def evac_copy(self, dst, dbuf, bk, rows=128, eng=None):
        S_ = self.S
        e = eng or ("act" if self.rot("ev", 2) == 0 else "dve")
        if e == "act":
            S_.act("copy", out=dst, in_=self.ps[bk][0:rows, :], R=[S_.b("ps", bk)], W=[dbuf])
        else:
            S_.dve("tensor_copy", out=dst, in_=self.ps[bk][0:rows, :], R=[S_.b("ps", bk)], W=[dbuf])

    def load_xT(self, src, t0, dep=(), slot=0):
        S_ = self.S
        v = self.BFA[:, slot * 16384:(slot + 1) * 16384].rearrange("p (k t) -> p k t", t=TB)
        for hlf in range(2):
            self.ld("bfa_x%d_%d" % (slot, hlf), v[:, hlf * 8:(hlf + 1) * 8, :],
                    src[hlf * 1024:(hlf + 1) * 1024, t0:t0 + TB].rearrange("(k p) t -> p k t", p=128),
                    R=list(dep), W=[S_.b("bfa_x", slot, hlf)])
        return v, [S_.b("bfa_x", slot, 0), S_.b("bfa_x", slot, 1)]

    def setup(self):
        S_ = self.S
        self.ld("cst_f", self.cst_f[:, :], self.cst_in[:, 2560:2568], W=[S_.b("cst_f")])
        S_.dma("pool", "cst_b", self.cst_b[:, :], self.cst_in[:, 0:2560], W=[S_.b("cst_b")])
        S_.dve("memset", self.ones_f[:, :], 1.0, W=[S_.b("ones_f")])
        S_.dve("memset", self.ones_b[:, :], 1.0, W=[S_.b("ones_b")])
        S_.dve("memset", self.halo[:, :, :], 0.0, W=[S_.b("halo", ct) for ct in range(88)])
        for tb in range(2):
            t0 = tb * TB
            pi = self.posi
            src = self.pos_in[0:1, t0:t0 + TB]
            bsrc = bass.AP(src.tensor, src.offset, [[0, 64], [1, TB]])
            self.ld("posi", pi[:, :], bsrc, W=[S_.b("posi")])
            pf = self.tmp_f[0][0:64, :]
            S_.dve("tensor_copy", out=pf, in_=pi[:, :], R=[S_.b("posi")], W=[S_.b("tmp_f", 0)])
            for which in range(2):
                ang = self.tmp_f[1][0:64, :]
                S_.dve("tensor_scalar", out=ang, in0=pf, scalar1=self.cst_f[0:64, which:which + 1], scalar2=None,
                       op0=ALU.mult, R=[S_.b("tmp_f", 0), S_.b("cst_f")], W=[S_.b("tmp_f", 1)])
                for cs in range(2):
                    red = self.tmp_f[2][0:64, :]
                    S_.dve("tensor_scalar", out=red, in0=ang, scalar1=1.0 / (2 * math.pi), scalar2=(0.25 if cs == 0 else 0.0),
                           op0=ALU.mult, op1=ALU.add, R=[S_.b("tmp_f", 1)], W=[S_.b("tmp_f", 2)])
                    S_.dve("tensor_copy", out=self.posi[:, :], in_=red, R=[S_.b("tmp_f", 2)], W=[S_.b("posi")])
                    tab = self.tmp_f[3][0:64, :]
                    S_.dve("tensor_copy", out=tab, in_=self.posi[:, :], R=[S_.b("posi")], W=[S_.b("tmp_f", 3)])
                    S_.dve("tensor_tensor", out=red, in0=red, in1=tab, op=ALU.subtract, R=[S_.b("tmp_f", 2), S_.b("tmp_f", 3)],
                           W=[S_.b("tmp_f", 2)])
                    S_.dve("scalar_tensor_tensor", out=red, in0=red, scalar=0.5, in1=red, op0=ALU.is_ge, op1=ALU.subtract,
                           R=[S_.b("tmp_f", 2)], W=[S_.b("tmp_f", 2)])
                    S_.act("activation", out=tab, in_=red, func=AF.Sin, scale=-2 * math.pi, R=[S_.b("tmp_f", 2)], W=[S_.b("tmp_f", 3)])
                    if cs == 1:
                        S_.dve("tensor_scalar", out=tab, in0=tab, scalar1=self.cst_f[0:64, 2 + which:3 + which], scalar2=None,
                               op0=ALU.mult, R=[S_.b("tmp_f", 3), S_.b("cst_f")], W=[S_.b("tmp_f", 3)])
                    self.st("tmp_f3", self.ROPE[which * 2 + cs, :, t0:t0 + TB], tab, R=[S_.b("tmp_f", 3)],
                            W=[S_.b("ROPE", tb)])
        for kc in range(16):
            i = self.rot("stg_f", 2)
            for hh in range(2):
                self.ld("stg_f%d" % i, self.stg_f[i][:, :], self.xT_in[kc * 128:(kc + 1) * 128, hh * TB:(hh + 1) * TB],
                        W=[S_.b("stg_f", i)], q=self.q())
                self.st("stg_f%d" % i, self.XF[kc * 128:(kc + 1) * 128, hh * TB:(hh + 1) * TB], self.stg_f[i][:, :],
                        R=[S_.b("stg_f", i)], W=[S_.b("XF", hh)], q=self.q())
                j = self.rot("stg_b", 2)
                S_.dve("tensor_copy", out=self.stg_b[j][:, :], in_=self.stg_f[i][:, :], R=[S_.b("stg_f", i)], W=[S_.b("stg_b", j)])
                self.st("stg_b%d" % j, self.XB[kc * 128:(kc + 1) * 128, hh * TB:(hh + 1) * TB], self.stg_b[j][:, :],
                        R=[S_.b("stg_b", j)], W=[S_.b("XB", hh)], q=self.q())
                i = self.rot("stg_f", 2)

    def load_vecs(self, l):
        S_ = self.S
        self.ld("vecs", self.vecs[:, :], self.w["vecs"][l], W=[S_.b("vecs")])

    def vcol(self, c, rows=128):
        return self.vecs[0:rows, c:c + 1]

    def load_rope(self, which, tb, dst_c, dst_s, bufs):
        S_ = self.S
        t0 = tb * TB
        self.ld("k_" + "_".join(map(str, bufs[0].name)), dst_c, self.ROPE[which * 2, :, t0:t0 + TB], W=[bufs[0]], R=[S_.b("ROPE", tb)])
        self.ld("k_" + "_".join(map(str, bufs[1].name)), dst_s, self.ROPE[which * 2 + 1, :, t0:t0 + TB], W=[bufs[1]], R=[S_.b("ROPE", tb)])

    def rope_evac(self, bk, c, rows, half, cosT, sinT, tbufs, dst, dbuf):
        S_ = self.S
        P = self.ps[bk]
        cs = slice(c * 512, (c + 1) * 512)
        t2 = self.tmp_f[2][0:rows, 0:512]
        t1 = self.tmp_f[3][0:rows, 0:512]
        R0 = [S_.b("ps", bk)] + list(tbufs)
        S_.dve("tensor_tensor", out=t2, in0=P[0:rows, :], in1=cosT[0:rows, cs], op=ALU.mult, R=R0, W=[S_.b("tmp_f", 2)])
        if rows == 48:
            S_.dve("memset", self.tmp_f[3][0:48, 0:512], 0.0, W=[S_.b("tmp_f", 3)])
        S_.dve("tensor_tensor", out=self.tmp_f[3][0:half, 0:512], in0=P[32:32 + half, :], in1=sinT[0:half, cs], op=ALU.mult,
               R=R0, W=[S_.b("tmp_f", 3)])
        S_.dve("tensor_tensor", out=self.tmp_f[3][32:32 + half, 0:512], in0=P[0:half, :], in1=sinT[32:32 + half, cs],
               op=ALU.mult, R=R0, W=[S_.b("tmp_f", 3)])
        S_.dve("tensor_tensor", out=dst, in0=t2, in1=t1, op=ALU.add, R=[S_.b("tmp_f", 2), S_.b("tmp_f", 3)], W=[dbuf])

    def stage_P(self, l):
        S_ = self.S
        S_.fence()
        blocks = []
        for tb in range(2):
            xv, xb = self.load_xT(self.XB, tb * TB, [S_.b("XB", tb)], slot=tb)
            blocks.append((tb, xv, xb))

        def tab(which, cs, tb):
            o = (which * 2 + cs) * 2048 + tb * 1024
            return self.FA[0:64, o:o + 1024]

        rbuf = {}
        for which in range(2):
            for tb in range(2):
                bufs = [S_.b("fa_r", which * 2, tb), S_.b("fa_r", which * 2 + 1, tb)]
                self.load_rope(which, tb, tab(which, 0, tb), tab(which, 1, tb), bufs)
                rbuf[(which, tb)] = bufs

        def ev1(pm, pp, tb):
            t0 = tb * TB
            (g, mt), banks = pm[0], pp[0]
            i = self.rot("stg_f", 2)
            for c in range(2):
                self.evac_copy(self.stg_f[i][:, c * 512:(c + 1) * 512], S_.b("stg_f", i), banks[c])
            dst = self.CQ if g < 2 else self.UP
            row = (g % 2) * 512 + mt * 128
            self.st("stg_f%d" % i, dst[row:row + 128, t0:t0 + TB], self.stg_f[i][:, :], R=[S_.b("stg_f", i)],
                    W=[S_.b("CQ" if g < 2 else "UP", tb)])

        self.linear_fm(self.w["w_p1"][l], 16, 512, range(4), None, None, ev1, blocks=blocks)

        def evk(pm, pp, tb):
            t0 = tb * TB
            banks = pp[0]
            i = self.rot("stg_b", 2)
            for c in range(2):
                self.rope_evac(banks[c], c, 64, 32, tab(0, 0, tb), tab(0, 1, tb), rbuf[(0, tb)],
                               self.stg_b[i][0:64, c * 512:(c + 1) * 512], S_.b("stg_b", i))
            self.st("stg_b%d" % i, self.MKPE[:, t0:t0 + TB], self.stg_b[i][0:64, :], R=[S_.b("stg_b", i)], W=[S_.b("MKPE", tb)])

        self.linear_fm(self.w["w_kpe"][l], 16, 64, range(1), None, None, evk, mw=64, blocks=blocks)

        def ev2(pm, pp, tb):
            t0 = tb * TB
            (g, mt), ba = pm[0], pp[0]
            bb = pp[1]
            m = g * 2 + mt // 2
            i = self.rot("stg_f", 2)
            for c in range(2):
                sg = self.tmp_f[0][:, c * 512:(c + 1) * 512]
                S_.act("activation", out=sg, in_=self.ps[bb[c]][:, :], func=AF.Sigmoid, R=[S_.b("ps", bb[c])], W=[S_.b("tmp_f", 0)])
                S_.dve("tensor_tensor", out=self.stg_f[i][:, c * 512:(c + 1) * 512], in0=self.ps[ba[c]][:, :], in1=sg, op=ALU.mult,
                       R=[S_.b("ps", ba[c]), S_.b("tmp_f", 0)], W=[S_.b("stg_f", i)])
            self.st("stg_f%d" % i, self.GL[m * 128:(m + 1) * 128, t0:t0 + TB], self.stg_f[i][:, :], R=[S_.b("stg_f", i)],
                    W=[S_.b("GL", tb)])

        self.linear_fm(self.w["w_p2"][l], 16, 512, range(4), None, None, ev2, per_evac=2, blocks=blocks)

        def ev3(pm, pp, tb):
            t0 = tb * TB
            (g, mt), banks = pm[0], pp[0]
            tix = g * 4 + mt
            grp, which, h = tix // 16, (tix // 8) % 2, tix % 8
            i = self.rot("stg_b", 2)
            sb_ = self.stg_b[i]
            for c in range(2):
                P = self.ps[banks[c]]
                cs = slice(c * 512, (c + 1) * 512)
                S_.dve("tensor_copy", out=sb_[64:128, cs], in_=P[64:128, :], R=[S_.b("ps", banks[c])], W=[S_.b("stg_b", i)])
                self.rope_evac(banks[c], c, 64, 32, tab(1, 0, tb), tab(1, 1, tb), rbuf[(1, tb)], sb_[0:64, cs], S_.b("stg_b", i))
            dst = (self.DQ if which == 0 else self.DK)[grp, h * 128:(h + 1) * 128, t0:t0 + TB]
            self.st("stg_b%d" % i, dst, sb_[:, :], R=[S_.b("stg_b", i)], W=[S_.b("DQK", tb)])

        self.linear_fm(self.w["w_dqk"][l], 16, 512, range(12), None, None, ev3, blocks=blocks)

        def ev4(g, tt, bk, tb):
            t0 = tb * TB
            grp, hf = g // 2, g % 2
            i = self.rot("pt", 3)
            self.evac_copy(self.pt[i][:, :], S_.b("pt", i), bk)
            self.st("pt%d" % i, self.DV[grp, t0 + tt * 128:t0 + (tt + 1) * 128, hf * 512:(hf + 1) * 512], self.pt[i][:, :],
                    R=[S_.b("pt", i)], W=[S_.b("DV", tb)])

        self.linear_tm(self.w["w_dv"][l], 16, range(6), None, None, ev4, blocks=blocks)

    def colsum(self, bk, srcs, sbufs):
        S_ = self.S
        n = len(srcs)
        for i, (s, b) in enumerate(zip(srcs, sbufs)):
            S_.pe("matmul", self.ps[bk][:, :], lhsT=self.ones_f[:, :], rhs=s, start=(i == 0), stop=(i == n - 1),
                  R=[S_.b("ones_f"), b], W=[S_.b("ps", bk)])

    def stage_A1(self, l, tb):
        S_ = self.S
        S_.fence()
        t0 = tb * TB
        cq = self.FA[:, 0:8192].rearrange("p (k t) -> p k t", t=TB)
        for hf in range(2):
            self.ld("fa_cq%d" % hf, cq[:, hf * 4:(hf + 1) * 4, :],
                    self.CQ[hf * 512:(hf + 1) * 512, t0:t0 + TB].rearrange("(k p) t -> p k t", p=128),
                    R=[S_.b("CQ", tb)], W=[S_.b("fa_cq", hf)], q=self.q())
        cosM, sinM = self.FA[0:64, 8192:9216], self.FA[0:64, 9216:10240]
        rb = [S_.b("fa_r", 0), S_.b("fa_r", 1)]
        self.load_rope(0, tb, cosM, sinM, rb)
        xn = self.BFA[:, 0:8192].rearrange("p (k t) -> p k t", t=TB)
        for hf in range(2):
            for c in range(2):
                bk = self.next_ps(1)[0]
                for k in range(4):
                    sq = self.tmp_f[k][:, 0:512]
                    S_.act("activation", out=sq, in_=cq[:, hf * 4 + k, c * 512:(c + 1) * 512], func=AF.Square,
                           R=[S_.b("fa_cq", hf)], W=[S_.b("tmp_f", k)])
                self.colsum(bk, [self.tmp_f[k][:, 0:512] for k in range(4)], [S_.b("tmp_f", k) for k in range(4)])
                rs = self.stg_f[0][:, c * 512:(c + 1) * 512]
                S_.dve("tensor_scalar", out=rs, in0=self.ps[bk][:, :], scalar1=1.0 / 512, scalar2=1e-6, op0=ALU.mult, op1=ALU.add,
                       R=[S_.b("ps", bk)], W=[S_.b("stg_f", 0)])
                S_.act("activation", out=rs, in_=rs, func=AF.Sqrt, R=[S_.b("stg_f", 0)], W=[S_.b("stg_f", 0)])
                S_.dve("reciprocal", out=rs, in_=rs, R=[S_.b("stg_f", 0)], W=[S_.b("stg_f", 0)])
                for k in range(4):
                    gc = self.vcol((V_GQ if hf == 0 else V_GKV) + k)
                    S_.dve("scalar_tensor_tensor", out=xn[:, hf * 4 + k, c * 512:(c + 1) * 512],
                           in0=cq[:, hf * 4 + k, c * 512:(c + 1) * 512], scalar=gc, in1=rs, op0=ALU.mult, op1=ALU.mult,
                           R=[S_.b("fa_cq", hf), S_.b("vecs"), S_.b("stg_f", 0)], W=[S_.b("bfa_xn", hf)])
        xq, xkv = xn[:, 0:4, :], xn[:, 4:8, :]

        def ev_plain(dst_of):
            def ev(pm, pp):
                (g, mt), banks = pm[0], pp[0]
                h = g * 4 + mt
                i = self.rot("stg_b", 2)
                for c in range(2):
                    self.evac_copy(self.stg_b[i][:, c * 512:(c + 1) * 512], S_.b("stg_b", i), banks[c])
                self.st("stg_b%d" % i, dst_of(h), self.stg_b[i][:, :], R=[S_.b("stg_b", i)], W=[S_.b("MQK", tb)], q=self.q())
            return ev

        self.linear_fm(self.w["w_uqn"][l], 4, 512, range(2), xq, [S_.b("bfa_xn", 0)],
                       ev_plain(lambda h: self.MQ[h, 0:128, t0:t0 + TB]))

        def ev_qr(pm, pp):
            (g, mt), banks = pm[0], pp[0]
            h = mt
            i = self.rot("stg_b", 2)
            for c in range(2):
                self.rope_evac(banks[c], c, 64, 32, cosM, sinM, rb, self.stg_b[i][0:64, c * 512:(c + 1) * 512], S_.b("stg_b", i))
            self.st("stg_b%d" % i, self.MQ[h, 128:192, t0:t0 + TB], self.stg_b[i][0:64, :], R=[S_.b("stg_b", i)],
                    W=[S_.b("MQK", tb)], q=self.q())

        self.linear_fm(self.w["w_uqr"][l], 4, 512, range(1), xq, [S_.b("bfa_xn", 0)], ev_qr, mw=64)
        self.linear_fm(self.w["w_ukn"][l], 4, 512, range(2), xkv, [S_.b("bfa_xn", 1)],
                       ev_plain(lambda h: self.MK[h, :, t0:t0 + TB]))

        def ev_v(g, tt, bk):
            i = self.rot("pt", 3)
            self.evac_copy(self.pt[i][:, :], S_.b("pt", i), bk)
            self.st("pt%d" % i, self.MV[t0 + tt * 128:t0 + (tt + 1) * 128, g * 512:(g + 1) * 512], self.pt[i][:, :],
                    R=[S_.b("pt", i)], W=[S_.b("MV", tb)], q=self.q())

        self.linear_tm(self.w["w_ukv"][l], 4, range(2), xkv, [S_.b("bfa_xn", 1)], ev_v)

    def stage_A2(self, l, tb):
        S_ = self.S
        S_.fence()
        t0 = tb * TB
        nk = t0 + TB
        nkb = nk // 128
        scale = 192 ** -0.5
        kpe = self.BFA[0:64, 0:2048]
        self.ld("bfa_kpe", kpe[:, 0:nk], self.MKPE[:, 0:nk], R=[S_.b("MKPE", 0), S_.b("MKPE", 1)], W=[S_.b("bfa_kpe")])
        masks = self.cst_b[:, 0:2048]
        for h in range(8):
            s = h % 2
            base = 2048 + s * 8192
            qn = self.BFA[:, base:base + 1024]
            qr = self.BFA[0:64, base + 1024:base + 2048]
            kn = self.BFA[:, base + 2048:base + 4096]
            vv = self.BFA[:, base + 4096:base + 6144].rearrange("p (k d) -> p k d", d=128)
            bq, bk_, bv = S_.b("a2_q", s), S_.b("a2_k", s), S_.b("a2_v", s)
            dep = [S_.b("MQK", 0), S_.b("MQK", 1)]
            self.ld("a2_qn%d" % s, qn, self.MQ[h, 0:128, t0:t0 + TB], R=dep, W=[bq])
            self.ld("a2_qr%d" % s, qr, self.MQ[h, 128:192, t0:t0 + TB], R=dep, W=[bq])
            self.ld("a2_k%d" % s, kn[:, 0:nk], self.MK[h, :, 0:nk], R=dep, W=[bk_])
            self.ld("a2_v%d" % s, vv[:, 0:nkb, :], self.MV[0:nk, h * 128:(h + 1) * 128].rearrange("(k p) d -> p k d", p=128),
                    R=[S_.b("MV", 0), S_.b("MV", 1)], W=[bv])
            for qc in range(2):
                qb0 = t0 // 128 + qc * 4
                nkeys = qb0 + 4
                ob, zb = self.acc_banks()
                for kb in range(nkeys):
                    sb_ = self.s_bank()
                    qs = slice(qc * 512, (qc + 1) * 512)
                    S_.pe("matmul", self.ps[sb_][:, :], lhsT=kn[:, kb * 128:(kb + 1) * 128], rhs=qn[:, qs], start=True, stop=False,
                          R=[bk_, bq], W=[S_.b("ps", sb_)])
                    S_.pe("matmul", self.ps[sb_][:, :], lhsT=kpe[:, kb * 128:(kb + 1) * 128], rhs=qr[:, qs], start=False, stop=True,
                          R=[S_.b("bfa_kpe"), bq], W=[S_.b("ps", sb_)])
                    pi = self.rot("pt", 3)
                    P = self.pt[pi]
                    S_.act("activation", out=P[:, :], in_=self.ps[sb_][:, :], func=AF.Exp, scale=scale,
                           R=[S_.b("ps", sb_)], W=[S_.b("pt", pi)])
                    if kb >= qb0:
                        i = kb - qb0
                        S_.dve("tensor_tensor", out=P[:, :], in0=P[:, :], in1=masks[:, i * 512:(i + 1) * 512], op=ALU.mult,
                               R=[S_.b("pt", pi), S_.b("cst_b")], W=[S_.b("pt", pi)])
                    S_.pe("matmul", self.ps[ob][:, :], lhsT=vv[:, kb, :], rhs=P[:, :], start=(kb == 0), stop=(kb == nkeys - 1),
                          R=[bv, S_.b("pt", pi)], W=[S_.b("ps", ob)])
                    S_.pe("matmul", self.ps[zb][:, :], lhsT=self.ones_b[:, :], rhs=P[:, :], start=(kb == 0), stop=(kb == nkeys - 1),
                          R=[S_.b("ones_b"), S_.b("pt", pi)], W=[S_.b("ps", zb)])
                rc = self.tmp_f[0][:, 0:512]
                S_.dve("reciprocal", out=rc, in_=self.ps[zb][:, :], R=[S_.b("ps", zb)], W=[S_.b("tmp_f", 0)])
                i = self.rot("stg_b", 2)
                S_.dve("tensor_tensor", out=self.stg_b[i][:, 0:512], in0=self.ps[ob][:, :], in1=rc, op=ALU.mult,
                       R=[S_.b("ps", ob), S_.b("tmp_f", 0)], W=[S_.b("stg_b", i)])
                self.st("stg_b%d" % i, self.ACTS[0, h * 128:(h + 1) * 128, t0 + qc * 512:t0 + (qc + 1) * 512], self.stg_b[i][:, 0:512],
                        R=[S_.b("stg_b", i)], W=[S_.b("ACTS", 0, tb)], q=self.q())

    def stage_B(self, l, tb):
        S_ = self.S
        S_.fence()
        t0 = tb * TB
        HL = 16
        up = self.FA[:, 0:8 * 1040].rearrange("p (k t) -> p k t", t=1040)
        ic = self.FA[:, 8320:8320 + 4096].rearrange("p (g t) -> p g t", t=TB)
        src = self.icnt_in[0:4, t0:t0 + TB]
        self.ld("fa_ic", ic, bass.AP(src.tensor, src.offset, [[0, 128], [S, 4], [1, TB]]), W=[S_.b("fa_ic")])
        pw = self.wslots[1][:, 0:2048].rearrange("p (g o) -> p g o", o=256)
        S_.dma("pool", "wslot1", self.wslots[1][:, 0:2048], self.w["w_pool"][l], W=[S_.b("wslot", 1)])
        dT = self.BFA[:, 0:8192].rearrange("p (k t) -> p k t", t=TB)
        if tb == 0:
            S_.dve("memset", up[:, :, 0:HL], 0.0, W=[S_.b("fa_up")])
            self.ld("fa_up", up[:, :, HL:HL + TB], self.UP[:, 0:TB].rearrange("(k p) t -> p k t", p=128),
                    R=[S_.b("UP", 0)], W=[S_.b("fa_up")])
        else:
            self.ld("fa_up", up[:, :, :], self.UP[:, t0 - HL:t0 + TB].rearrange("(k p) t -> p k t", p=128),
                    R=[S_.b("UP", 0), S_.b("UP", 1)], W=[S_.b("fa_up")])
        for ct in range(8):
            g = ct // 2
            cur = up[:, ct, :]
            cb = S_.b("fa_up")
            lo = 0
            step = 1
            n = 0
            while step < (2 << g):
                ti = n % 2
                nxt = self.tmp_f[ti][:, :]
                nv = self.FA[:, 12416 + ti * 1040:12416 + (ti + 1) * 1040]
                S_.dve("tensor_tensor", out=nv[:, step:1040], in0=cur[:, step:1040], in1=cur[:, 0:1040 - step], op=ALU.add,
                       R=[cb], W=[S_.b("fa_pp", ti)])
                cur = nv
                cb = S_.b("fa_pp", ti)
                step *= 2
                n += 1
            mean = self.tmp_f[2][:, :]
            S_.dve("tensor_tensor", out=mean, in0=cur[:, HL:HL + TB], in1=ic[:, g, :], op=ALU.mult, R=[cb, S_.b("fa_ic")],
                   W=[S_.b("tmp_f", 2)])
            S_.dve("tensor_tensor", out=dT[:, ct, :], in0=mean, in1=up[:, ct, HL:HL + TB], op=ALU.subtract,
                   R=[S_.b("tmp_f", 2), S_.b("fa_up")], W=[S_.b("bfa_d")])
        for ct in range(8):
            g, hf = ct // 2, ct % 2
            banks = self.next_ps(2)
            for kc in range(2):
                for c in range(2):
                    S_.pe("matmul", self.ps[banks[c]][:, :], lhsT=pw[:, g * 2 + kc, hf * 128:(hf + 1) * 128],
                          rhs=dT[:, g * 2 + kc, c * 512:(c + 1) * 512], start=(kc == 0), stop=(kc == 1),
                          R=[S_.b("wslot", 1), S_.b("bfa_d")], W=[S_.b("ps", banks[c])])
            i = self.rot("stg_b", 2)
            for c in range(2):
                S_.act("activation", out=self.stg_b[i][:, c * 512:(c + 1) * 512], in_=self.ps[banks[c]][:, :], func=AF.Copy,
                       scale=self.vcol(V_PS + ct), R=[S_.b("ps", banks[c]), S_.b("vecs")], W=[S_.b("stg_b", i)])
            self.st("stg_b%d" % i, self.ACTS[1, ct * 128:(ct + 1) * 128, t0:t0 + TB], self.stg_b[i][:, :], R=[S_.b("stg_b", i)],
                    W=[S_.b("ACTS", 1, tb)], q=self.q())

    def ln_stats(self, tiles, tbufs, nfeat, eps, mean, rstd, mbuf):
        S_ = self.S
        b1, b2 = self.next_ps(2)
        self.colsum(b1, tiles, tbufs)
        n = len(tiles)
        for i, (s, b) in enumerate(zip(tiles, tbufs)):
            k = self.rot("sq", 2)
            sq = self.tmp_f[k][:, 0:512]
            S_.act("activation", out=sq, in_=s, func=AF.Square, R=[b], W=[S_.b("tmp_f", k)])
            S_.pe("matmul", self.ps[b2][:, :], lhsT=self.ones_f[:, :], rhs=sq, start=(i == 0), stop=(i == n - 1),
                  R=[S_.b("ones_f"), S_.b("tmp_f", k)], W=[S_.b("ps", b2)])
        S_.dve("tensor_scalar", out=mean, in0=self.ps[b1][:, :], scalar1=1.0 / nfeat, scalar2=None, op0=ALU.mult,
               R=[S_.b("ps", b1)], W=[mbuf])
        m2 = self.tmp_f[2][:, 512:1024]
        S_.dve("tensor_tensor", out=m2, in0=mean, in1=mean, op=ALU.mult, R=[mbuf], W=[S_.b("tmp_f", 2)])
        S_.dve("scalar_tensor_tensor", out=rstd, in0=self.ps[b2][:, :], scalar=1.0 / nfeat, in1=m2, op0=ALU.mult, op1=ALU.subtract,
               R=[S_.b("ps", b2), S_.b("tmp_f", 2)], W=[mbuf])
        S_.dve("tensor_scalar", out=rstd, in0=rstd, scalar1=eps, scalar2=None, op0=ALU.add, R=[mbuf], W=[mbuf])
        S_.act("activation", out=rstd, in_=rstd, func=AF.Sqrt, R=[mbuf], W=[mbuf])
        S_.dve("reciprocal", out=rstd, in_=rstd, R=[mbuf], W=[mbuf])

    def stage_C(self, l, tb):
        S_ = self.S
        S_.fence()
        t0 = tb * TB
        HL = 30
        acc = self.FA[:, 0:8192].rearrange("p (k t) -> p k t", t=TB)
        for ct in range(8):
            i = ct % 2
            gl = self.FA[:, 8192 + i * 1054:8192 + (i + 1) * 1054]
            gb = S_.b("fa_gl", i)
            if tb == 0:
                S_.dve("memset", gl[:, 0:HL], 0.0, W=[gb])
                self.ld("fa_gl%d" % i, gl[:, HL:HL + TB], self.GL[ct * 128:(ct + 1) * 128, 0:TB], R=[S_.b("GL", 0)], W=[gb], q=self.q())
            else:
                self.ld("fa_gl%d" % i, gl[:, :], self.GL[ct * 128:(ct + 1) * 128, t0 - HL:t0 + TB], R=[S_.b("GL", 0), S_.b("GL", 1)],
                        W=[gb], q=self.q())
            a = acc[:, ct, :]
            ab = S_.b("fa_acc", ct)
            S_.act("activation", out=a, in_=gl[:, 30:30 + TB], func=AF.Identity, scale=self.vcol(V_CW + 30 * 8 + ct),
                   bias=self.vcol(V_CB + ct), R=[gb, S_.b("vecs")], W=[ab])
            for k in range(30):
                S_.dve("scalar_tensor_tensor", out=a, in0=gl[:, k:k + TB], scalar=self.vcol(V_CW + k * 8 + ct), in1=a,
                       op0=ALU.mult, op1=ALU.add, R=[gb, S_.b("vecs"), ab], W=[ab])
        for c in range(2):
            mean = self.FA[:, 10400:10912]
            rstd = self.FA[:, 10912:11424]
            mb = S_.b("fa_mr")
            self.ln_stats([acc[:, ct, c * 512:(c + 1) * 512] for ct in range(8)], [S_.b("fa_acc", ct) for ct in range(8)],
                          1024, 1e-5, mean, rstd, mb)
            for ct in range(8):
                t = self.tmp_f[3][:, 0:512]
                S_.dve("tensor_tensor", out=t, in0=acc[:, ct, c * 512:(c + 1) * 512], in1=mean, op=ALU.subtract,
                       R=[S_.b("fa_acc", ct), mb], W=[S_.b("tmp_f", 3)])
                S_.dve("tensor_tensor", out=t, in0=t, in1=rstd, op=ALU.mult, R=[S_.b("tmp_f", 3), mb], W=[S_.b("tmp_f", 3)])
                i = self.rot("pt", 3)
                S_.act("activation", out=self.pt[i][:, :], in_=t, func=AF.Silu, scale=self.vcol(V_CG + ct), bias=self.vcol(V_CBB + ct),
                       R=[S_.b("tmp_f", 3), S_.b("vecs")], W=[S_.b("pt", i)])
                self.st("pt%d" % i, self.ACTS[2, ct * 128:(ct + 1) * 128, t0 + c * 512:t0 + (c + 1) * 512], self.pt[i][:, :],
                        R=[S_.b("pt", i)], W=[S_.b("ACTS", 2, tb)], q=self.q())

    def stage_D(self, l, tb):
        S_ = self.S
        S_.fence()
        t0 = tb * TB
        scale = 128 ** -0.5
        dm = self.cst_b[:, 2048:2560]
        diag = self.cst_b[:, 2048 + 128:2048 + 256]
        for h in range(8):
            accU = self.FA[:, 0:1024]
            accZ = self.FA[:, 1024:2048]
            au, az = S_.b("fa_accU"), S_.b("fa_accZ")
            for gi, d in enumerate(DIL_D):
                s = (h * 3 + gi) % 2
                base = s * 8192
                L = S // d
                n = TB // d
                Q = self.BFA[:, base:base + 1024].rearrange("p (j r) -> p r j", r=d)
                K = self.BFA[:, base + 1024:base + 3072].rearrange("p (j r) -> p r j", r=d)
                bq, bk_ = S_.b("d_q", s), S_.b("d_k", s)
                dep = [S_.b("DQK", 0), S_.b("DQK", 1)]
                self.ld("d_q%d" % s, self.BFA[:, base:base + 1024], self.DQ[gi, h * 128:(h + 1) * 128, t0:t0 + TB], R=dep, W=[bq])
                self.ld("d_k%d" % s, self.BFA[:, base + 1024:base + 3072], self.DK[gi, h * 128:(h + 1) * 128, :], R=dep, W=[bk_])
                units = []
                if d == 16:
                    for r in range(16):
                        qa = 64 * tb
                        nkk = 64 * (tb + 1)
                        units.append((r, qa, 64, [(0, nkk, diag[0:nkk, qa:qa + 64])]))
                else:
                    for r in range(d):
                        for nb in range(n // 128):
                            blk = tb * (n // 128) + nb
                            kt = []
                            if blk > 0:
                                kt.append(((blk - 1) * 128, 128, dm[:, 0:128]))
                            kt.append((blk * 128, 128, dm[:, 128:256]))
                            units.append((r, blk * 128, 128, kt))
                bv = S_.b("d_v", s)
                vdep = [S_.b("DV", 0), S_.b("DV", 1)]
                vsrc = self.DV[gi, :, h * 128:(h + 1) * 128].rearrange("(jt p r) c -> p r jt c", p=128, r=d)
                if d == 16:
                    jt_lo, njt = 0, 1
                    nkk = 64 * (tb + 1)
                    vt = self.BFA[:, base + 3072:base + 3072 + 16 * 128].rearrange("p (r jt c) -> p r jt c", r=16, jt=1)
                    self.ld("d_v%d" % s, vt[0:nkk, :, 0, :], vsrc[0:nkk, :, 0, :], R=vdep, W=[bv])
                else:
                    blk0 = tb * (n // 128)
                    jt_lo = max(0, blk0 - 1)
                    njt = blk0 + n // 128 - jt_lo
                    vt = self.BFA[:, base + 3072:base + 3072 + d * njt * 128].rearrange("p (r jt c) -> p r jt c", r=d, jt=njt)
                    for r in range(d):
                        self.ld("d_v%d" % s, vt[:, r, :, :], vsrc[:, r, jt_lo:jt_lo + njt, :], R=vdep, W=[bv])
                vmap = {}
                for (r, qa, nq, kts) in units:
                    for (klo, nkk_, m) in kts:
                        vmap[(r, klo)] = (r, klo // 128 - jt_lo)
                for u0 in range(0, len(units), 2):
                    grp = units[u0:u0 + 2]
                    sbk = self.s_bank()
                    ub, zb = self.acc_banks()
                    pi = self.rot("pt", 3)
                    P = self.pt[pi]
                    col = 0
                    lay = []
                    for (r, qa, nq, kts) in grp:
                        ql = qa - tb * n
                        for (klo, nkk, m) in kts:
                            S_.pe("matmul", self.ps[sbk][0:nkk, col:col + nq], lhsT=K[:, r, klo:klo + nkk], rhs=Q[:, r, ql:ql + nq],
                                  start=True, stop=True, R=[bk_, bq], W=[S_.b("ps", sbk)])
                            lay.append((r, qa, nq, klo, nkk, m, col))
                            col += nq
                    rows = max(x[4] for x in lay)
                    S_.act("activation", out=P[0:rows, 0:col], in_=self.ps[sbk][0:rows, 0:col], func=AF.Exp, scale=scale,
                           R=[S_.b("ps", sbk)], W=[S_.b("pt", pi)])
                    for (r, qa, nq, klo, nkk, m, c0) in lay:
                        S_.dve("tensor_tensor", out=P[0:nkk, c0:c0 + nq], in0=P[0:nkk, c0:c0 + nq], in1=m, op=ALU.mult,
                               R=[S_.b("pt", pi), S_.b("cst_b")], W=[S_.b("pt", pi)])
                    oc = 0
                    outs = []
                    for ui, (r, qa, nq, kts) in enumerate(grp):
                        mine = [x for x in lay if x[0] == r and x[1] == qa]
                        for j, (r_, qa_, nq_, klo, nkk, m, c0) in enumerate(mine):
                            vr, vj = vmap[(r, klo)]
                            S_.pe("matmul", self.ps[ub][:, oc:oc + nq], lhsT=vt[0:nkk, vr, vj, :], rhs=P[0:nkk, c0:c0 + nq],
                                  start=(j == 0), stop=(j == len(mine) - 1), R=[bv, S_.b("pt", pi)], W=[S_.b("ps", ub)])
                        for j, (r_, qa_, nq_, klo, nkk, m, c0) in enumerate(mine):
                            S_.pe("matmul", self.ps[zb][:, oc:oc + nq], lhsT=self.ones_b[0:nkk, :], rhs=P[0:nkk, c0:c0 + nq],
                                  start=(j == 0), stop=(j == len(mine) - 1), R=[S_.b("ones_b"), S_.b("pt", pi)], W=[S_.b("ps", zb)])
                        outs.append((r, qa, nq, oc))
                        oc += nq
                    for (r, qa, nq, oc_) in outs:
                        ql = qa - tb * n
                        for (accT, ab, bank) in ((accU, au, ub), (accZ, az, zb)):
                            if d == 1:
                                dst = accT[:, ql:ql + nq]
                            else:
                                dst = accT.rearrange("p (j r) -> p r j", r=d)[:, r, ql:ql + nq]
                            src = self.ps[bank][:, oc_:oc_ + nq]
                            if gi == 0:
                                S_.act("copy", out=dst, in_=src, R=[S_.b("ps", bank)], W=[ab])
                            else:
                                S_.dve("tensor_tensor", out=dst, in0=src, in1=dst, op=ALU.add, R=[S_.b("ps", bank), ab], W=[ab])
            rc = self.tmp_f[0][:, :]
            S_.dve("reciprocal", out=rc, in_=accZ, R=[az], W=[S_.b("tmp_f", 0)])
            i = self.rot("stg_b", 2)
            S_.dve("tensor_tensor", out=self.stg_b[i][:, :], in0=accU, in1=rc, op=ALU.mult, R=[au, S_.b("tmp_f", 0)], W=[S_.b("stg_b", i)])
            self.st("stg_b%d" % i, self.ACTS[3, h * 128:(h + 1) * 128, t0:t0 + TB], self.stg_b[i][:, :], R=[S_.b("stg_b", i)],
                    W=[S_.b("ACTS", 3, tb)], q=self.q())

    def stage_M(self, l):
        S_ = self.S
        S_.fence()
        blocks = []
        for tb in range(2):
            xv, xb = self.load_xT(self.XB, tb * TB, [S_.b("XB", tb)], slot=tb)
            blocks.append((xv, xb))
        mg = self.FA[:, 0:8192].rearrange("p (m t) -> p m t", t=2048)
        actb = self.FA[:, 8192:16384].bitcast(BF16)
        av = [actb[:, tb * 8192:(tb + 1) * 8192].rearrange("p (k t) -> p k t", t=TB) for tb in range(2)]
        steps = [(mgp, b, half) for mgp in range(4) for b in range(4) for half in range(2)]
        slot_of = {}

        def loadw(i):
            mgp, b, half = steps[i]
            wi = self.rot("ws", 2)
            g = b * 8 + mgp * 2 + half
            S_.dma("pool", "wslot%d" % wi, self.wslots[wi][:, 0:4096], self.w["w_gate"][l][g], W=[S_.b("wslot", wi)])
            S_.dma("pool", "wslot%d" % wi, self.wslots[wi][:, 4096:6144], self.w["w_proj"][l][g], W=[S_.b("wslot", wi)])
            slot_of[i] = wi

        loadw(0)
        for i, (mgp, b, half) in enumerate(steps):
            if half == 0:
                for tb in range(2):
                    self.ld("fa_act%d" % tb, av[tb], self.ACTS[b, :, tb * TB:(tb + 1) * TB].rearrange("(k p) t -> p k t", p=128),
                            R=[S_.b("ACTS", b, tb)], W=[S_.b("fa_act", tb)])
            if i + 1 < len(steps):
                loadw(i + 1)
            wi = slot_of.pop(i)
            wb = S_.b("wslot", wi)
            wg = self.wslots[wi][:, 0:4096].rearrange("p (k c) -> p k c", c=256)
            wp = self.wslots[wi][:, 4096:6144].rearrange("p (k c) -> p k c", c=256)
            for tb in range(2):
                xv, xb = blocks[tb]
                for mt2 in range(2):
                    ml = half * 2 + mt2
                    m = mgp * 4 + ml
                    gbk = self.next_ps(2)
                    for kc in range(16):
                        for c in range(2):
                            S_.pe("matmul", self.ps[gbk[c]][:, :], lhsT=wg[:, kc, mt2 * 128:(mt2 + 1) * 128],
                                  rhs=xv[:, kc, c * 512:(c + 1) * 512], start=(kc == 0), stop=(kc == 15),
                                  R=[wb] + list(xb), W=[S_.b("ps", gbk[c])])
                    sgs = []
                    for c in range(2):
                        k = self.rot("sg", 2)
                        sg = self.stg_f[k][:, c * 512:(c + 1) * 512]
                        S_.act("activation", out=sg, in_=self.ps[gbk[c]][:, :], func=AF.Sigmoid, bias=self.vcol(V_BG + b * 16 + m),
                               R=[S_.b("ps", gbk[c]), S_.b("vecs")], W=[S_.b("stg_f", k)])
                        sgs.append((sg, S_.b("stg_f", k)))
                    pbk = self.next_ps(2)
                    for kc in range(8):
                        for c in range(2):
                            S_.pe("matmul", self.ps[pbk[c]][:, :], lhsT=wp[:, kc, mt2 * 128:(mt2 + 1) * 128],
                                  rhs=av[tb][:, kc, c * 512:(c + 1) * 512], start=(kc == 0), stop=(kc == 7),
                                  R=[wb, S_.b("fa_act", tb)], W=[S_.b("ps", pbk[c])])
                    for c in range(2):
                        sg, sgb = sgs[c]
                        dst = mg[:, ml, tb * 1024 + c * 512:tb * 1024 + (c + 1) * 512]
                        mb = S_.b("fa_mg", ml, tb)
                        if b == 0:
                            S_.dve("tensor_tensor", out=dst, in0=self.ps[pbk[c]][:, :], in1=sg, op=ALU.mult,
                                   R=[S_.b("ps", pbk[c]), sgb], W=[mb])
                        else:
                            t = self.tmp_f[c][:, 0:512]
                            S_.dve("tensor_tensor", out=t, in0=self.ps[pbk[c]][:, :], in1=sg, op=ALU.mult,
                                   R=[S_.b("ps", pbk[c]), sgb], W=[S_.b("tmp_f", c)])
                            S_.dve("tensor_tensor", out=dst, in0=dst, in1=t, op=ALU.add, R=[S_.b("tmp_f", c), mb], W=[mb])
            if b == 3 and half == 1:
                for ml in range(4):
                    for tb in range(2):
                        j = self.rot("stg_b", 2)
                        S_.dve("tensor_copy", out=self.stg_b[j][:, :], in_=mg[:, ml, tb * 1024:(tb + 1) * 1024],
                               R=[S_.b("fa_mg", ml, tb)], W=[S_.b("stg_b", j)])
                        mrow = (mgp * 4 + ml) * 128
                        self.st("stg_b%d" % j, self.MG[mrow:mrow + 128, tb * TB:(tb + 1) * TB], self.stg_b[j][:, :],
                                R=[S_.b("stg_b", j)], W=[S_.b("MG", tb)])

    def stage_resid(self, w_dram, KC, G, ngroups, src_act, act_dep, res, res_dep, tb, nch, c0=0):
        S_ = self.S
        S_.fence()
        t0 = tb * TB + c0
        W_ = nch * 512
        kcap = min(KC, 32768 // W_)
        av = self.BFA[:, 0:kcap * W_].rearrange("p (k t) -> p k t", t=W_)
        xb = []
        kh = kcap // 2
        for hf, (k0, k1) in enumerate(((0, kh), (kh, kcap))):
            self.ld("bfa_rx%d" % hf, av[:, k0:k1, :], src_act[k0 * 128:k1 * 128, t0:t0 + W_].rearrange("(k p) t -> p k t", p=128),
                    R=act_dep, W=[S_.b("bfa_rx", hf)])
            xb.append(S_.b("bfa_rx", hf))
        av2 = None
        if KC > kcap:
            kx = KC - kcap
            av2 = self.FA[:, 0:kx * W_ // 2].bitcast(BF16).rearrange("p (k t) -> p k t", t=W_)
            self.ld("fa_rx", av2, src_act[kcap * 128:KC * 128, t0:t0 + W_].rearrange("(k p) t -> p k t", p=128),
                    R=act_dep, W=[S_.b("fa_rx")])
            xb.append(S_.b("fa_rx"))

        def xget(kc, c):
            if kc < kcap:
                return av[:, kc, c * 512:(c + 1) * 512]
            return av2[:, kc - kcap, c * 512:(c + 1) * 512]

        def ev(pm, pp):
            (g, mt), banks = pm[0], pp[0]
            m = g * (G // 128) + mt
            i = self.rot("stg_f", 2)
            k = self.rot("rs", 2)
            rs = self.tmp_f[k][:, 0:W_]
            self.ld("tmp_f%d" % k, rs, res[m * 128:(m + 1) * 128, t0:t0 + W_], R=res_dep, W=[S_.b("tmp_f", k)], q=self.q())
            for c in range(nch):
                S_.dve("scalar_tensor_tensor", out=self.stg_f[i][:, c * 512:(c + 1) * 512], in0=rs[:, c * 512:(c + 1) * 512], scalar=ALPHA,
                       in1=self.ps[banks[c]][:, :], op0=ALU.mult, op1=ALU.add, R=[S_.b("tmp_f", k), S_.b("ps", banks[c])],
                       W=[S_.b("stg_f", i)])
            self.st("stg_f%d" % i, self.Z[m * 128:(m + 1) * 128, t0:t0 + W_], self.stg_f[i][:, 0:W_], R=[S_.b("stg_f", i)],
                    W=[S_.b("Z", tb)], q=self.q())

        self.linear_fm(w_dram, KC, G, range(ngroups), xget, xb, ev, nch=nch)

    def stage_N(self, l, tb, gcol, bcol, dstF, dstB, fkey, final=False):
        S_ = self.S
        S_.fence()
        for c in range(2):
            t0 = tb * TB + c * 512
            z = self.FA[:, 0:8192].rearrange("p (k t) -> p k t", t=512)
            for hf in range(2):
                self.ld("fa_z%d" % hf, z[:, hf * 8:(hf + 1) * 8, :], self.Z[hf * 1024:(hf + 1) * 1024, t0:t0 + 512].rearrange("(k p) t -> p k t", p=128),
                        R=[S_.b("Z", tb)], W=[S_.b("fa_z", hf)], q=self.q())
            mean = self.FA[:, 8192:8704]
            rstd = self.FA[:, 8704:9216]
            mb = S_.b("fa_mr")
            self.ln_stats([z[:, k, :] for k in range(16)], [S_.b("fa_z", k // 8) for k in range(16)], 2048, 1e-5, mean, rstd, mb)
            for k in range(16):
                t = self.tmp_f[3][:, 0:512]
                S_.dve("tensor_tensor", out=t, in0=z[:, k, :], in1=mean, op=ALU.subtract, R=[S_.b("fa_z", k // 8), mb], W=[S_.b("tmp_f", 3)])
                S_.dve("tensor_tensor", out=t, in0=t, in1=rstd, op=ALU.mult, R=[S_.b("tmp_f", 3), mb], W=[S_.b("tmp_f", 3)])
                i = self.rot("stg_f", 2)
                o = self.stg_f[i][:, 0:512]
                S_.act("activation", out=o, in_=t, func=AF.Identity, scale=self.vcol(gcol + k), bias=self.vcol(bcol + k),
                       R=[S_.b("tmp_f", 3), S_.b("vecs")], W=[S_.b("stg_f", i)])
                if final:
                    self.st("stg_f%d" % i, self.out[k * 128:(k + 1) * 128, t0:t0 + 512], o, R=[S_.b("stg_f", i)], q=self.q(), is_out=True)
                    continue
                self.st("stg_f%d" % i, dstF[k * 128:(k + 1) * 128, t0:t0 + 512], o, R=[S_.b("stg_f", i)], W=[S_.b(fkey + "F", tb)], q=self.q())
                j = self.rot("pt", 3)
                S_.act("activation", out=self.pt[j][:, :], in_=t, func=AF.Identity, scale=self.vcol(gcol + k), bias=self.vcol(bcol + k),
                       R=[S_.b("tmp_f", 3), S_.b("vecs")], W=[S_.b("pt", j)])
                self.st("pt%d" % j, dstB[k * 128:(k + 1) * 128, t0:t0 + 512], self.pt[j][:, :], R=[S_.b("pt", j)], W=[S_.b(fkey + "B", tb)], q=self.q())

    def stage_F1(self, l):
        S_ = self.S
        S_.fence()
        blocks = []
        for tb_ in range(2):
            xv, xb = self.load_xT(self.X1B, tb_ * TB, [S_.b("X1B", tb_)], slot=tb_)
            blocks.append((tb_, xv, xb))

        def ev(pm, pp, tb):
            t0 = tb * TB
            g, mt0 = pm[0]
            m = g * 2 + mt0 // 2
            res = []
            for j, ((g_, mt), banks) in enumerate(zip(pm, pp)):
                ct = m + 44 * j
                k = self.rot("hb", 4)
                hb = self.FA[:, k * 1026:(k + 1) * 1026]
                hbb = S_.b("fa_hb", k)
                if tb == 0:
                    S_.dve("memset", hb[:, 0:2], 0.0, W=[hbb])
                else:
                    S_.dve("tensor_copy", out=hb[:, 0:2], in_=self.halo[:, ct, :], R=[S_.b("halo", ct)], W=[hbb])
                for c in range(2):
                    self.evac_copy(hb[:, 2 + c * 512:2 + (c + 1) * 512], hbb, banks[c], eng="act")
                S_.dve("tensor_copy", out=self.halo[:, ct, :], in_=hb[:, 1024:1026], R=[hbb], W=[S_.b("halo", ct)])
                a = self.FA[:, 4200 + k * 1024:4200 + (k + 1) * 1024]
                ab = S_.b("fa_cv", k)
                S_.dve("tensor_scalar", out=a, in0=hb[:, 2:1026], scalar1=self.vcol(V_FW + 2 * 88 + ct), scalar2=self.vcol(V_FB + ct),
                       op0=ALU.mult, op1=ALU.add, R=[hbb, S_.b("vecs")], W=[ab])
                S_.dve("scalar_tensor_tensor", out=a, in0=hb[:, 1:1025], scalar=self.vcol(V_FW + 88 + ct), in1=a, op0=ALU.mult, op1=ALU.add,
                       R=[hbb, S_.b("vecs"), ab], W=[ab])
                S_.dve("scalar_tensor_tensor", out=a, in0=hb[:, 0:1024], scalar=self.vcol(V_FW + ct), in1=a, op0=ALU.mult, op1=ALU.add,
                       R=[hbb, S_.b("vecs"), ab], W=[ab])
                res.append((a, ab))
            (va, vb), (ga, gb) = res
            S_.act("activation", out=ga, in_=ga, func=AF.Silu, R=[gb], W=[gb])
            i = self.rot("stg_b", 2)
            S_.dve("tensor_tensor", out=self.stg_b[i][:, :], in0=va, in1=ga, op=ALU.mult, R=[vb, gb], W=[S_.b("stg_b", i)])
            self.st("stg_b%d" % i, self.HH[m * 128:(m + 1) * 128, t0:t0 + TB], self.stg_b[i][:, :], R=[S_.b("stg_b", i)],
                    W=[S_.b("HH", tb)])

        self.linear_fm(self.w["w_up"][l], 16, 512, range(22), None, None, ev, per_evac=2, blocks=blocks)

    def layer(self, l, last):
        S_ = self.S
        S_.fence()
        self.load_vecs(l)
        self.stage_P(l)
        for tb in range(2):
            self.stage_A1(l, tb)
            self.stage_A2(l, tb)
            self.stage_B(l, tb)
            self.stage_C(l, tb)
            self.stage_D(l, tb)
        self.stage_M(l)
        for tb in range(2):
            self.stage_resid(self.w["w_mix"][l], 16, 512, 4, self.MG, [S_.b("MG", tb)], self.XF, [S_.b("XF", tb)], tb, 2)
            self.stage_N(l, tb, V_L1G, V_L1B, self.X1F, self.X1B, "X1")
        self.stage_F1(l)
        for tb in range(2):
            self.stage_resid(self.w["w_dn"][l], 44, 128, 16, self.HH, [S_.b("HH", tb)], self.X1F, [S_.b("X1F", tb)], tb, 2)
            self.stage_N(l, tb, V_L2G, V_L2B, self.XF, self.XB, "X", final=last)


def build(n_layers, debug=(), stages=None):
    nc = bass.Bass("TRN2", target_bir_lowering=False)
    es = ExitStack()
    P = Prog(nc, es, n_layers, debug)
    P.setup()
    if stages is not None:
        P.load_vecs(0)
        for st in stages:
            if st == "none":
                continue
            name, tb = st[:-1], int(st[-1])
            if name in ("P", "F1", "M"):
                getattr(P, "stage_" + name)(0)
            else:
                getattr(P, "stage_" + name)(0, tb)
    for l in range(n_layers if stages is None else 0):
        P.layer(l, last=(l == n_layers - 1))
    P.S.finish()
    P.S.emit()
    es.close()
    return nc, P


def make_inputs(inp, b, n_layers):
    cst, icnt = host_consts()
    m = {"xT": np.ascontiguousarray(inp["x"][b].T), "pos": np.ascontiguousarray(inp["positions"][b][None, :]).astype(np.int32),
         "cst": cst, "icnt": icnt}
    return m


def kernel(**inputs):
    inp = {k: np.asarray(v) for k, v in inputs.items()}
    nl = DEPTH
    nc, P = build(nl)
    per_layer = [host_layout(inp, l) for l in range(nl)]
    wts = {k: np.stack([per_layer[l][k] for l in range(nl)], 0) for k in W_SHAPES}
    in_maps = []
    for b in range(4):
        m = make_inputs(inp, b, nl)
        m.update(wts)
        in_maps.append(m)
    res = run_bass_kernel_spmd(nc, in_maps, core_ids=list(range(4)))
    out = np.stack([np.ascontiguousarray(res.results[b]["yT"].T) for b in range(4)], 0)
    return out.astype(np.float32)
```
